# Optimizing a Trainium2 kernel written in Bass

```python
import math
import jax, jax.numpy as jnp
from jax import lax
import numpy as np

D_MODEL = 2048
BATCH = 4
SEQ = 2048
DEPTH = 2
DEC_BATCH = 128
DEC_SEQ = 4
PAST_LEN = 16384
PAGE_SIZE = 128

N_META = 16
CHUNK = 64
N_EVEN = (DEPTH + 1) // 2
N_ODD = DEPTH // 2

H_A = 8
DK_A = 128
DV_A = 128
H_B = 4
DK_B = 128
DV_B = 256
EVEN_SIZES = (H_A * DK_A, H_A * DK_A, H_A * DV_A, H_A * DV_A, H_B * DK_B, H_B * DK_B, H_B * DV_B, H_B * DV_B)
EVEN_IN = sum(EVEN_SIZES)
EVEN_MIX = H_A * DV_A + H_B * DV_B

H_C = 16
P_C = 64
N_C = 128
G_C = 2
CONV_W = 4
DI_C = H_C * P_C
CONV_DIM = DI_C + 2 * G_C * N_C
H_D = 16
P_D = 64
DI_D = H_D * P_D
R_W = 64
R_A = 64
R_G = 160
RWKV_SIZES = (DI_D, DI_D, DI_D, R_W, R_A, R_G)
SHIFT_DIM = sum(RWKV_SIZES)
ODD_SIZES = (DI_C, CONV_DIM, H_C, SHIFT_DIM)
ODD_IN = sum(ODD_SIZES)
ODD_MIX = DI_C + DI_D

PEER_KEYS = 128
PEER_EXPERTS = PEER_KEYS * PEER_KEYS
PEER_HEADS = 8
PEER_TOPK = 16
PEER_QDIM = 256
PEER_BLOCK = 128

ALPHA = (2.0 * DEPTH) ** 0.25
BETA = (8.0 * DEPTH) ** -0.25
LN_EPS = 1e-5
RMS_EPS = 1e-6
RWKV_GN_EPS = 64e-5
ROPE_BASE = 10000.0
F32 = jnp.float32

kernel_name = 'hybrid_hgrn2_retnet_mamba2_rwkv7_peer_step'


def split_cols(a, sizes):
    out, start = [], 0
    for s in sizes:
        out.append(a[..., start:start + s])
        start += s
    return out


def layer_norm(x, g=None, b=None, eps=LN_EPS):
    xf = x.astype(F32)
    mu = jnp.mean(xf, -1, keepdims=True)
    var = jnp.mean(jnp.square(xf - mu), -1, keepdims=True)
    y = (xf - mu) * lax.rsqrt(var + eps)
    if g is not None:
        y = y * g + b
    return y.astype(x.dtype)


def rms_norm(x, g, eps=RMS_EPS):
    xf = x.astype(F32)
    return (xf * lax.rsqrt(jnp.mean(xf * xf, -1, keepdims=True) + eps) * g).astype(x.dtype)


def to_heads(a, h):
    bn, t, _ = a.shape
    return a.reshape(bn, t, h, -1).transpose(0, 2, 1, 3)


def rotary(x, pos):
    half = x.shape[-1] // 2
    inv = ROPE_BASE ** (-jnp.arange(half, dtype=F32) / half)
    ang = pos.astype(F32)[:, None] * inv
    cos, sin = jnp.cos(ang)[:, None, :], jnp.sin(ang)[:, None, :]
    x1, x2 = x[..., :half], x[..., half:]
    return jnp.concatenate([x1 * cos - x2 * sin, x1 * sin + x2 * cos], -1).astype(x.dtype)


def scalar_decay_chunk(q, k, v, logf, s0):
    L = q.shape[2]
    causal = jnp.tril(jnp.ones((L, L), dtype=bool))
    b = jnp.cumsum(logf.astype(F32), axis=-1)
    seg = jnp.exp(jnp.where(causal, b[..., :, None] - b[..., None, :], -jnp.inf))
    scores = jnp.einsum('bhtk,bhsk->bhts', q, k) * seg
    o = jnp.einsum('bhts,bhsv->bhtv', scores, v) + jnp.einsum('bhtk,bhkv->bhtv', q * jnp.exp(b)[..., None], s0)
    b_end = b[..., -1:]
    s = jnp.exp(b_end)[..., None] * s0 + jnp.einsum('bhsk,bhsv->bhkv', k * jnp.exp(b_end - b)[..., None], v)
    return o, s


def vector_decay_chunk(q, k, v, logf, s0):
    L = q.shape[2]
    causal = jnp.tril(jnp.ones((L, L), dtype=bool))
    b = jnp.cumsum(logf.astype(F32), axis=2)
    diff = b[:, :, :, None, :] - b[:, :, None, :, :]
    seg = jnp.exp(jnp.where(causal[:, :, None], diff, -jnp.inf))
    scores = jnp.einsum('bhtk,bhtsk,bhsk->bhts', q, seg, k)
    o = jnp.einsum('bhts,bhsv->bhtv', scores, v) + jnp.einsum('bhtk,bhkv->bhtv', q * jnp.exp(b), s0)
    b_end = b[:, :, -1:, :]
    s = jnp.exp(b_end[:, :, 0, :])[..., None] * s0 + jnp.einsum('bhsk,bhsv->bhkv', k * jnp.exp(b_end - b), v)
    return o, s


def run_chunks(chunk_fn, q, k, v, logf, s0, lead):
    T = q.shape[2]
    s = s0.astype(F32)
    parts = []
    if lead > 0:
        o_head, s = chunk_fn(q[:, :, :lead], k[:, :, :lead], v[:, :, :lead], logf[:, :, :lead], s)
        parts.append(o_head)
    rest = T - lead
    if rest > 0:
        n = rest // CHUNK

        def split(a):
            a = a[:, :, lead:]
            a = a.reshape(a.shape[:2] + (n, CHUNK) + a.shape[3:])
            return jnp.moveaxis(a, 2, 0)

        def step(carry, xs):
            o_c, carry = chunk_fn(xs[0], xs[1], xs[2], xs[3], carry)
            return carry, o_c

        s, o_rest = lax.scan(step, s, (split(q), split(k), split(v), split(logf)))
        o_rest = jnp.moveaxis(o_rest, 0, 2)
        parts.append(o_rest.reshape(o_rest.shape[:2] + (rest,) + o_rest.shape[4:]))
    return jnp.concatenate(parts, axis=2), s


def rwkv7_scan(r, decay, k, v, kk, a, s0):
    def step(s, inp):
        r_t, w_t, k_t, v_t, kk_t, a_t = inp
        sa = jnp.einsum('bhij,bhj->bhi', s, -kk_t)
        s = s * w_t[:, :, None, :] + sa[..., None] * (kk_t * a_t)[:, :, None, :] + v_t[..., None] * k_t[:, :, None, :]
        return s, jnp.einsum('bhij,bhj->bhi', s, r_t)

    xs = tuple(jnp.moveaxis(t.astype(F32), 1, 0) for t in (r, decay, k, v, kk, a))
    s, ys = lax.scan(step, s0.astype(F32), xs)
    return jnp.moveaxis(ys, 0, 1), s


def even_mixer(x, pos, lead, s_hgrn, s_ret, lb, w_in, norm_g, w_out):
    bn, T, _ = x.shape
    aq, af, ai, ag, bq, bk, bv, bg = split_cols(x @ w_in, EVEN_SIZES)
    af32 = af.astype(F32)
    log_f = jnp.log(lb + (1.0 - lb) * jax.nn.sigmoid(af32))
    k_a = (1.0 - lb) * jax.nn.sigmoid(-af32)
    o_a, s_hgrn_new = run_chunks(vector_decay_chunk, to_heads(jax.nn.silu(aq), H_A), to_heads(k_a, H_A),
                                 to_heads(ai, H_A), to_heads(log_f, H_A), s_hgrn, lead)
    o_a = rms_norm(o_a.transpose(0, 2, 1, 3), norm_g).reshape(bn, T, -1) * jax.nn.silu(ag)
    qb = rotary(bq.reshape(bn, T, H_B, DK_B), pos)
    kb = rotary(bk.reshape(bn, T, H_B, DK_B), pos) * DK_B ** -0.5
    log_gamma = jnp.log(1.0 - 2.0 ** (-5.0 - jnp.arange(H_B, dtype=F32)))
    log_f_b = jnp.broadcast_to(log_gamma[None, :, None], (bn, H_B, T))
    o_b, s_ret_new = run_chunks(scalar_decay_chunk, qb.transpose(0, 2, 1, 3), kb.transpose(0, 2, 1, 3),
                                to_heads(bv, H_B), log_f_b, s_ret, lead)
    o_b = layer_norm(o_b.transpose(0, 2, 1, 3)).reshape(bn, T, -1) * jax.nn.silu(bg)
    y = jnp.concatenate([o_a, o_b], -1) @ w_out
    return y, s_hgrn_new.astype(s_hgrn.dtype), s_ret_new.astype(s_ret.dtype)


def odd_mixer(x, lead, s_ssm, s_conv, s_wkv, s_shift, w, i):
    bn, T, _ = x.shape
    z, xbc, dt_raw, rw = split_cols(x @ w['odd_w_in'][i], ODD_SIZES)
    xpad = jnp.concatenate([s_conv.astype(xbc.dtype), xbc], axis=1)
    new_conv = xpad[:, -(CONV_W - 1):]
    cw = w['conv_w'][i]
    conv = w['conv_b'][i] + xpad[:, 0:T] * cw[0]
    for j in range(1, CONV_W):
        conv = conv + xpad[:, j:j + T] * cw[j]
    xc, bm, cm = split_cols(jax.nn.silu(conv), (DI_C, G_C * N_C, G_C * N_C))
    dt = jax.nn.softplus(dt_raw.astype(F32) + w['dt_bias'][i])
    log_f = jnp.transpose(dt * -jnp.exp(w['a_log'][i].astype(F32)), (0, 2, 1))
    xh = xc.reshape(bn, T, H_C, P_C)
    rep = H_C // G_C
    bh = jnp.repeat(bm.reshape(bn, T, G_C, N_C), rep, axis=2)
    ch = jnp.repeat(cm.reshape(bn, T, G_C, N_C), rep, axis=2)
    o_c, s_ssm_new = run_chunks(scalar_decay_chunk, ch.transpose(0, 2, 1, 3), bh.transpose(0, 2, 1, 3),
                                (xh * dt[..., None]).transpose(0, 2, 1, 3), log_f, s_ssm, lead)
    y_c = o_c.transpose(0, 2, 1, 3) + xh * w['d_skip'][i][:, None]
    y_c = (y_c.reshape(bn, T, DI_C) * jax.nn.silu(z)).reshape(bn, T, G_C, DI_C // G_C)
    y_c = rms_norm(y_c, w['ssm_norm_g'][i].reshape(G_C, DI_C // G_C)).reshape(bn, T, DI_C)
    prev = jnp.concatenate([s_shift[:, None].astype(rw.dtype), rw[:, :-1]], axis=1)
    new_shift = rw[:, -1]
    rw = rw + (prev - rw) * w['shift_mu'][i]
    r, k, v, dw, da, dg = split_cols(rw, RWKV_SIZES)
    w_log = -jax.nn.softplus(-(w['rwkv_w0'][i] + jnp.tanh(dw) @ w['rwkv_w2'][i]).astype(F32)) - 0.5
    decay = jnp.exp(-jnp.exp(w_log))
    a = jax.nn.sigmoid(w['rwkv_a0'][i] + da @ w['rwkv_a2'][i])
    g = jax.nn.sigmoid(dg) @ w['rwkv_g2'][i]

    def hd(t):
        return t.reshape(bn, T, H_D, P_D)

    kk = hd(k * w['rwkv_k_k'][i]).astype(F32)
    kk = kk / jnp.maximum(jnp.linalg.norm(kk, axis=-1, keepdims=True), 1e-12)
    k = k * (1.0 + (a - 1.0) * w['rwkv_k_a'][i])
    y_d, s_wkv_new = rwkv7_scan(hd(r), hd(decay), hd(k), hd(v), kk, hd(a), s_wkv)
    y_d = layer_norm(y_d, w['lnx_g'][i].reshape(H_D, P_D), w['lnx_b'][i].reshape(H_D, P_D), RWKV_GN_EPS)
    y_d = y_d + jnp.sum(hd(r) * hd(k) * w['rwkv_r_k'][i], -1, keepdims=True) * hd(v)
    y_d = y_d.reshape(bn, T, DI_D) * g
    y = jnp.concatenate([y_c, y_d], -1) @ w['odd_w_out'][i]
    return (y, s_ssm_new.astype(s_ssm.dtype), new_conv.astype(s_conv.dtype),
            s_wkv_new.astype(s_wkv.dtype), new_shift.astype(s_shift.dtype))


def peer(x, w_query, sub_keys, u_tab, v_tab):
    bn, T, _ = x.shape
    xt = x.reshape(-1, D_MODEL)
    M = xt.shape[0]
    q = (xt @ w_query).reshape(M, PEER_HEADS, 2, PEER_QDIM // 2)
    s = jnp.einsum('mhcd,hcnd->mhcn', q, sub_keys).astype(F32)
    top_s, top_i = lax.top_k(s, PEER_TOPK)
    cand = top_s[:, :, 0, :, None] + top_s[:, :, 1, None, :]
    cand_i = top_i[:, :, 0, :, None] * PEER_KEYS + top_i[:, :, 1, None, :]
    best_s, best_j = lax.top_k(cand.reshape(M, PEER_HEADS, PEER_TOPK * PEER_TOPK), PEER_TOPK)
    idx = jnp.take_along_axis(cand_i.reshape(M, PEER_HEADS, PEER_TOPK * PEER_TOPK), best_j, axis=-1)
    gate = jax.nn.softmax(best_s, axis=-1)
    pad = (-M) % PEER_BLOCK
    nb = (M + pad) // PEER_BLOCK
    xb = jnp.pad(xt, ((0, pad), (0, 0))).reshape(nb, PEER_BLOCK, D_MODEL)
    ib = jnp.pad(idx.reshape(M, -1), ((0, pad), (0, 0))).reshape(nb, PEER_BLOCK, PEER_HEADS * PEER_TOPK)
    gb = jnp.pad(gate.reshape(M, -1), ((0, pad), (0, 0))).reshape(nb, PEER_BLOCK, PEER_HEADS * PEER_TOPK)

    def block(args):
        x_blk, i_blk, g_blk = args
        act = jax.nn.gelu(jnp.einsum('md,mkd->mk', x_blk, u_tab[i_blk]).astype(F32), approximate=False)
        return jnp.einsum('mk,mkd->md', g_blk * act, v_tab[i_blk])

    out = lax.map(block, (xb, ib, gb)).reshape(-1, D_MODEL)[:M]
    return out.reshape(bn, T, D_MODEL)


def run_trunk(x, pos, lead, st_hgrn, st_ret, st_ssm, st_conv, st_wkv, st_shift, w):
    hg, rt, ssm, cv, wkv, sh = [], [], [], [], [], []
    lb_table = jnp.cumsum(jax.nn.softmax(w['hgrn_lb_logits'].astype(F32), axis=0), axis=0)
    for l in range(DEPTH):
        i = l // 2
        if l % 2 == 0:
            mix, s_h, s_r = even_mixer(x, pos, lead, st_hgrn[i], st_ret[i], lb_table[l], w['even_w_in'][i],
                                       w['hgrn_norm_g'][i], w['even_w_out'][i])
            hg.append(s_h)
            rt.append(s_r)
        else:
            mix, s_s, s_c, s_w, s_sh = odd_mixer(x, lead, st_ssm[i], st_conv[i], st_wkv[i], st_shift[i], w, i)
            ssm.append(s_s)
            cv.append(s_c)
            wkv.append(s_w)
            sh.append(s_sh)
        x = layer_norm(ALPHA * x + mix, w['ln_g'][l, 0], w['ln_b'][l, 0])
        ffn = peer(x, w['peer_w_query'][l], w['peer_sub_keys'][l], w['peer_u'][l], w['peer_v'][l])
        x = layer_norm(ALPHA * x + ffn, w['ln_g'][l, 1], w['ln_b'][l, 1])
    return x, jnp.stack(hg), jnp.stack(rt), jnp.stack(ssm), jnp.stack(cv), jnp.stack(wkv), jnp.stack(sh)


def setup_inputs(seed: int = 0) -> dict:
    key = jax.random.key(seed)
    keys = list(jax.random.split(key, 48))

    def nrm(shape, scale):
        return jax.random.normal(keys.pop(), shape, F32) * scale

    def unif(shape, lo, hi):
        return jax.random.uniform(keys.pop(), shape, F32, lo, hi)

    dt0 = jnp.exp(unif((N_ODD, H_C), math.log(1e-3), math.log(1e-1)))
    return {
        'x_prompt': nrm((BATCH, SEQ, D_MODEL), 1.0),
        'x_sample': nrm((DEC_BATCH, DEC_SEQ, D_MODEL), 1.0),
        'state_hgrn': nrm((N_EVEN, DEC_BATCH, H_A, DK_A, DV_A), 0.5),
        'state_ret': nrm((N_EVEN, DEC_BATCH, H_B, DK_B, DV_B), 1.0),
        'state_ssm': nrm((N_ODD, DEC_BATCH, H_C, N_C, P_C), 0.5),
        'state_conv': nrm((N_ODD, DEC_BATCH, CONV_W - 1, CONV_DIM), 1.0),
        'state_wkv': nrm((N_ODD, DEC_BATCH, H_D, P_D, P_D), 0.3),
        'state_shift': nrm((N_ODD, DEC_BATCH, SHIFT_DIM), 1.0),
        'meta_tokens': nrm((N_META, D_MODEL), 1.0),
        'ln_g': 1.0 + nrm((DEPTH, 2, D_MODEL), 0.02),
        'ln_b': nrm((DEPTH, 2, D_MODEL), 0.02),
        'even_w_in': nrm((N_EVEN, D_MODEL, EVEN_IN), D_MODEL ** -0.5),
        'hgrn_lb_logits': nrm((DEPTH + 1, H_A * DK_A), 0.5),
        'hgrn_norm_g': 1.0 + nrm((N_EVEN, DV_A), 0.02),
        'even_w_out': nrm((N_EVEN, EVEN_MIX, D_MODEL), EVEN_MIX ** -0.5 * BETA),
        'odd_w_in': nrm((N_ODD, D_MODEL, ODD_IN), D_MODEL ** -0.5),
        'conv_w': nrm((N_ODD, CONV_W, CONV_DIM), CONV_W ** -0.5),
        'conv_b': nrm((N_ODD, CONV_DIM), 0.02),
        'dt_bias': dt0 + jnp.log(-jnp.expm1(-dt0)),
        'a_log': jnp.log(unif((N_ODD, H_C), 1.0, 16.0)),
        'd_skip': 1.0 + nrm((N_ODD, H_C), 0.1),
        'ssm_norm_g': 1.0 + nrm((N_ODD, DI_C), 0.02),
        'shift_mu': unif((N_ODD, SHIFT_DIM), 0.0, 1.0),
        'rwkv_w0': -1.0 + nrm((N_ODD, DI_D), 0.5),
        'rwkv_w2': nrm((N_ODD, R_W, DI_D), 0.1),
        'rwkv_a0': nrm((N_ODD, DI_D), 0.1),
        'rwkv_a2': nrm((N_ODD, R_A, DI_D), 0.1),
        'rwkv_g2': nrm((N_ODD, R_G, DI_D), R_G ** -0.5),
        'rwkv_k_k': 0.85 + nrm((N_ODD, DI_D), 0.05),
        'rwkv_k_a': 1.0 + nrm((N_ODD, DI_D), 0.05),
        'rwkv_r_k': nrm((N_ODD, H_D, P_D), 0.1),
        'lnx_g': 1.0 + nrm((N_ODD, DI_D), 0.02),
        'lnx_b': nrm((N_ODD, DI_D), 0.02),
        'odd_w_out': nrm((N_ODD, ODD_MIX, D_MODEL), ODD_MIX ** -0.5 * BETA),
        'peer_w_query': nrm((DEPTH, D_MODEL, PEER_HEADS * PEER_QDIM), D_MODEL ** -0.5),
        'peer_sub_keys': nrm((DEPTH, PEER_HEADS, 2, PEER_KEYS, PEER_QDIM // 2), (PEER_QDIM // 2) ** -0.5),
        'peer_u': nrm((DEPTH, PEER_EXPERTS, D_MODEL), D_MODEL ** -0.5),
        'peer_v': nrm((DEPTH, PEER_EXPERTS, D_MODEL), BETA * PEER_HEADS ** -0.5),
    }


def reference(x_prompt, x_sample, state_hgrn, state_ret, state_ssm, state_conv, state_wkv, state_shift,
              meta_tokens, ln_g, ln_b, even_w_in, hgrn_lb_logits, hgrn_norm_g, even_w_out, odd_w_in, conv_w,
              conv_b, dt_bias, a_log, d_skip, ssm_norm_g, shift_mu, rwkv_w0, rwkv_w2, rwkv_a0, rwkv_a2, rwkv_g2,
              rwkv_k_k, rwkv_k_a, rwkv_r_k, lnx_g, lnx_b, odd_w_out, peer_w_query, peer_sub_keys, peer_u, peer_v):
    w = dict(ln_g=ln_g, ln_b=ln_b, even_w_in=even_w_in, hgrn_lb_logits=hgrn_lb_logits, hgrn_norm_g=hgrn_norm_g,
             even_w_out=even_w_out, odd_w_in=odd_w_in, conv_w=conv_w, conv_b=conv_b, dt_bias=dt_bias,
             a_log=a_log, d_skip=d_skip, ssm_norm_g=ssm_norm_g, shift_mu=shift_mu, rwkv_w0=rwkv_w0,
             rwkv_w2=rwkv_w2, rwkv_a0=rwkv_a0, rwkv_a2=rwkv_a2, rwkv_g2=rwkv_g2, rwkv_k_k=rwkv_k_k,
             rwkv_k_a=rwkv_k_a, rwkv_r_k=rwkv_r_k, lnx_g=lnx_g, lnx_b=lnx_b, odd_w_out=odd_w_out,
             peer_w_query=peer_w_query, peer_sub_keys=peer_sub_keys, peer_u=peer_u, peer_v=peer_v)
    bp, sp = x_prompt.shape[0], x_prompt.shape[1]
    dt = x_prompt.dtype
    xp = jnp.concatenate([jnp.broadcast_to(meta_tokens.astype(dt)[None], (bp, N_META, D_MODEL)), x_prompt], axis=1)
    pos_p = jnp.arange(N_META + sp)
    yp, hgrn_p, ret_p, ssm_p, conv_p, wkv_p, shift_p = run_trunk(
        xp, pos_p, N_META,
        jnp.zeros((N_EVEN, bp, H_A, DK_A, DV_A), dt), jnp.zeros((N_EVEN, bp, H_B, DK_B, DV_B), dt),
        jnp.zeros((N_ODD, bp, H_C, N_C, P_C), dt), jnp.zeros((N_ODD, bp, CONV_W - 1, CONV_DIM), dt),
        jnp.zeros((N_ODD, bp, H_D, P_D, P_D), dt), jnp.zeros((N_ODD, bp, SHIFT_DIM), dt), w)
    y_prompt = yp[:, N_META:]
    ds = x_sample.shape[1]
    pos_s = PAST_LEN + jnp.arange(ds)
    y_sample, hgrn_s, ret_s, ssm_s, conv_s, wkv_s, shift_s = run_trunk(
        x_sample, pos_s, ds % CHUNK, state_hgrn, state_ret, state_ssm, state_conv, state_wkv, state_shift, w)
    return (y_prompt, y_sample, hgrn_p, hgrn_s, ret_p, ret_s, ssm_p, ssm_s, conv_p, conv_s, wkv_p, wkv_s, shift_p, shift_s)
```

```python
import contextlib
import os
import math
import numpy as np
import concourse.bass as bass
import concourse.mybir as mybir
from concourse.bass_utils import run_bass_kernel_spmd

F32 = mybir.dt.float32
BF16 = mybir.dt.bfloat16
I32 = mybir.dt.int32
U32 = mybir.dt.uint32
AF = mybir.ActivationFunctionType
ALU = mybir.AluOpType
AX = mybir.AxisListType
ENGS = ["pe", "act", "dve", "pool", "sp"]

D = 2048
EVEN_IN = 7168
ODD_IN = 5936
ALPHA = 4.0 ** 0.25
NSEQ = 16
GS = int(os.environ.get("KGS", "4"))


class Buf:
    def __init__(self, name, t):
        self.name = name
        self.t = t
        self.last_write = None
        self.reads = []
        self.dsem = None
        self.dcnt = 0

    def __getitem__(self, k):
        return self.t[k]


class Prog:
    def __init__(self, nc):
        self.nc = nc
        self.stack = contextlib.ExitStack()
        self.ops = {e: [] for e in ENGS}
        self.cnt = {e: 0 for e in ENGS}
        self.esem = {}
        self.waited = {e: {} for e in ENGS}
        self.sem_free = {"sw": [], "hw": []}
        self.sem_val = {}
        self.sem_obj = {}
        self.nsem = 0
        for e in ENGS:
            if e != "sp":
                self.esem[e] = self._sem("e_" + e)

    def _sem(self, name):
        self.nsem += 1
        return self.stack.enter_context(self.nc.semaphore(name))

    def sb(self, name, shape, dt=F32):
        return Buf(name, self.stack.enter_context(self.nc.sbuf_tensor(name, list(shape), dt)))

    def ps(self, name, shape, dt=F32):
        return Buf(name, self.stack.enter_context(self.nc.psum_tensor(name, list(shape), dt)))

    def dram(self, name, shape, dt=F32, kind="Internal"):
        return Buf(name, self.nc.dram_tensor(name, list(shape), dt, kind=kind).ap())

    def _collect(self, eng, reads, writes):
        waits = []
        for b in reads:
            if b.last_write is not None:
                waits.append(b.last_write)
        for b in writes:
            if b.last_write is not None:
                waits.append(b.last_write)
            waits.extend(b.reads)
        w = self.waited[eng]
        best = {}
        for (s, v, src) in waits:
            if src == eng and eng == "pe":
                continue
            if w.get(id(s), (None, 0))[1] >= v:
                continue
            if id(s) not in best or best[id(s)][1] < v:
                best[id(s)] = (s, v)
        out = []
        for k, (s, v) in best.items():
            w[k] = (s, v)
            out.append((s, v))
        return out

    def op(self, eng, fn, reads=(), writes=()):
        waits = self._collect(eng, reads, writes)
        self.cnt[eng] += 1
        tok = (self.esem[eng], self.cnt[eng], eng)
        self.ops[eng].append((waits, fn, (self.esem[eng], 1)))
        for b in reads:
            b.reads.append(tok)
        for b in writes:
            b.last_write = tok
            b.reads = []

    def dma(self, q, fn, src, dst, extra_reads=()):
        qt = "sw" if q == "pool" else "hw"
        if dst.dsem is None:
            dst.dsem = {}
        if qt not in dst.dsem:
            if self.sem_free[qt]:
                sem = self.sem_free[qt].pop()
            else:
                sem = self._sem("d%d" % self.nsem)
                self.sem_val[id(sem)] = 0
                self.sem_obj[id(sem)] = sem
            dst.dsem[qt] = sem
        sem = dst.dsem[qt]
        rd = ([src] if src is not None else []) + list(extra_reads)
        waits = self._collect(q, rd, [dst])
        self.sem_val[id(sem)] += 16
        tok = (sem, self.sem_val[id(sem)], "dma")
        self.ops[q].append((waits, fn, (sem, 16)))
        for b in rd:
            b.reads.append(tok)
        dst.last_write = tok
        dst.reads = []

    def barrier(self):
        allw = [(self.esem[e], self.cnt[e]) for e in self.esem if self.cnt[e] > 0]
        allw += [(self.sem_obj[k], v) for k, v in self.sem_val.items() if v > 0]
        for e in ENGS:
            w = self.waited[e]
            ws = []
            for (s, v) in allw:
                if w.get(id(s), (None, 0))[1] >= v:
                    continue
                w[id(s)] = (s, v)
                ws.append((s, v))
            self.ops[e].append((ws, None, None))

    def emit(self):
        nc = self.nc
        with nc.allow_low_precision("bf16 matmul operands"), nc.Block() as block:
            def run(eng, name):
                for (waits, fn, inc) in self.ops[name]:
                    for (s, v) in waits:
                        eng.wait_ge(s, v)
                    if fn is not None:
                        fn(eng).then_inc(inc[0], inc[1])

            @block.tensor
            def _(e):
                run(e, "pe")

            @block.scalar
            def _(e):
                run(e, "act")

            @block.vector
            def _(e):
                run(e, "dve")

            @block.gpsimd
            def _(e):
                run(e, "pool")

            @block.sync
            def _(e):
                run(e, "sp")
        self.stack.close()

    def retire(self, bufs):
        for b in bufs:
            if b.dsem is not None:
                for qt, sem in b.dsem.items():
                    self.sem_free[qt].append(sem)
                b.dsem = None


class Arena:
    def __init__(self, buf, size, prog):
        self.buf = buf
        self.size = size
        self.off = 0
        self.prog = prog
        self.live = []

    def reset(self):
        self.off = 0
        self.prog.retire(self.live)
        self.live = []

    def alias(self, name, ap):
        b = Buf(name, ap)
        self.live.append(b)
        return b

    def alloc(self, name, n):
        assert self.off + n <= self.size, (name, self.off, n, self.size)
        b = Buf(name, self.buf.t[:, self.off:self.off + n])
        self.live.append(b)
        self.off += n
        return b


def v3(ap, a):
    return ap.rearrange("p (a b) -> p a b", a=a)


def bc(ap, shape, axis):
    return ap.unsqueeze(axis).to_broadcast(list(shape))


class K:
    def __init__(self, NF):
        self.NF = NF
        self.NT = NF + 1
        self.TOK = 128 * (NF + 1)
        self.NP = 128 * NF + 16
        nc = bass.Bass("TRN2", target_bir_lowering=False)
        self.nc = nc
        self.P = Prog(nc)

    def TT(self, eng, out, a, b, op, R, W):
        self.P.op(eng, lambda e: e.tensor_tensor(out=out, in0=a, in1=b, op=op), R, W)

    def TS(self, eng, out, a, s1, s2, op0, op1, R, W):
        if s2 is None:
            self.P.op(eng, lambda e: e.tensor_scalar(out=out, in0=a, scalar1=s1, scalar2=None, op0=op0), R, W)
        else:
            self.P.op(eng, lambda e: e.tensor_scalar(out=out, in0=a, scalar1=s1, scalar2=s2, op0=op0, op1=op1), R, W)

    def STT(self, eng, out, a, s, b, op0, op1, R, W, accum=None):
        if accum is None:
            self.P.op(eng, lambda e: e.scalar_tensor_tensor(out=out, in0=a, scalar=s, in1=b, op0=op0, op1=op1), R, W)
        else:
            self.P.op(eng, lambda e: e.scalar_tensor_tensor(out=out, in0=a, scalar=s, in1=b, op0=op0, op1=op1,
                                                            accum_out=accum), R, W)

    def ACT(self, out, a, func, R, W, scale=None, bias=None, accum=None):
        kw = {}
        if scale is not None:
            kw["scale"] = scale
        if bias is not None:
            kw["bias"] = bias
        if accum is not None:
            kw["accum_out"] = accum
        self.P.op("act", lambda e: e.activation(out=out, in_=a, func=func, **kw), R, W)

    def CP(self, eng, out, a, R, W):
        if eng == "act":
            self.P.op("act", lambda e: e.copy(out=out, in_=a), R, W)
        else:
            self.P.op(eng, lambda e: e.tensor_copy(out=out, in_=a), R, W)

    def RED(self, out, a, op, R, W):
        self.P.op("dve", lambda e: e.tensor_reduce(out=out, in_=a, axis=AX.X, op=op), R, W)

    def MM(self, out, lhsT, rhs, start, stop, R, W):
        self.P.op("pe", lambda e: e.matmul(out, lhsT=lhsT, rhs=rhs, start=start, stop=stop), R, W)

    def TR(self, out, a, R, W):
        ident = self.ident
        n = a.shape[0]
        self.P.op("pe", lambda e: e.transpose(out=out, in_=a, identity=ident[0:n, 0:n]), list(R) + [ident], W)

    def DMA(self, q, out, in_, src, dst, extra=()):
        self.P.dma(q, lambda e: e.dma_start(out=out, in_=in_), src, dst, extra)

    def MEMSET(self, eng, ap, val, W):
        self.P.op(eng, lambda e: e.memset(ap, val), [], W)

    def rsqrt(self, out, a, mul, eps, R, W):
        self.TS("dve", out, a, mul, eps, ALU.mult, ALU.add, R, W)
        self.ACT(out, out, AF.Sqrt, W, W)
        self.P.op("dve", lambda e: e.reciprocal(out=out, in_=out), W, W)

    def bload(self, q, dst, n, src_buf, src_ap):
        if isinstance(src_ap, Buf):
            src_ap = src_ap.t
        self.DMA(q, dst[:, 0:n], src_ap.partition_broadcast(128), src_buf, dst)

    def declare(self):
        P, TOK = self.P, self.TOK
        di = lambda n, s, dt=F32: P.dram(n, s, dt, kind="ExternalInput")
        do = lambda n, s, dt=F32: P.dram(n, s, dt, kind="ExternalOutput")
        self.X = di("X", [TOK, D])
        self.st_even = di("st_even", [NSEQ, 128, 2048])
        self.st_odd = di("st_odd", [NSEQ, 128, 1536])
        self.st_conv = di("st_conv", [3, NSEQ, 1536])
        self.st_shift = di("st_shift", [NSEQ, 3360])
        self.rot = di("rot", [4, TOK, 64])
        self.ln_g = di("ln_g", [4, D])
        self.ln_b = di("ln_b", [4, D])
        self.even_w_in = di("even_w_in", [D, EVEN_IN])
        self.lb_logits = di("lb_logits", [3, 1024])
        self.hgrn_norm_g = di("hgrn_norm_g", [128])
        self.even_w_out = di("even_w_out", [D, D])
        self.odd_w_in = di("odd_w_in", [D, ODD_IN])
        self.conv_w = di("conv_w", [4, 1536])
        self.conv_b = di("conv_b", [1536])
        self.dt_bias = di("dt_bias", [16])
        self.a_log = di("a_log", [16])
        self.d_skip = di("d_skip", [16])
        self.ssm_norm_g = di("ssm_norm_g", [1024])
        self.shift_mu = di("shift_mu", [3360])
        self.rwkv_vec = di("rwkv_vec", [7, 1024])
        self.rwkv_w2 = di("rwkv_w2", [64, 1024])
        self.rwkv_a2 = di("rwkv_a2", [64, 1024])
        self.rwkv_g2 = di("rwkv_g2", [160, 1024])
        self.odd_w_out = di("odd_w_out", [D, D])
        self.peer_wq = di("peer_wq", [2, D, D])
        self.keysT = di("keysT", [2, 128, 16, 128])
        self.peer_u = [di("peer_u%d" % l, [16384, D]) for l in range(2)]
        self.peer_v = [di("peer_v%d" % l, [16384, D]) for l in range(2)]
        self.zeros_d = di("zeros_d", [4, 3360])
        self.rowoff = di("rowoff", [TOK, 1])
        self.ub_d = [P.dram("ub_d%d" % l, [16384, D], BF16) for l in range(2)]
        self.vb_d = [P.dram("vb_d%d" % l, [16384, D], BF16) for l in range(2)]
        self.Y = do("Y", [TOK, D])
        self.So_even = do("So_even", [NSEQ + 1, 128, 2048])
        self.So_odd = do("So_odd", [NSEQ + 1, 128, 1536])
        self.conv_o = do("conv_o", [3, NSEQ + 1, 1536])
        self.shift_o = do("shift_o", [NSEQ + 1, 3360])
        self.outs = [self.Y, self.So_even, self.So_odd, self.conv_o, self.shift_o]
        self.proj_d = P.dram("proj_d", [TOK, EVEN_IN])
        self.mix_d = P.dram("mix_d", [TOK, D])
        self.ym_d = P.dram("ym_d", [TOK, D])
        self.h_d = P.dram("h_d", [TOK, D])
        self.q_d = P.dram("q_d", [TOK, D])
        self.x1_d = P.dram("x1_d", [TOK, D])
        self.vrow_d = P.dram("vrow_d", [TOK, 2080])
        self.xbcp_d = P.dram("xbcp_d", [self.NP + 3, 1536])
        self.xbcs_d = P.dram("xbcs_d", [7, NSEQ, 1536])
        self.rwp_d = P.dram("rwp_d", [self.NP + 1, 3360])
        self.rws_d = P.dram("rws_d", [5, NSEQ, 3360])
        self.AR = P.sb("AR", [128, 38000])
        self.ar = Arena(self.AR, 38000, P)
        self.XT = P.sb("XT", [128, 16 * 128 * GS], BF16)
        self.WBF = P.sb("WBF", [128, 8192], BF16)
        self.TQ = P.sb("TQ", [128, 2048], BF16)
        self.TQ32 = P.sb("TQ32", [128, 2048])
        self.ident = P.sb("ident", [128, 128])
        self.blk1 = P.sb("blk1", [128, 128])
        self.Z = P.sb("Z", [128, 256], BF16)
        self.Z0 = P.sb("Z0", [128, 256], BF16)
        self.Z1 = P.sb("Z1", [128, 256], BF16)
        self.iot = P.sb("iot", [128, 256])
        self.PS = [P.ps("ps%d" % i, [128, 512]) for i in range(8)]

    def consts(self):
        P = self.P
        iot, ident = self.iot, self.ident
        P.op("pool", lambda e: e.iota(iot[:, 0:128], pattern=[[1, 128]], base=0, channel_multiplier=-1,
                                      allow_small_or_imprecise_dtypes=True), [], [iot])
        self.TS("dve", ident[:], iot[:, 0:128], 0.0, None, ALU.is_equal, None, [iot], [ident])
        self.MEMSET("dve", self.blk1[:], 0.0, [self.blk1])
        self.MEMSET("dve", self.blk1[0:64, 0:64], 1.0, [self.blk1])
        self.MEMSET("dve", self.blk1[64:128, 64:128], 1.0, [self.blk1])
        for z in (self.Z, self.Z0, self.Z1):
            self.MEMSET("dve", z[:], 0.0, [z])
        self.MEMSET("dve", self.Z[:, 127:128], 1.0, [self.Z])
        self.MEMSET("dve", self.Z0[0:64, 127:128], 1.0, [self.Z0])
        self.MEMSET("dve", self.Z1[64:128, 127:128], 1.0, [self.Z1])
        P.op("pool", lambda e: e.iota(iot[:], pattern=[[1, 256]], base=0, channel_multiplier=0,
                                      allow_small_or_imprecise_dtypes=True), [ident], [iot])

    def project(self, src, W_ap, W_buf, ncols, dst, dst_col0=0):
        P, ar = self.P, self.ar
        P.barrier()
        ar.reset()
        wst = [ar.alloc("wst%d" % i, 8192) for i in range(2)]
        xt = [ar.alloc("xt%d" % i, 2048) for i in range(2)]
        ev = [ar.alloc("ev%d" % i, 512) for i in range(2)]
        XT, WBF, PS = self.XT, self.WBF, self.PS
        nblk = (ncols + 511) // 512
        cnt = 0
        wc = 0
        ngrp = (self.NT + GS - 1) // GS
        bounds = [(self.NT * g) // ngrp for g in range(ngrp + 1)]
        for gi in range(ngrp):
            tiles = list(range(bounds[gi], bounds[gi + 1]))
            for ti, i in enumerate(tiles):
                xb = xt[i % 2]
                self.DMA("sp", xb[:], src[i * 128:(i + 1) * 128, :], src, xb)
                for kb in range(4):
                    pb = PS[kb % 2]
                    for kk in range(4):
                        kc = kb * 4 + kk
                        self.TR(pb[:, kk * 128:(kk + 1) * 128], xb[:, kc * 128:(kc + 1) * 128], [xb], [pb])
                    o = v3(XT[:], 16)[:, kb * 4:(kb + 1) * 4, ti * 128:(ti + 1) * 128]
                    self.CP("act", o, v3(pb[:], 4), [pb], [XT])
            for cb in range(nblk):
                c0 = cb * 512
                w = min(512, ncols - c0)
                ws = wst[wc % 2]
                wc += 1
                self.DMA("pool", v3(ws[:], 16)[:, :, 0:w],
                         W_ap[:, c0:c0 + w].rearrange("(kc p) c -> p kc c", p=128), W_buf, ws)
                self.CP("dve", v3(WBF[:], 16)[:, :, 0:w], v3(ws[:], 16)[:, :, 0:w], [ws], [WBF])
                for ti, i in enumerate(tiles):
                    pb = PS[2 + cnt % 2]
                    eb = ev[cnt % 2]
                    cnt += 1
                    for kc in range(16):
                        self.MM(pb[:, 0:w], v3(XT[:], 16)[:, kc, ti * 128:(ti + 1) * 128], v3(WBF[:], 16)[:, kc, 0:w],
                                kc == 0, kc == 15, [XT, WBF], [pb])
                    self.CP("act", eb[:, 0:w], pb[:, 0:w], [pb], [eb])
                    self.DMA("sp", dst[i * 128:(i + 1) * 128, dst_col0 + c0:dst_col0 + c0 + w], eb[:, 0:w], eb, dst)

    def ln_tile(self, A, Bt, G, Bb, OUT, tmp, st):
        self.STT("dve", A[:], A[:], ALPHA, Bt[:], ALU.mult, ALU.add, [A, Bt], [A])
        self.RED(st[:, 0:1], A[:], ALU.add, [A], [st])
        self.TS("dve", st[:, 0:1], st[:, 0:1], -1.0 / D, None, ALU.mult, None, [st], [st])
        self.TS("dve", A[:], A[:], st[:, 0:1], None, ALU.add, None, [A, st], [A])
        self.ACT(tmp[:], A[:], AF.Square, [A], [tmp, st], accum=st[:, 1:2])
        self.rsqrt(st[:, 2:3], st[:, 1:2], 1.0 / D, 1e-5, [st], [st])
        self.TS("dve", A[:], A[:], st[:, 2:3], None, ALU.mult, None, [A, st], [A])
        self.TT("dve", A[:], A[:], G[:], ALU.mult, [A, G], [A])
        self.TT("dve", OUT[:], A[:], Bb[:], ALU.add, [A, Bb], [OUT])

    def ln_phase(self, a_d, b_d, lni, out_d):
        P, ar = self.P, self.ar
        P.barrier()
        ar.reset()
        G = ar.alloc("lnG", 2048)
        Bb = ar.alloc("lnB", 2048)
        A = [ar.alloc("lnA%d" % i, 2048) for i in range(2)]
        Bt = [ar.alloc("lnBt%d" % i, 2048) for i in range(2)]
        O = [ar.alloc("lnO%d" % i, 2048) for i in range(2)]
        tmp = ar.alloc("lntmp", 2048)
        st = ar.alloc("lnst", 4)
        self.bload("sp", G, D, self.ln_g, self.ln_g[lni])
        self.bload("sp", Bb, D, self.ln_b, self.ln_b[lni])
        for i in range(self.NT):
            a, b, o = A[i % 2], Bt[i % 2], O[i % 2]
            rows = slice(i * 128, (i + 1) * 128)
            self.DMA("sp", a[:], a_d[rows, :], a_d, a)
            self.DMA("sp", b[:], b_d[rows, :], b_d, b)
            self.ln_tile(a, b, G, Bb, o, tmp, st)
            self.DMA("sp", out_d[rows, :], o[:], o, out_d)

    def scan_step(self, m, first, last, S, ns, dv, Dap, Kap, Qap, Vb, Vap, tmpKV, obanks, extra=None):
        n = ns * dv
        TQ = self.TQ
        S3 = v3(S[:, 0:n], ns)
        shp = [128, ns, dv]
        self.TT("pool", v3(tmpKV[:, 0:n], ns), v3(Vap, ns), bc(Kap[0], shp, 2), ALU.mult, [Vb, Kap[1]], [tmpKV])
        if extra is not None:
            extra[0]()
        self.TT("dve", S3, S3, bc(Dap[0], shp, 2), ALU.mult, [S, Dap[1]], [S])
        if extra is not None:
            extra[1]()
        self.TT("dve", S3, S3, v3(tmpKV[:, 0:n], ns), ALU.add, [S, tmpKV], [S])
        TQ32 = self.TQ32
        self.TT("dve", v3(TQ32[:, 0:n], ns), S3, bc(Qap[0], shp, 2), ALU.mult, [S, Qap[1]], [TQ32])
        self.CP("act", TQ[:, 0:n], TQ32[:, 0:n], [TQ32], [TQ])
        for (pb, c0, w, z) in obanks:
            self.MM(pb[:, 0:w], z[:, 127 - m:255 - m], TQ[:, c0:c0 + w], first, last, [TQ, z], [pb])

    def tile_tokens(self, i):
        if i < self.NF:
            return [(m, i * 128 + m, 0) for m in range(128)]
        toks = [(m, i * 128 + m, 0) for m in range(16)]
        for s in range(NSEQ):
            for t in range(4):
                m = 16 + 16 * t + s
                toks.append((m, i * 128 + m, 1 + s))
        return toks

    def run_scan(self, i, Sbufs, ncol, st_in, st_col0, So, step_fn, zero_fn):
        toks = self.tile_tokens(i)
        cur = getattr(self, "_cur_seq", 0)
        for idx, (m, row, seq) in enumerate(toks):
            if seq != cur:
                Sp = Sbufs[cur % 2]
                self.DMA("sp", So[cur][:, st_col0:st_col0 + ncol], Sp[:, 0:ncol], Sp, So)
                cur = seq
                Sn = Sbufs[cur % 2]
                self.DMA("sp", Sn[:, 0:ncol], st_in[seq - 1][:, st_col0:st_col0 + ncol], st_in, Sn)
            step_fn(m, row, Sbufs[cur % 2], idx == 0, idx == len(toks) - 1)
        self._cur_seq = cur
        if i == self.NT - 1:
            Sp = Sbufs[cur % 2]
            self.DMA("sp", So[cur][:, st_col0:st_col0 + ncol], Sp[:, 0:ncol], Sp, So)
            self._cur_seq = 0

    def chunk_consts(self):
        P = self.P
        self.dm = P.sb("dm", [64, 64])
        self.maskT = P.sb("maskT", [64, 64])
        self.ones64 = P.sb("ones64", [64, 64])
        self.SEG = P.sb("SEG", [64, 256])
        self.GQK = P.sb("GQK", [64, 16])
        dm, maskT, SEG, GQK = self.dm, self.maskT, self.SEG, self.GQK
        P.op("pool", lambda e: e.iota(dm[:], pattern=[[1, 64]], base=0, channel_multiplier=-1,
                                      allow_small_or_imprecise_dtypes=True), [], [dm])
        self.TS("dve", maskT[:], dm[:], 0.0, None, ALU.is_ge, None, [dm], [maskT])
        self.MEMSET("dve", self.ones64[:], 1.0, [self.ones64])
        P.op("pool", lambda e: e.iota(GQK[:, 8:9], pattern=[[0, 1]], base=1, channel_multiplier=1,
                                      allow_small_or_imprecise_dtypes=True), [], [GQK])
        P.op("pool", lambda e: e.iota(GQK[:, 9:10], pattern=[[0, 1]], base=63, channel_multiplier=-1,
                                      allow_small_or_imprecise_dtypes=True), [GQK], [GQK])
        for h in range(4):
            lg = math.log(1.0 - 2.0 ** (-5.0 - h))
            self.ACT(SEG[:, h * 64:(h + 1) * 64], dm[:], AF.Exp, [dm], [SEG], scale=lg)
            self.TT("dve", SEG[:, h * 64:(h + 1) * 64], SEG[:, h * 64:(h + 1) * 64], maskT[:], ALU.mult,
                    [SEG, maskT], [SEG])
            self.ACT(GQK[:, h:h + 1], GQK[:, 8:9], AF.Exp, [GQK], [GQK], scale=lg)
            self.ACT(GQK[:, 4 + h:5 + h], GQK[:, 9:10], AF.Exp, [GQK], [GQK], scale=lg)

    def even_pass(self):
        P, ar, PS = self.P, self.ar, self.PS
        P.barrier()
        ar.reset()
        al = ar.alloc
        Pt = al("Pt", EVEN_IN)
        R0 = ar.off
        Dc, Kc, Qc = al("Dc", 2048), al("Kc", 2048), al("Qc", 2048)
        Vb = [al("Vb0", 2048), al("Vb1", 2048)]
        tmpKV = al("tmpKV", 2048)
        R1 = ar.off
        S = [al("S0", 2048), al("S1", 2048)]
        W1, W2, W3, W4 = al("W1", 1024), al("W2", 1024), al("W3", 1024), al("W4", 1024)
        RQ, RK = al("RQ", 512), al("RK", 512)
        T1, T2 = al("T1", 256), al("T2", 256)
        lb, oml, gn = al("lb", 1024), al("oml", 1024), al("gn", 128)
        L3 = al("L3", 3072)
        rot = al("rot", 256)
        MIX = al("MIX", 2048)
        st = al("st", 32)
        DEC = al("DEC", 8)
        ro = [R0]

        def ral(name, n):
            b_ = ar.alias(name, self.AR.t[:, ro[0]:ro[0] + n])
            ro[0] += n
            assert ro[0] <= R1
            return b_
        LF, EB, ENB, Bs = ral("LF", 1024), ral("EB", 1024), ral("ENB", 1024), ral("Bs", 1024)
        QT, KT, QB = ral("QT", 1024), ral("KT", 1024), ral("QB", 512)
        wb = self.WBF.t
        SBF = ar.alias("SBF", wb[:, 0:2048])
        SBFh = [ar.alias("SBFh%d" % h, wb[:, h * 128:(h + 1) * 128]) for h in range(8)] + \
               [ar.alias("SBFr%d" % h, wb[:, 1024 + h * 256:1024 + (h + 1) * 256]) for h in range(4)]
        VBF = ar.alias("VBF", wb[:, 2048:4096])
        KEB = ar.alias("KEB", wb[:, 4096:5632])
        TRBp = [ar.alias("TRB%d" % i, wb[:, 5632 + 256 * i:5632 + 256 * (i + 1)]) for i in range(2)]
        PTBp = [ar.alias("PTB%d" % i, wb[:, 6144 + 64 * i:6144 + 64 * (i + 1)]) for i in range(2)]
        maskT, ones64, SEG, GQK = self.maskT, self.ones64, self.SEG, self.GQK
        for r in range(3):
            self.bload("sp", ar.alias("L3v", L3.t[:, r * 1024:(r + 1) * 1024]), 1024, self.lb_logits, self.lb_logits[r])
        P.barrier()
        self.ACT(L3[:], L3[:], AF.Exp, [L3], [L3])
        self.TT("dve", W1[:], L3[:, 0:1024], L3[:, 1024:2048], ALU.add, [L3], [W1])
        self.TT("dve", W1[:], W1[:], L3[:, 2048:3072], ALU.add, [L3, W1], [W1])
        P.op("dve", lambda e: e.reciprocal(out=W1[:], in_=W1[:]), [W1], [W1])
        self.TT("dve", lb[:], L3[:, 0:1024], W1[:], ALU.mult, [L3, W1], [lb])
        self.TS("dve", oml[:], lb[:], -1.0, 1.0, ALU.mult, ALU.add, [lb], [oml])
        self.bload("sp", gn, 128, self.hgrn_norm_g, self.hgrn_norm_g)
        self.MEMSET("dve", S[0][:], 0.0, [S[0]])
        for sb_ in SBFh:
            self.MEMSET("dve", sb_[:, :], 0.0, [sb_])
        self._cur_seq = 0
        proj = self.proj_d

        def hgrn_elem(n):
            pp = slice(0, n)
            self.ACT(W1[pp, :], Pt[pp, 1024:2048], AF.Sigmoid, [Pt], [W1])
            self.TT("dve", W1[pp, :], W1[pp, :], oml[pp, :], ALU.mult, [W1, oml], [W1])
            self.TT("dve", W1[pp, :], W1[pp, :], lb[pp, :], ALU.add, [W1, lb], [W1])
            self.TS("dve", W2[pp, :], W1[pp, :], -1.0, 1.0, ALU.mult, ALU.add, [W1], [W2])
            self.ACT(W3[pp, :], Pt[pp, 0:1024], AF.Silu, [Pt], [W3])

        def rotary(n):
            pp = slice(0, n)
            for (dst, c0, ci, si) in ((RQ, 4096, 0, 1), (RK, 4608, 2, 3)):
                xin = v3(Pt[pp, c0:c0 + 512], 4)
                x1, x2 = xin[:, :, 0:64], xin[:, :, 64:128]
                cosb = bc(rot[pp, ci * 64:(ci + 1) * 64], [n, 4, 64], 1)
                sinb = bc(rot[pp, si * 64:(si + 1) * 64], [n, 4, 64], 1)
                d3 = v3(dst[pp, :], 4)
                t1, t2 = v3(T1[pp, :], 4), v3(T2[pp, :], 4)
                self.TT("dve", t1, x1, cosb, ALU.mult, [Pt, rot], [T1])
                self.TT("dve", t2, x2, sinb, ALU.mult, [Pt, rot], [T2])
                self.TT("dve", d3[:, :, 0:64], t1, t2, ALU.subtract, [T1, T2], [dst])
                self.TT("dve", t1, x1, sinb, ALU.mult, [Pt, rot], [T1])
                self.TT("dve", t2, x2, cosb, ALU.mult, [Pt, rot], [T2])
                self.TT("dve", d3[:, :, 64:128], t1, t2, ALU.add, [T1, T2], [dst])

        def post(n, rows):
            pp = slice(0, n)
            for j in range(2):
                self.ACT(W4[pp, 512 * j:512 * (j + 1)], PS[j][pp, :], AF.Square, [PS[j]], [W4])
            self.RED(st[pp, 0:8], v3(W4[pp, :], 8), ALU.add, [W4], [st])
            self.rsqrt(st[pp, 0:8], st[pp, 0:8], 1.0 / 128, 1e-6, [st], [st])
            for j in range(2):
                self.TT("dve", v3(MIX[pp, 512 * j:512 * (j + 1)], 4), v3(PS[j][pp, :], 4),
                        bc(st[pp, 4 * j:4 * j + 4], [n, 4, 128], 2), ALU.mult, [PS[j], st], [MIX])
            self.TT("dve", v3(MIX[pp, 0:1024], 8), v3(MIX[pp, 0:1024], 8), bc(gn[pp, :], [n, 8, 128], 1), ALU.mult,
                    [MIX, gn], [MIX])
            self.ACT(W4[pp, :], Pt[pp, 3072:4096], AF.Silu, [Pt], [W4])
            self.TT("dve", MIX[pp, 0:1024], MIX[pp, 0:1024], W4[pp, :], ALU.mult, [MIX, W4], [MIX])
            for j in range(2):
                self.CP("act", W1[pp, 512 * j:512 * (j + 1)], PS[2 + j][pp, :], [PS[2 + j]], [W1])
            self.RED(st[pp, 8:12], v3(W1[pp, :], 4), ALU.add, [W1], [st])
            self.TS("dve", st[pp, 8:12], st[pp, 8:12], -1.0 / 256, None, ALU.mult, None, [st], [st])
            self.TT("dve", v3(W1[pp, :], 4), v3(W1[pp, :], 4), bc(st[pp, 8:12], [n, 4, 256], 2), ALU.add, [W1, st], [W1])
            self.ACT(W2[pp, :], W1[pp, :], AF.Square, [W1], [W2])
            self.RED(st[pp, 12:16], v3(W2[pp, :], 4), ALU.add, [W2], [st])
            self.rsqrt(st[pp, 12:16], st[pp, 12:16], 1.0 / 256, 1e-5, [st], [st])
            self.TT("dve", v3(MIX[pp, 1024:2048], 4), v3(W1[pp, :], 4), bc(st[pp, 12:16], [n, 4, 256], 2), ALU.mult,
                    [W1, st], [MIX])
            self.ACT(W4[pp, :], Pt[pp, 6144:7168], AF.Silu, [Pt], [W4])
            self.TT("dve", MIX[pp, 1024:2048], MIX[pp, 1024:2048], W4[pp, :], ALU.mult, [MIX, W4], [MIX])
            self.DMA("sp", self.mix_d[rows, :], MIX[pp, :], MIX, self.mix_d)

        S0 = S[0]
        g64 = [(1.0 - 2.0 ** (-5.0 - h)) ** 64 for h in range(4)]
        pp = slice(0, 64)
        for c in range(2 * self.NF):
            r0 = 64 * c
            rows = slice(r0, r0 + 64)
            self.DMA("sp", Pt[pp, :], proj[rows, 0:EVEN_IN], proj, Pt)
            self.DMA("sp", v3(rot[pp, :], 4), self.rot[:, rows, :].rearrange("a p c -> p a c"), self.rot, rot)
            hgrn_elem(64)
            self.ACT(LF[pp, :], W1[pp, :], AF.Ln, [W1], [LF])
            self.CP("act", VBF[pp, 0:1024], Pt[pp, 2048:3072], [Pt], [VBF])
            self.CP("act", VBF[pp, 1024:2048], Pt[pp, 5120:6144], [Pt], [VBF])
            for j in range(2):
                self.MM(PS[4 + j][pp, :], maskT[:], LF[pp, 512 * j:512 * (j + 1)], True, True, [maskT, LF], [PS[4 + j]])
                self.MM(PS[6 + j][pp, :], ones64[:], LF[pp, 512 * j:512 * (j + 1)], True, True, [ones64, LF], [PS[6 + j]])
            for h in range(8):
                self.MM(PS[0][:, h:h + 1], LF[pp, h * 128:(h + 1) * 128], ones64[:, 0:1], True, True, [LF, ones64], [PS[0]])
            self.ACT(DEC[:, 0:8], PS[0][:, 0:8], AF.Exp, [PS[0]], [DEC])
            for j in range(2):
                cs = slice(512 * j, 512 * (j + 1))
                self.CP("act", Bs[pp, cs], PS[4 + j][pp, :], [PS[4 + j]], [Bs])
                self.ACT(EB[pp, cs], PS[4 + j][pp, :], AF.Exp, [PS[4 + j]], [EB])
                self.ACT(ENB[pp, cs], PS[4 + j][pp, :], AF.Exp, [PS[4 + j]], [ENB], scale=-1.0)
                self.TT("dve", Bs[pp, cs], PS[6 + j][pp, :], Bs[pp, cs], ALU.subtract, [PS[6 + j], Bs], [Bs])
            self.ACT(Bs[pp, :], Bs[pp, :], AF.Exp, [Bs], [Bs])
            self.TT("dve", QT[pp, :], W3[pp, :], EB[pp, :], ALU.mult, [W3, EB], [QT])
            self.TT("dve", KT[pp, :], W2[pp, :], ENB[pp, :], ALU.mult, [W2, ENB], [KT])
            self.TT("dve", KEB[pp, 0:1024], W2[pp, :], Bs[pp, :], ALU.mult, [W2, Bs], [KEB])
            def h_stage1(h):
                par = h % 2
                hs = slice(h * 128, (h + 1) * 128)
                pbT = PS[4 + par]
                self.TR(pbT[:, 0:64], QT[pp, hs], [QT], [pbT])
                self.TR(pbT[:, 64:128], KT[pp, hs], [KT], [pbT])
                self.CP("act", TRBp[par][:, 0:128], pbT[:, 0:128], [pbT], [TRBp[par]])

            def h_stage2(h):
                par = h % 2
                hs = slice(h * 128, (h + 1) * 128)
                pbS = PS[6 + par]
                trb, ptb = TRBp[par], PTBp[par]
                qTb, kTb = trb[:, 0:64], trb[:, 64:128]
                self.MM(pbS[pp, 0:64], kTb, qTb, True, True, [trb], [pbS])
                self.TT("dve", ptb[pp, 0:64], pbS[pp, 0:64], maskT[:], ALU.mult, [pbS, maskT], [ptb])
                ob = PS[h // 4][pp, (h % 4) * 128:(h % 4 + 1) * 128]
                self.MM(ob, ptb[pp, 0:64], VBF[pp, hs], True, False, [ptb, VBF], [PS[h // 4]])
                self.MM(ob, qTb, SBFh[h][:, :], False, True, [trb, SBFh[h]], [PS[h // 4]])
                self.MM(pbS[:, 128:256], KEB[pp, hs], VBF[pp, hs], True, True, [KEB, VBF], [pbS])
                self.STT("dve", S0[:, hs], S0[:, hs], DEC[:, h:h + 1], pbS[:, 128:256], ALU.mult, ALU.add,
                         [S0, DEC, pbS], [S0])
                self.CP("act", SBFh[h][:, :], S0[:, hs], [S0], [SBFh[h]])

            for h in range(9):
                if h < 8:
                    h_stage1(h)
                if h > 0:
                    h_stage2(h - 1)
            rotary(64)
            self.TT("dve", v3(QB[pp, :], 4), v3(RQ[pp, :], 4), bc(GQK[:, 0:4], [64, 4, 128], 2), ALU.mult, [RQ, GQK], [QB])
            self.TT("dve", v3(KEB[pp, 1024:1536], 4), v3(RK[pp, :], 4), bc(GQK[:, 4:8], [64, 4, 128], 2), ALU.mult,
                    [RK, GQK], [KEB])

            def r_stage1(h):
                par = h % 2
                hs = slice(h * 128, (h + 1) * 128)
                pbT = PS[4 + par]
                self.TR(pbT[:, 0:64], RQ[pp, hs], [RQ], [pbT])
                self.TR(pbT[:, 64:128], RK[pp, hs], [RK], [pbT])
                self.TR(pbT[:, 128:192], QB[pp, hs], [QB], [pbT])
                self.CP("act", TRBp[par][:, 0:192], pbT[:, 0:192], [pbT], [TRBp[par]])

            def r_stage2(h):
                par = h % 2
                vs = slice(1024 + h * 256, 1024 + (h + 1) * 256)
                pbS = PS[6 + par]
                trb, ptb = TRBp[par], PTBp[par]
                qTb, kTb, qbTb = trb[:, 0:64], trb[:, 64:128], trb[:, 128:192]
                self.MM(pbS[pp, 0:64], kTb, qTb, True, True, [trb], [pbS])
                self.TT("dve", ptb[pp, 0:64], pbS[pp, 0:64], SEG[:, h * 64:(h + 1) * 64], ALU.mult, [pbS, SEG], [ptb])
                ob = PS[2 + h // 2][pp, (h % 2) * 256:(h % 2 + 1) * 256]
                self.MM(ob, ptb[pp, 0:64], VBF[pp, vs], True, False, [ptb, VBF], [PS[2 + h // 2]])
                self.MM(ob, qbTb, SBFh[8 + h][:, :], False, True, [trb, SBFh[8 + h]], [PS[2 + h // 2]])
                self.MM(pbS[:, 128:384], KEB[pp, 1024 + h * 128:1024 + (h + 1) * 128], VBF[pp, vs], True, True,
                        [KEB, VBF], [pbS])
                self.STT("dve", S0[:, vs], S0[:, vs], g64[h], pbS[:, 128:384], ALU.mult, ALU.add, [S0, pbS], [S0])
                self.CP("act", SBFh[8 + h][:, :], S0[:, vs], [S0], [SBFh[8 + h]])

            for h in range(5):
                if h < 4:
                    r_stage1(h)
                if h > 0:
                    r_stage2(h - 1)
            post(64, rows)
        P.barrier()
        for h in range(4):
            gam = 1.0 - 2.0 ** (-5.0 - h)
            self.MEMSET("dve", Dc[:, (8 + 2 * h) * 128:(10 + 2 * h) * 128], gam, [Dc])
        for i in range(self.NF, self.NT):
            rows = slice(i * 128, (i + 1) * 128)
            self.DMA("sp", Pt[:], proj[rows, 0:EVEN_IN], proj, Pt)
            self.DMA("sp", v3(rot[:], 4), self.rot[:, rows, :].rearrange("a p c -> p a c"), self.rot, rot)
            hgrn_elem(128)
            rotary(128)
            pbi = 0
            for (src, dstc) in ((W1, Dc), (W2, Kc), (W3, Qc)):
                for hb in range(2):
                    pb = PS[4 + pbi % 4]
                    pbi += 1
                    for k4 in range(4):
                        h = hb * 4 + k4
                        self.TR(pb[:, k4 * 128:(k4 + 1) * 128], src[:, h * 128:(h + 1) * 128], [src], [pb])
                    self.CP("act", dstc[:, hb * 512:(hb + 1) * 512], pb[:], [pb], [dstc])
            for (src, dstc) in ((RK, Kc), (RQ, Qc)):
                pb = PS[4 + pbi % 4]
                pbi += 1
                for h in range(4):
                    self.TR(pb[:, h * 128:(h + 1) * 128], src[:, h * 128:(h + 1) * 128], [src], [pb])
                o = dstc[:, 1024:2048].rearrange("p (h r t) -> p h r t", h=4, r=2)
                self.CP("act", o, bc(v3(pb[:], 4), [128, 4, 2, 128], 2), [pb], [dstc])
            cntr = [0]

            def step(m, row, Sb, first, last):
                vb = Vb[cntr[0] % 2]
                cntr[0] += 1
                self.DMA("sp", vb[:, 0:1024], proj[row, 2048:3072].partition_broadcast(128), proj, vb)
                self.DMA("sp", vb[:, 1024:2048], proj[row, 5120:6144].partition_broadcast(128), proj, vb)
                col = lambda c_: (v3(c_[:], 16)[:, :, m], c_)
                self.scan_step(m, first, last, Sb, 16, 128, col(Dc), col(Kc), col(Qc), vb, vb[:], tmpKV,
                               [(PS[j], 512 * j, 512, self.Z) for j in range(4)], None)

            self.run_scan(i, S, 2048, self.st_even, 0, self.So_even, step, None)
            post(128, rows)

    def odd_pads(self):
        P, proj, NP = self.P, self.proj_d, self.NP
        P.barrier()
        q = "sp"
        self.DMA(q, self.xbcp_d[0:3, :], self.zeros_d[0:3, 0:1536], self.zeros_d, self.xbcp_d)
        self.DMA(q, self.xbcp_d[3:3 + NP, :], proj[0:NP, 1024:2560], proj, self.xbcp_d)
        self.DMA(q, self.xbcs_d[0:3], self.st_conv[:], self.st_conv, self.xbcs_d)
        self.DMA(q, self.xbcs_d[3:7].rearrange("t s c -> (t s) c"), proj[NP:NP + 64, 1024:2560], proj, self.xbcs_d)
        self.DMA(q, self.rwp_d[0:1, :], self.zeros_d[0:1, :], self.zeros_d, self.rwp_d)
        self.DMA(q, self.rwp_d[1:1 + NP, :], proj[0:NP, 2576:5936], proj, self.rwp_d)
        self.DMA(q, self.rws_d[0], self.st_shift[:], self.st_shift, self.rws_d)
        self.DMA(q, self.rws_d[1:5].rearrange("t s c -> (t s) c"), proj[NP:NP + 64, 2576:5936], proj, self.rws_d)
        self.DMA(q, self.conv_o[:, 0, :], proj[NP - 3:NP, 1024:2560], proj, self.conv_o)
        for t in range(3):
            self.DMA(q, self.conv_o[t, 1:NSEQ + 1, :], proj[NP + 16 * (t + 1):NP + 16 * (t + 2), 1024:2560], proj,
                     self.conv_o)
        self.DMA(q, self.shift_o[0:1, :], proj[NP - 1:NP, 2576:5936], proj, self.shift_o)
        self.DMA(q, self.shift_o[1:NSEQ + 1, :], proj[NP + 48:NP + 64, 2576:5936], proj, self.shift_o)

    def shifted_load(self, i, dst, pad_p, pad_s, lead, shift, ncol):
        if i < self.NF:
            r = i * 128 + lead - shift
            self.DMA("sp", dst[:, 0:ncol], pad_p[r:r + 128, :], pad_p, dst)
        else:
            r = i * 128 + lead - shift
            self.DMA("sp", dst[0:16, 0:ncol], pad_p[r:r + 16, :], pad_p, dst)
            self.DMA("sp", dst[16:80, 0:ncol], pad_s[lead - shift:lead - shift + 4].rearrange("t s c -> (t s) c"),
                     pad_s, dst)

    def ssd_pass(self):
        P, ar, PS = self.P, self.ar, self.PS
        P.barrier()
        ar.reset()
        al = ar.alloc
        proj = self.proj_d
        Zt, DT = al("Zt", 1024), al("DT", 16)
        XS = [al("XS%d" % s, 1536) for s in range(4)]
        ACC, TMP = al("ACC", 1536), al("TMP", 1536)
        Kc, Qc = al("Kc", 2048), al("Qc", 2048)
        Vb = [al("Vb0", 1040), al("Vb1", 1040)]
        S = [al("S0", 1024), al("S1", 1024)]
        tmpKV = al("tmpKV", 1024)
        CW = al("CW", 4 * 1536)
        CB = al("CB", 1536)
        dtb, Ab, dsk = al("dtb", 16), al("Ab", 16), al("dsk", 16)
        sg = al("sg", 1024)
        VR = al("VR", 1040)
        MIX = al("MIX", 1024)
        st = al("st", 8)
        SEGT, RB, BLK, O2 = al("SEGT", 1024), al("RB", 1024), al("BLK", 1024), al("O2", 1024)
        MB, NBT = al("MB", 64), al("NBT", 64)
        LFs, Bsb, EBt, WE, DECB = al("LFs", 16), al("Bsb", 16), al("EBt", 16), al("WE", 16), al("DECB", 16)
        ones128 = al("ones128", 128)
        wb = self.WBF.t
        SBF = ar.alias("SBF", wb[:, 0:1024])
        VB = ar.alias("VB", wb[:, 1024:2048])
        VE = ar.alias("VE", wb[:, 2048:3072])
        BMb = ar.alias("BMb", wb[:, 3072:3328])
        TRB = ar.alias("TRB", wb[:, 3328:3584])
        PT = ar.alias("PT", wb[:, 3584:4608])
        maskT, ones64, dm, ident = self.maskT, self.ones64, self.dm, self.ident
        for j in range(4):
            self.bload("sp", ar.alias("CWj", CW.t[:, j * 1536:(j + 1) * 1536]), 1536, self.conv_w, self.conv_w[j])
        self.bload("sp", CB, 1536, self.conv_b, self.conv_b)
        self.bload("sp", dtb, 16, self.dt_bias, self.dt_bias)
        self.bload("sp", Ab, 16, self.a_log, self.a_log)
        self.bload("sp", dsk, 16, self.d_skip, self.d_skip)
        self.bload("sp", sg, 1024, self.ssm_norm_g, self.ssm_norm_g)
        P.barrier()
        self.ACT(Ab[:], Ab[:], AF.Exp, [Ab], [Ab])
        for xs in XS:
            self.MEMSET("dve", xs[:], 0.0, [xs])
        self.MEMSET("dve", S[0][:], 0.0, [S[0]])
        self.MEMSET("dve", SBF[:], 0.0, [SBF])
        self.MEMSET("dve", ones128[0:64, :], 1.0, [ones128])
        P.op("pool", lambda e: e.iota(v3(BLK[0:16, :], 16), pattern=[[1, 16], [0, 64]], base=0, channel_multiplier=-1,
                                      allow_small_or_imprecise_dtypes=True), [], [BLK])
        self.TS("dve", BLK[0:16, :], BLK[0:16, :], 0.0, None, ALU.is_equal, None, [BLK], [BLK])
        self.TS("dve", MB[0:64, :], dm[:], 0.0, -30000.0, ALU.is_lt, ALU.mult, [dm], [MB])
        self._cur_seq = 0

        def prep(n, XSl):
            pp = slice(0, n)
            self.TT("dve", ACC[pp, :], XSl[0][pp, :], CW[pp, 3 * 1536:4 * 1536], ALU.mult, [XSl[0], CW], [ACC])
            for s_ in range(1, 4):
                self.TT("dve", TMP[pp, :], XSl[s_][pp, :], CW[pp, (3 - s_) * 1536:(4 - s_) * 1536], ALU.mult,
                        [XSl[s_], CW], [TMP])
                self.TT("dve", ACC[pp, :], ACC[pp, :], TMP[pp, :], ALU.add, [ACC, TMP], [ACC])
            self.TT("dve", ACC[pp, :], ACC[pp, :], CB[pp, :], ALU.add, [ACC, CB], [ACC])
            self.ACT(ACC[pp, :], ACC[pp, :], AF.Silu, [ACC], [ACC])
            self.TT("dve", DT[pp, :], DT[pp, :], dtb[pp, :], ALU.add, [DT, dtb], [DT])
            self.ACT(DT[pp, :], DT[pp, :], AF.Exp, [DT], [DT])
            self.ACT(DT[pp, :], DT[pp, :], AF.Ln, [DT], [DT], bias=1.0)

        def post(n, rows):
            pp = slice(0, n)
            self.TT("dve", v3(TMP[pp, 0:1024], 16), v3(ACC[pp, 0:1024], 16), bc(dsk[pp, :], [n, 16, 64], 2), ALU.mult,
                    [ACC, dsk], [TMP])
            self.TT("dve", MIX[pp, :], MIX[pp, :], TMP[pp, 0:1024], ALU.add, [MIX, TMP], [MIX])
            self.ACT(Zt[pp, :], Zt[pp, :], AF.Silu, [Zt], [Zt])
            self.TT("dve", MIX[pp, :], MIX[pp, :], Zt[pp, :], ALU.mult, [MIX, Zt], [MIX])
            for g in range(2):
                self.ACT(TMP[pp, 0:512], MIX[pp, 512 * g:512 * (g + 1)], AF.Square, [MIX], [TMP, st],
                         accum=st[pp, g:g + 1])
            self.rsqrt(st[pp, 0:2], st[pp, 0:2], 1.0 / 512, 1e-6, [st], [st])
            self.TT("dve", v3(MIX[pp, :], 2), v3(MIX[pp, :], 2), bc(st[pp, 0:2], [n, 2, 512], 2), ALU.mult, [MIX, st], [MIX])
            self.TT("dve", MIX[pp, :], MIX[pp, :], sg[pp, :], ALU.mult, [MIX, sg], [MIX])
            self.DMA("sp", self.mix_d[rows, 0:1024], MIX[pp, :], MIX, self.mix_d)

        pp = slice(0, 64)
        S0 = S[0]
        for c in range(2 * self.NF):
            r0 = 64 * c
            rows = slice(r0, r0 + 64)
            self.DMA("sp", Zt[pp, :], proj[rows, 0:1024], proj, Zt)
            self.DMA("sp", DT[pp, :], proj[rows, 2560:2576], proj, DT)
            for s_ in range(4):
                self.DMA("sp", XS[s_][pp, :], self.xbcp_d[r0 + 3 - s_:r0 + 3 - s_ + 64, :], self.xbcp_d, XS[s_])
            prep(64, XS)
            self.STT("dve", LFs[pp, :], DT[pp, :], -1.0, Ab[pp, :], ALU.mult, ALU.mult, [DT, Ab], [LFs])
            self.MM(PS[6][pp, 0:16], maskT[:], LFs[pp, :], True, True, [maskT, LFs], [PS[6]])
            self.MM(PS[6][pp, 16:32], ones64[:], LFs[pp, :], True, True, [ones64, LFs], [PS[6]])
            self.MM(PS[6][:, 32:48], ones128[0:64, :], LFs[pp, :], True, True, [ones128, LFs], [PS[6]])
            self.CP("act", Bsb[pp, :], PS[6][pp, 0:16], [PS[6]], [Bsb])
            self.ACT(EBt[pp, :], PS[6][pp, 0:16], AF.Exp, [PS[6]], [EBt])
            self.TT("dve", WE[pp, :], PS[6][pp, 16:32], Bsb[pp, :], ALU.subtract, [PS[6], Bsb], [WE])
            self.ACT(WE[pp, :], WE[pp, :], AF.Exp, [WE], [WE])
            self.ACT(DECB[:, :], PS[6][:, 32:48], AF.Exp, [PS[6]], [DECB])
            self.TT("dve", v3(O2[pp, :], 16), v3(ACC[pp, 0:1024], 16), bc(DT[pp, :], [64, 16, 64], 2), ALU.mult,
                    [ACC, DT], [O2])
            self.CP("act", VB[pp, :], O2[pp, :], [O2], [VB])
            self.TT("dve", v3(VE[pp, :], 16), v3(O2[pp, :], 16), bc(WE[pp, :], [64, 16, 64], 2), ALU.mult, [O2, WE], [VE])
            self.CP("act", BMb[pp, :], ACC[pp, 1024:1280], [ACC], [BMb])
            self.TR(PS[6][0:16, 64:128], Bsb[pp, :], [Bsb], [PS[6]])
            self.ACT(NBT[0:16, :], PS[6][0:16, 64:128], AF.Copy, [PS[6]], [NBT], scale=-1.0)
            self.TT("dve", v3(RB[pp, :], 16), bc(Bsb[pp, :], [64, 16, 64], 2), bc(ident[0:64, 0:64], [64, 16, 64], 1),
                    ALU.mult, [Bsb, ident], [RB])
            for j in range(2):
                cs = slice(512 * j, 512 * (j + 1))
                self.MM(PS[4 + j][pp, :], NBT[0:16, :], BLK[0:16, cs], True, False, [NBT, BLK], [PS[4 + j]])
                self.MM(PS[4 + j][pp, :], ones64[:], RB[pp, cs], False, True, [ones64, RB], [PS[4 + j]])
                self.STT("dve", v3(SEGT[pp, cs], 8), v3(PS[4 + j][pp, :], 8), 0.0, bc(MB[pp, :], [64, 8, 64], 1),
                         ALU.min, ALU.add, [PS[4 + j], MB], [SEGT])
            self.ACT(SEGT[pp, :], SEGT[pp, :], AF.Exp, [SEGT], [SEGT])
            for g in range(2):
                self.TR(PS[7][:, g * 128:g * 128 + 64], ACC[pp, 1024 + g * 128:1024 + (g + 1) * 128], [ACC], [PS[7]])
                self.TR(PS[7][:, g * 128 + 64:(g + 1) * 128], ACC[pp, 1280 + g * 128:1280 + (g + 1) * 128], [ACC], [PS[7]])
            self.CP("act", TRB[:, 0:256], PS[7][:, 0:256], [PS[7]], [TRB])
            for g in range(2):
                BTg, CTg = TRB[:, g * 128:g * 128 + 64], TRB[:, g * 128 + 64:(g + 1) * 128]
                self.MM(PS[7][pp, 256 + g * 64:256 + (g + 1) * 64], BTg, CTg, True, True, [TRB], [PS[7]])
            for g in range(2):
                cs = slice(512 * g, 512 * (g + 1))
                self.TT("dve", v3(PT[pp, cs], 8), v3(SEGT[pp, cs], 8),
                        bc(PS[7][pp, 256 + g * 64:256 + (g + 1) * 64], [64, 8, 64], 1), ALU.mult, [SEGT, PS[7]], [PT])
            for h in range(16):
                hs = slice(h * 64, (h + 1) * 64)
                self.MM(PS[h // 8][pp, (h % 8) * 64:(h % 8 + 1) * 64], PT[pp, hs], VB[pp, hs], True, True,
                        [PT, VB], [PS[h // 8]])
            for g in range(2):
                cs = slice(512 * g, 512 * (g + 1))
                CTg = TRB[:, g * 128 + 64:(g + 1) * 128]
                self.MM(PS[2 + g][pp, :], CTg, SBF[:, cs], True, True, [TRB, SBF], [PS[2 + g]])
                self.TT("dve", v3(O2[pp, cs], 8), v3(PS[2 + g][pp, :], 8), bc(EBt[pp, 8 * g:8 * g + 8], [64, 8, 64], 2),
                        ALU.mult, [PS[2 + g], EBt], [O2])
                self.TT("dve", MIX[pp, cs], PS[g][pp, :], O2[pp, cs], ALU.add, [PS[g], O2], [MIX])
            for g in range(2):
                cs = slice(512 * g, 512 * (g + 1))
                self.MM(PS[4 + g][:, :], BMb[pp, g * 128:(g + 1) * 128], VE[pp, cs], True, True, [BMb, VE], [PS[4 + g]])
                self.TT("dve", v3(S0[:, cs], 8), v3(S0[:, cs], 8), bc(DECB[:, 8 * g:8 * g + 8], [128, 8, 64], 2), ALU.mult,
                        [S0, DECB], [S0])
                self.TT("dve", S0[:, cs], S0[:, cs], PS[4 + g][:, :], ALU.add, [S0, PS[4 + g]], [S0])
            self.CP("act", SBF[:, :], S0[:, :], [S0], [SBF])
            post(64, rows)
        P.barrier()
        for i in range(self.NF, self.NT):
            rows = slice(i * 128, (i + 1) * 128)
            self.DMA("sp", Zt[:], proj[rows, 0:1024], proj, Zt)
            self.DMA("sp", DT[:], proj[rows, 2560:2576], proj, DT)
            for s_ in range(4):
                self.shifted_load(i, XS[s_], self.xbcp_d, self.xbcs_d, 3, s_, 1536)
            prep(128, XS)
            self.TT("dve", VR[:, 1024:1040], DT[:], Ab[:], ALU.mult, [DT, Ab], [VR])
            self.ACT(VR[:, 1024:1040], VR[:, 1024:1040], AF.Exp, [VR], [VR], scale=-1.0)
            self.TT("dve", v3(VR[:, 0:1024], 16), v3(ACC[:, 0:1024], 16), bc(DT[:], [128, 16, 64], 2), ALU.mult,
                    [ACC, DT], [VR])
            self.DMA("sp", self.vrow_d[rows, 0:1040], VR[:], VR, self.vrow_d)
            for (c0, dstc, pb) in ((1024, Kc, PS[4]), (1280, Qc, PS[5])):
                for g in range(2):
                    self.TR(pb[:, g * 128:(g + 1) * 128], ACC[:, c0 + g * 128:c0 + (g + 1) * 128], [ACC], [pb])
                o = dstc[:].rearrange("p (g h t) -> p g h t", g=2, h=8)
                self.CP("act", o, bc(v3(pb[:, 0:256], 2), [128, 2, 8, 128], 2), [pb], [dstc])
            cntr = [0]

            def step(m, row, Sb, first, last):
                vb = Vb[cntr[0] % 2]
                cntr[0] += 1
                self.DMA("sp", vb[:], self.vrow_d[row, 0:1040].partition_broadcast(128), self.vrow_d, vb)
                col = lambda c_: (v3(c_[:], 16)[:, :, m], c_)
                self.scan_step(m, first, last, Sb, 16, 64, (vb[:, 1024:1040], vb), col(Kc), col(Qc), vb,
                               vb[:, 0:1024], tmpKV, [(PS[j], 512 * j, 512, self.Z) for j in range(2)], None)

            self.run_scan(i, S, 1024, self.st_odd, 0, self.So_odd, step, None)
            for j in range(2):
                self.CP("act", MIX[:, 512 * j:512 * (j + 1)], PS[j][:], [PS[j]], [MIX])
            post(128, rows)

    def rwkv_pass(self):
        P, ar, PS = self.P, self.ar, self.PS
        P.barrier()
        ar.reset()
        al = ar.alloc
        proj = self.proj_d
        RW, PV, MU = al("RW", 3360), al("PV", 3360), al("MU", 3360)
        Dc, Kc, Qc, NKc, KAc = al("Dc", 1024), al("Kc", 1024), al("Qc", 1024), al("NKc", 1024), al("KAc", 1024)
        Vb = [al("Vb0", 512), al("Vb1", 512)]
        S = [al("S0", 512), al("S1", 512)]
        tmpKV, tmpA, Ub = al("tmpKV", 512), al("tmpA", 512), al("Ub", 512)
        VEC = al("VEC", 7 * 1024)
        w2, a2, g2 = al("w2", 1024), al("a2", 1024), al("g2", 2048)
        WD, AA, GG, KK, KP, T1 = al("WD", 1024), al("AA", 1024), al("GG", 1024), al("KK", 1024), al("KP", 1024), al("T1", 1024)
        KA = al("KA", 1024)
        NK = KK
        TW = al("TW", 384)
        SGd = al("SGd", 160)
        st = al("st", 64)
        vec = lambda j: VEC[:, j * 1024:(j + 1) * 1024]
        w0b, a0b, kkb, kab, rkb, lgb, lbb = [vec(j) for j in range(7)]
        for j in range(7):
            self.bload("sp", ar.alias("VECj", VEC.t[:, j * 1024:(j + 1) * 1024]), 1024, self.rwkv_vec, self.rwkv_vec[j])
        self.bload("sp", MU, 3360, self.shift_mu, self.shift_mu)
        self.DMA("sp", w2[0:64, :], self.rwkv_w2[:], self.rwkv_w2, w2)
        self.DMA("sp", a2[0:64, :], self.rwkv_a2[:], self.rwkv_a2, a2)
        self.DMA("sp", g2[:, 0:1024], self.rwkv_g2[0:128, :], self.rwkv_g2, g2)
        self.DMA("sp", g2[0:32, 1024:2048], self.rwkv_g2[128:160, :], self.rwkv_g2, g2)
        P.barrier()
        self.MEMSET("dve", PV[:], 0.0, [PV])
        self.MEMSET("dve", RW[:], 0.0, [RW])
        self.MEMSET("dve", S[0][:], 0.0, [S[0]])
        self._cur_seq = 0
        vrow = self.vrow_d
        for i in range(self.NT):
            rows = slice(i * 128, (i + 1) * 128)
            self.DMA("sp", RW[:], proj[rows, 2576:5936], proj, RW)
            self.shifted_load(i, PV, self.rwp_d, self.rws_d, 1, 1, 3360)
            self.TT("dve", PV[:], PV[:], RW[:], ALU.subtract, [PV, RW], [PV])
            self.TT("dve", PV[:], PV[:], MU[:], ALU.mult, [PV, MU], [PV])
            self.TT("dve", RW[:], RW[:], PV[:], ALU.add, [RW, PV], [RW])
            r_, k_, v_ = RW[:, 0:1024], RW[:, 1024:2048], RW[:, 2048:3072]
            self.DMA("sp", vrow[rows, 1040:2064], v_, RW, vrow)
            self.ACT(SGd[:, 0:64], RW[:, 3072:3136], AF.Tanh, [RW], [SGd])
            self.TR(PS[4][0:64, 0:128], SGd[:, 0:64], [SGd], [PS[4]])
            self.TR(PS[4][0:64, 128:256], RW[:, 3136:3200], [RW], [PS[4]])
            self.CP("act", TW[0:64, 0:256], PS[4][0:64, 0:256], [PS[4]], [TW])
            for (lo, wgt, dst, bias) in ((0, w2, WD, w0b), (128, a2, AA, a0b)):
                for j in range(2):
                    self.MM(PS[5 + j][:], TW[0:64, lo:lo + 128], wgt[0:64, 512 * j:512 * (j + 1)], True, True,
                            [TW, wgt], [PS[5 + j]])
                    self.TT("dve", dst[:, 512 * j:512 * (j + 1)], PS[5 + j][:], bias[:, 512 * j:512 * (j + 1)],
                            ALU.add, [PS[5 + j], VEC], [dst])
            self.ACT(WD[:], WD[:], AF.Exp, [WD], [WD], scale=-1.0)
            self.ACT(WD[:], WD[:], AF.Ln, [WD], [WD], bias=1.0)
            self.ACT(WD[:], WD[:], AF.Exp, [WD], [WD], scale=-1.0)
            self.ACT(WD[:], WD[:], AF.Exp, [WD], [WD], scale=-math.exp(-0.5))
            self.ACT(AA[:], AA[:], AF.Sigmoid, [AA], [AA])
            self.ACT(SGd[:], RW[:, 3200:3360], AF.Sigmoid, [RW], [SGd])
            self.TR(PS[4][:, 0:128], SGd[:, 0:128], [SGd], [PS[4]])
            self.TR(PS[4][0:32, 128:256], SGd[:, 128:160], [SGd], [PS[4]])
            self.CP("act", TW[:, 0:128], PS[4][:, 0:128], [PS[4]], [TW])
            self.CP("act", TW[0:32, 128:256], PS[4][0:32, 128:256], [PS[4]], [TW])
            for j in range(2):
                self.MM(PS[5 + j][:], TW[:, 0:128], g2[:, 512 * j:512 * (j + 1)], True, False, [TW, g2], [PS[5 + j]])
                self.MM(PS[5 + j][:], TW[0:32, 128:256], g2[0:32, 1024 + 512 * j:1024 + 512 * (j + 1)], False, True,
                        [TW, g2], [PS[5 + j]])
                self.CP("act", GG[:, 512 * j:512 * (j + 1)], PS[5 + j][:], [PS[5 + j]], [GG])
            self.TT("dve", KK[:], k_, kkb, ALU.mult, [RW, VEC], [KK])
            self.ACT(T1[:], KK[:], AF.Square, [KK], [T1])
            self.RED(st[:, 0:16], v3(T1[:], 16), ALU.add, [T1], [st])
            self.TS("dve", st[:, 0:16], st[:, 0:16], 1e-24, None, ALU.max, None, [st], [st])
            self.ACT(st[:, 0:16], st[:, 0:16], AF.Sqrt, [st], [st])
            P.op("dve", lambda e: e.reciprocal(out=st[:, 0:16], in_=st[:, 0:16]), [st], [st])
            self.TT("dve", v3(KK[:], 16), v3(KK[:], 16), bc(st[:, 0:16], [128, 16, 64], 2), ALU.mult, [KK, st], [KK])
            self.STT("dve", T1[:], AA[:], -1.0, kab, ALU.add, ALU.mult, [AA, VEC], [T1])
            self.STT("dve", KP[:], T1[:], 1.0, k_, ALU.add, ALU.mult, [T1, RW], [KP])
            self.TT("dve", KA[:], KK[:], AA[:], ALU.mult, [KK, AA], [KA])
            self.TS("dve", KK[:], KK[:], -1.0, None, ALU.mult, None, [KK], [KK])
            pbi = 0
            for (src, sb_, dstc) in ((WD[:], WD, Dc), (KP[:], KP, Kc), (r_, RW, Qc), (NK[:], NK, NKc), (KA[:], KA, KAc)):
                for hb2 in range(2):
                    pb = PS[5 + pbi % 3]
                    pbi += 1
                    for k4 in range(4):
                        hb = hb2 * 4 + k4
                        self.TR(pb[:, k4 * 128:(k4 + 1) * 128], src[:, hb * 128:(hb + 1) * 128], [sb_], [pb])
                    self.CP("act", dstc[:, hb2 * 512:(hb2 + 1) * 512], pb[:], [pb], [dstc])
            cntr = [0]

            def step(m, row, Sb, first, last, i=i):
                vb = Vb[cntr[0] % 2]
                cntr[0] += 1
                src = vrow[row, 1040:2064].rearrange("(hb k i) -> hb k i", hb=8, k=2)
                for k in range(2):
                    self.DMA("sp", v3(vb[64 * k:64 * (k + 1), :], 8), src[:, k, :].partition_broadcast(64), vrow, vb)
                col = lambda c, mm=m: v3(c[:], 8)[:, :, mm]
                S3 = v3(Sb[:], 8)
                shp = [128, 8, 64]
                TQ, TQ32 = self.TQ, self.TQ32

                def pre(mm):
                    self.TT("dve", v3(tmpA[:], 8), S3, bc(col(NKc, mm), shp, 2), ALU.mult, [Sb, NKc], [tmpA])
                    self.MM(PS[4][:], self.blk1[:], tmpA[:], True, True, [self.blk1, tmpA], [PS[4]])

                full = i < self.NF
                if (not full) or m == 0:
                    pre(m)
                self.TT("pool", v3(tmpKV[:], 8), v3(vb[:], 8), bc(col(Kc), shp, 2), ALU.mult, [vb, Kc], [tmpKV])
                self.TT("dve", S3, S3, bc(col(Dc), shp, 2), ALU.mult, [Sb, Dc], [Sb])
                self.TT("dve", S3, S3, v3(tmpKV[:], 8), ALU.add, [Sb, tmpKV], [Sb])
                self.TT("dve", v3(Ub[:], 8), v3(PS[4][:], 8), bc(col(KAc), shp, 2), ALU.mult, [PS[4], KAc], [Ub])
                self.TT("dve", S3, S3, v3(Ub[:], 8), ALU.add, [Sb, Ub], [Sb])
                if full and m < 127:
                    pre(m + 1)
                self.TT("dve", v3(TQ32[:, 0:512], 8), S3, bc(col(Qc), shp, 2), ALU.mult, [Sb, Qc], [TQ32])
                self.CP("act", TQ[:, 0:512], TQ32[:, 0:512], [TQ32], [TQ])
                for (pb, z) in ((PS[0], self.Z0), (PS[1], self.Z1)):
                    self.MM(pb[:, 0:512], z[:, 127 - m:255 - m], TQ[:, 0:512], first, last, [TQ, z], [pb])

            self.run_scan(i, S, 512, self.st_odd, 1024, self.So_odd, step, None)
            Y = T1
            Y4 = Y[:].rearrange("p (hb k i) -> p hb k i", hb=8, k=2)
            for k in range(2):
                self.CP("act", Y4[:, :, k, :], v3(PS[k][:], 8), [PS[k]], [Y])
            Y3 = v3(Y[:], 16)
            self.RED(st[:, 16:32], Y3, ALU.add, [Y], [st])
            self.TS("dve", st[:, 16:32], st[:, 16:32], -1.0 / 64, None, ALU.mult, None, [st], [st])
            self.TT("dve", Y3, Y3, bc(st[:, 16:32], [128, 16, 64], 2), ALU.add, [Y, st], [Y])
            self.ACT(WD[:], Y[:], AF.Square, [Y], [WD])
            self.RED(st[:, 32:48], v3(WD[:], 16), ALU.add, [WD], [st])
            self.rsqrt(st[:, 32:48], st[:, 32:48], 1.0 / 64, 64e-5, [st], [st])
            self.TT("dve", Y3, Y3, bc(st[:, 32:48], [128, 16, 64], 2), ALU.mult, [Y, st], [Y])
            self.TT("dve", Y[:], Y[:], lgb, ALU.mult, [Y, VEC], [Y])
            self.TT("dve", Y[:], Y[:], lbb, ALU.add, [Y, VEC], [Y])
            self.TT("dve", WD[:], r_, KP[:], ALU.mult, [RW, KP], [WD])
            self.TT("dve", WD[:], WD[:], rkb, ALU.mult, [WD, VEC], [WD])
            self.RED(st[:, 48:64], v3(WD[:], 16), ALU.add, [WD], [st])
            self.TT("dve", v3(WD[:], 16), v3(v_, 16), bc(st[:, 48:64], [128, 16, 64], 2), ALU.mult, [RW, st], [WD])
            self.TT("dve", Y[:], Y[:], WD[:], ALU.add, [Y, WD], [Y])
            self.TT("dve", Y[:], Y[:], GG[:], ALU.mult, [Y, GG], [Y])
            self.DMA("sp", self.mix_d[rows, 1024:2048], Y[:], Y, self.mix_d)

    def convert_tables(self):
        P, ar = self.P, self.ar
        P.barrier()
        ar.reset()
        stg = [ar.alloc("cst%d" % i, 8192) for i in range(2)]
        outb = [self.XT, self.WBF]
        k = 0
        for (src, dst) in ((self.peer_u[0], self.ub_d[0]), (self.peer_v[0], self.vb_d[0]),
                           (self.peer_u[1], self.ub_d[1]), (self.peer_v[1], self.vb_d[1])):
            sv = src.t.rearrange("(p r) d -> p r d", p=128)
            dv = dst.t.rearrange("(p r) d -> p r d", p=128)
            for c in range(32):
                sb_, ob = stg[k % 2], outb[k % 2]
                self.DMA("sp" if k % 2 == 0 else "pool", v3(sb_[:], 4), sv[:, 4 * c:4 * c + 4, :], src, sb_)
                self.CP("act" if k % 2 == 0 else "dve", ob[:, 0:8192], sb_[:], [sb_], [ob])
                self.DMA("sp", dv[:, 4 * c:4 * c + 4, :], v3(ob[:, 0:8192], 4), ob, dst)
                k += 1

    def peer_pass(self, l, h_d, q_d, lni, out_d):
        P, ar, PS = self.P, self.ar, self.PS
        P.barrier()
        ar.reset()
        al = ar.alloc
        Hs = [al("H0", 2048), al("H1", 2048)]
        Q, SC, LO = al("Q", 2048), al("SC", 2048), al("LO", 2048)
        TSv, TI, TIf = al("TSv", 256), al("TI", 256), al("TIf", 256)
        cand, candI = al("cand", 2048), al("candI", 2048)
        TMP, TMP2 = al("TMP", 128), al("TMP2", 256)
        BS, BJ, BJf = al("BS", 128), al("BJ", 128), al("BJf", 128)
        OH = al("OH", 4096)
        IDXf, ACTV = al("IDXf", 128), al("ACTV", 128)
        IDXs = [al("IDX0", 128), al("IDX1", 128)]
        GATEs = [al("GATE0", 128), al("GATE1", 128)]
        NG = 6
        UGf = [al("UG%d" % i, 1024) for i in range(NG)]
        UG = [Buf(b_.name, b_.t.bitcast(BF16)) for b_ in UGf]
        ar.live.extend(UG)
        DGf = [al("DG%d" % i, 64) for i in range(2)]
        DG = [Buf(b_.name, b_.t.bitcast(BF16)) for b_ in DGf]
        ROFF = al("ROFF", 1)
        junk, ACC = al("junk", 2048), al("ACC", 2048)
        G, Bb = al("G", 2048), al("Bb", 2048)
        KT32 = al("KT32", 2048)
        stA, stB = al("stA", 16), al("stB", 16)
        XT = self.XT
        KT = self.WBF
        TIu = TI[:].bitcast(U32)
        BJu = BJ[:].bitcast(U32)
        self.bload("sp", G, D, self.ln_g, self.ln_g[lni])
        self.bload("sp", Bb, D, self.ln_b, self.ln_b[lni])
        self.DMA("sp", v3(KT32[:], 16), self.keysT[l], self.keysT, KT32)
        self.CP("dve", KT[:, 0:2048], KT32[:], [KT32], [KT])
        Ut, Vt = self.ub_d[l].t, self.vb_d[l].t
        for ug in UG:
            self.MEMSET("dve", ug[:], 0.0, [ug])

        def stageA(i):
            par = i % 2
            H, IDX, GATE = Hs[par], IDXs[par], GATEs[par]
            IDXi = IDX[:].bitcast(I32)
            rows = slice(i * 128, (i + 1) * 128)
            self.DMA("sp", H[:], h_d[rows, :], h_d, H)
            self.DMA("sp", Q[:], q_d[rows, :], q_d, Q)
            self.DMA("sp", ROFF[:], self.rowoff[rows, :], self.rowoff, ROFF)
            for kb in range(4):
                pb = PS[kb % 2]
                for kk in range(4):
                    kc = kb * 4 + kk
                    self.TR(pb[:, kk * 128:(kk + 1) * 128], Q[:, kc * 128:(kc + 1) * 128], [Q], [pb])
                self.CP("act", XT[:, kb * 512:(kb + 1) * 512], pb[:], [pb], [XT])
            for kb in range(4):
                pb = PS[2 + kb % 2]
                for kk in range(4):
                    j = kb * 4 + kk
                    self.MM(pb[:, kk * 128:(kk + 1) * 128], XT[:, j * 128:(j + 1) * 128], KT[:, j * 128:(j + 1) * 128],
                            True, True, [XT, KT], [pb])
                self.CP("act", SC[:, kb * 512:(kb + 1) * 512], pb[:], [pb], [SC])
            yield
            for j in range(16):
                sg_ = SC[:, j * 128:(j + 1) * 128]
                a8, b8 = TSv[:, j * 16:j * 16 + 8], TSv[:, j * 16 + 8:j * 16 + 16]
                ia, ib = TIu[:, j * 16:j * 16 + 8], TIu[:, j * 16 + 8:j * 16 + 16]
                P.op("dve", lambda e, a8=a8, sg_=sg_: e.max(out=a8, in_=sg_), [SC], [TSv])
                P.op("dve", lambda e, a8=a8, sg_=sg_: e.match_replace(out=TMP[:], in_to_replace=a8, in_values=sg_,
                                                                    imm_value=-1e30), [SC, TSv], [TMP])
                P.op("dve", lambda e, b8=b8: e.max(out=b8, in_=TMP[:]), [TMP], [TSv])
                P.op("dve", lambda e, ia=ia, a8=a8, sg_=sg_: e.max_index(out=ia, in_max=a8, in_values=sg_), [SC, TSv], [TI])
                P.op("dve", lambda e, ib=ib, b8=b8: e.max_index(out=ib, in_max=b8, in_values=TMP[:]), [TMP, TSv], [TI])
                yield
            self.CP("dve", TIf[:], TIu, [TI], [TIf])
            s4 = TSv[:].rearrange("p (h c k) -> p h c k", h=8, c=2)
            i4 = TIf[:].rearrange("p (h c k) -> p h c k", h=8, c=2)
            c4 = cand[:].rearrange("p (h a b) -> p h a b", h=8, a=16)
            ci4 = candI[:].rearrange("p (h a b) -> p h a b", h=8, a=16)
            shp = [128, 8, 16, 16]
            self.TT("dve", c4, bc(s4[:, :, 0, :], shp, 3), bc(s4[:, :, 1, :], shp, 2), ALU.add, [TSv], [cand])
            yield
            for h in range(8):
                cg = cand[:, h * 256:(h + 1) * 256]
                a8, b8 = BS[:, h * 16:h * 16 + 8], BS[:, h * 16 + 8:h * 16 + 16]
                ia, ib = BJu[:, h * 16:h * 16 + 8], BJu[:, h * 16 + 8:h * 16 + 16]
                P.op("dve", lambda e, a8=a8, cg=cg: e.max(out=a8, in_=cg), [cand], [BS])
                P.op("dve", lambda e, a8=a8, cg=cg: e.match_replace(out=TMP2[:], in_to_replace=a8, in_values=cg,
                                                                  imm_value=-1e30), [cand, BS], [TMP2])
                P.op("dve", lambda e, b8=b8: e.max(out=b8, in_=TMP2[:]), [TMP2], [BS])
                P.op("dve", lambda e, ia=ia, a8=a8, cg=cg: e.max_index(out=ia, in_max=a8, in_values=cg), [cand, BS], [BJ])
                P.op("dve", lambda e, ib=ib, b8=b8: e.max_index(out=ib, in_max=b8, in_values=TMP2[:]), [TMP2, BS], [BJ])
                self.CP("dve", BJf[:, h * 16:(h + 1) * 16], BJu[:, h * 16:(h + 1) * 16], [BJ], [BJf])
                yield
            K1f, K2f, I1, K1i = candI[:, 0:128], candI[:, 128:256], candI[:, 256:384], candI[:, 384:512].bitcast(I32)
            self.TS("dve", K1f, BJf[:], 0.0625, -0.46875, ALU.mult, ALU.add, [BJf], [candI])
            self.CP("dve", K1i, K1f, [candI], [candI])
            self.CP("dve", K1f, K1i, [candI], [candI])
            self.STT("dve", K2f, K1f, -16.0, BJf[:], ALU.mult, ALU.add, [candI, BJf], [candI])
            yield
            oh4 = OH[:, 0:2048].rearrange("p (h r k) -> p h r k", h=8, r=16)
            s4d = [128, 8, 16, 16]
            io4 = self.iot[:, 0:16].unsqueeze(1).unsqueeze(1).to_broadcast(s4d)
            for (kf, c, dst) in ((K1f, 0, I1), (K2f, 1, IDXf[:])):
                self.TT("dve", oh4, io4, bc(v3(kf, 8), s4d, 3), ALU.is_equal, [self.iot, candI], [OH])
                yield
                self.TT("dve", oh4, oh4, bc(i4[:, :, c, :], s4d, 2), ALU.mult, [OH, TIf], [OH])
                yield
                self.RED(v3(dst, 8), oh4, ALU.add, [OH], [candI, IDXf])
                yield
            self.STT("dve", IDXf[:], I1, 128.0, IDXf[:], ALU.mult, ALU.add, [candI, IDXf], [IDXf])
            b3 = v3(BS[:], 8)
            self.TT("dve", v3(GATE[:], 8), b3, bc(b3[:, :, 0], [128, 8, 16], 2), ALU.subtract, [BS], [GATE])
            self.ACT(GATE[:], GATE[:], AF.Exp, [GATE], [GATE])
            self.RED(stA[:, 0:8], v3(GATE[:], 8), ALU.add, [GATE], [stA])
            P.op("dve", lambda e: e.reciprocal(out=stA[:, 0:8], in_=stA[:, 0:8]), [stA], [stA])
            self.TT("dve", v3(GATE[:], 8), v3(GATE[:], 8), bc(stA[:, 0:8], [128, 8, 16], 2), ALU.mult, [GATE, stA], [GATE])
            self.CP("dve", IDXi, IDXf[:], [IDXf], [IDX])
            yield

        def drain(g, n=None):
            if g is None:
                return None
            k = 0
            while n is None or k < n:
                try:
                    next(g)
                except StopIteration:
                    return None
                k += 1
            return g

        drain(stageA(0))
        for i in range(self.NT):
            par = i % 2
            H, IDX, GATE = Hs[par], IDXs[par], GATEs[par]
            IDXi = IDX[:].bitcast(I32)
            rows = slice(i * 128, (i + 1) * 128)
            gA = stageA(i + 1) if i + 1 < self.NT else None
            for j in range(128):
                ug = UG[j % NG]
                P.dma("pool", lambda e, ug=ug, j=j, IDXi=IDXi: e.indirect_dma_start(
                    out=ug[:], out_offset=None, in_=Ut,
                    in_offset=bass.IndirectOffsetOnAxis(ap=IDXi[:, j:j + 1], axis=0)), self.ub_d[l], ug, [IDX])
                self.STT("dve", junk[:], ug[:], 1.0, H[:], ALU.mult, ALU.mult, [ug, H], [junk, ACTV],
                         accum=ACTV[:, j:j + 1])
            self.ACT(ACTV[:], ACTV[:], AF.Gelu, [ACTV], [ACTV])
            self.TT("dve", ACTV[:], ACTV[:], GATE[:], ALU.mult, [ACTV, GATE], [ACTV])
            for j in range(128):
                ug = UG[j % NG]
                dg = DG[j % 2]
                P.dma("pool", lambda e, ug=ug, j=j, IDXi=IDXi: e.indirect_dma_start(
                    out=ug[:], out_offset=None, in_=Vt,
                    in_offset=bass.IndirectOffsetOnAxis(ap=IDXi[:, j:j + 1], axis=0)), self.vb_d[l], ug, [IDX])
                self.TS("dve", dg[:], self.ident[:], ACTV[:, j:j + 1], None, ALU.mult, None, [self.ident, ACTV], [dg])
                for b4 in range(4):
                    self.MM(PS[4 + b4][:], dg[:], ug[:, 512 * b4:512 * (b4 + 1)], j == 0, j == 127, [dg, ug], [PS[4 + b4]])
                if j % 2 == 1:
                    gA = drain(gA, 1)
            gA = drain(gA)
            for b4 in range(4):
                self.CP("act", ACC[:, 512 * b4:512 * (b4 + 1)], PS[4 + b4][:], [PS[4 + b4]], [ACC])
            self.ln_tile(H, ACC, G, Bb, LO, junk, stB)
            self.DMA("sp", out_d[rows, :], LO[:], LO, out_d)

    def build(self):
        P = self.P
        self.declare()
        self.consts()
        self.chunk_consts()
        self.convert_tables()
        self.project(self.X, self.even_w_in[:], self.even_w_in, EVEN_IN, self.proj_d)
        self.even_pass()
        self.project(self.mix_d, self.even_w_out[:], self.even_w_out, D, self.ym_d)
        self.ln_phase(self.X, self.ym_d, 0, self.h_d)
        self.project(self.h_d, self.peer_wq[0], self.peer_wq, D, self.q_d)
        self.peer_pass(0, self.h_d, self.q_d, 1, self.x1_d)
        self.project(self.x1_d, self.odd_w_in[:], self.odd_w_in, ODD_IN, self.proj_d)
        self.odd_pads()
        self.ssd_pass()
        self.rwkv_pass()
        self.project(self.mix_d, self.odd_w_out[:], self.odd_w_out, D, self.ym_d)
        self.ln_phase(self.x1_d, self.ym_d, 2, self.h_d)
        self.project(self.h_d, self.peer_wq[1], self.peer_wq, D, self.q_d)
        self.peer_pass(1, self.h_d, self.q_d, 3, self.Y)
        P.barrier()
        P.emit()
        return self.nc


def host_inputs(NF, c, inp, nprompt_b, pos_prompt0=0, past_len=16384):
    f = lambda a: np.ascontiguousarray(a, dtype=np.float32)
    NP = 128 * NF + 16
    TOK = 128 * (NF + 1)
    b = c % nprompt_b
    sl = slice(c * NSEQ, (c + 1) * NSEQ)
    X = np.zeros((TOK, D), np.float32)
    X[0:16] = inp["meta_tokens"]
    X[16:NP] = inp["x_prompt"][b]
    X[NP:NP + 64] = inp["x_sample"][sl].transpose(1, 0, 2).reshape(64, D)
    hg = inp["state_hgrn"][0, sl]
    rt = inp["state_ret"][0, sl]
    st_even = np.concatenate([hg.transpose(0, 2, 1, 3).reshape(NSEQ, 128, 1024),
                              rt.transpose(0, 2, 1, 3).reshape(NSEQ, 128, 1024)], axis=2)
    sm = inp["state_ssm"][0, sl]
    wk = inp["state_wkv"][0, sl]
    wk2 = wk.reshape(NSEQ, 8, 2, 64, 64).transpose(0, 2, 4, 1, 3).reshape(NSEQ, 128, 512)
    st_odd = np.concatenate([sm.transpose(0, 2, 1, 3).reshape(NSEQ, 128, 1024), wk2], axis=2)
    st_conv = inp["state_conv"][0, sl].transpose(1, 0, 2)
    st_shift = inp["state_shift"][0, sl]
    pos = np.zeros(TOK, np.float64)
    pos[0:NP] = pos_prompt0 + np.arange(NP)
    pos[NP:NP + 64] = past_len + np.repeat(np.arange(4), NSEQ)
    inv = (10000.0 ** (-np.arange(64, dtype=np.float32) / 64)).astype(np.float32)
    ang = pos.astype(np.float32)[:, None] * inv[None, :]
    cs, sn = np.cos(ang).astype(np.float32), np.sin(ang).astype(np.float32)
    sc = np.float32(128 ** -0.5)
    rot = np.stack([cs, sn, cs * sc, sn * sc])
    rowoff = np.full((TOK, 1), 1.0e6, np.float32)
    if c < nprompt_b:
        rowoff[0:NP] = 0.0
    rowoff[NP:NP + 64] = 0.0
    m = {
        "rowoff": rowoff, "X": X, "st_even": st_even, "st_odd": st_odd, "st_conv": st_conv, "st_shift": st_shift, "rot": rot,
        "ln_g": inp["ln_g"].reshape(4, D), "ln_b": inp["ln_b"].reshape(4, D),
        "even_w_in": inp["even_w_in"][0], "lb_logits": inp["hgrn_lb_logits"], "hgrn_norm_g": inp["hgrn_norm_g"][0],
        "even_w_out": inp["even_w_out"][0], "odd_w_in": inp["odd_w_in"][0], "conv_w": inp["conv_w"][0],
        "conv_b": inp["conv_b"][0], "dt_bias": inp["dt_bias"][0], "a_log": inp["a_log"][0], "d_skip": inp["d_skip"][0],
        "ssm_norm_g": inp["ssm_norm_g"][0], "shift_mu": inp["shift_mu"][0],
        "rwkv_vec": np.stack([inp["rwkv_w0"][0], inp["rwkv_a0"][0], inp["rwkv_k_k"][0], inp["rwkv_k_a"][0],
                              inp["rwkv_r_k"][0].reshape(1024), inp["lnx_g"][0], inp["lnx_b"][0]]),
        "rwkv_w2": inp["rwkv_w2"][0], "rwkv_a2": inp["rwkv_a2"][0], "rwkv_g2": inp["rwkv_g2"][0],
        "odd_w_out": inp["odd_w_out"][0], "peer_wq": inp["peer_w_query"],
        "keysT": inp["peer_sub_keys"].reshape(2, 16, 128, 128).transpose(0, 3, 1, 2),
        "peer_u0": inp["peer_u"][0], "peer_u1": inp["peer_u"][1], "peer_v0": inp["peer_v"][0],
        "peer_v1": inp["peer_v"][1], "zeros_d": np.zeros((4, 3360), np.float32),
    }
    return {k: f(v) for k, v in m.items()}


_NC_CACHE = {}


def kernel(**inputs):
    inp = {k: np.asarray(v) for k, v in inputs.items()}
    SEQ = inp["x_prompt"].shape[1]
    NF = (SEQ + 16 - 16) // 128
    assert 128 * NF == SEQ
    NP = SEQ + 16
    if NF not in _NC_CACHE:
        _NC_CACHE[NF] = K(NF).build()
    nc = _NC_CACHE[NF]
    B = inp["x_prompt"].shape[0]
    in_maps = [host_inputs(NF, c, inp, B) for c in range(8)]
    res = run_bass_kernel_spmd(nc, in_maps, core_ids=list(range(8))).results
    y_p = np.stack([res[b]["Y"][16:NP] for b in range(B)])
    y_s = np.concatenate([res[c]["Y"][NP:NP + 64].reshape(4, NSEQ, D).transpose(1, 0, 2) for c in range(8)])

    def even_split(a):
        n = a.shape[0]
        return (a[:, :, 0:1024].reshape(n, 128, 8, 128).transpose(0, 2, 1, 3),
                a[:, :, 1024:2048].reshape(n, 128, 4, 256).transpose(0, 2, 1, 3))

    def odd_split(a):
        n = a.shape[0]
        ssm = a[:, :, 0:1024].reshape(n, 128, 16, 64).transpose(0, 2, 1, 3)
        wkv = a[:, :, 1024:1536].reshape(n, 2, 64, 8, 64).transpose(0, 3, 1, 4, 2).reshape(n, 16, 64, 64)
        return ssm, wkv

    ep = np.stack([res[b]["So_even"][0] for b in range(B)])
    es = np.concatenate([res[c]["So_even"][1:] for c in range(8)])
    op_ = np.stack([res[b]["So_odd"][0] for b in range(B)])
    os_ = np.concatenate([res[c]["So_odd"][1:] for c in range(8)])
    hg_p, rt_p = even_split(ep)
    hg_s, rt_s = even_split(es)
    sm_p, wk_p = odd_split(op_)
    sm_s, wk_s = odd_split(os_)
    cv_p = np.stack([res[b]["conv_o"][:, 0, :] for b in range(B)])
    cv_s = np.concatenate([res[c]["conv_o"][:, 1:, :].transpose(1, 0, 2) for c in range(8)])
    sh_p = np.stack([res[b]["shift_o"][0] for b in range(B)])
    sh_s = np.concatenate([res[c]["shift_o"][1:] for c in range(8)])
    A = lambda a: np.ascontiguousarray(a, dtype=np.float32)
    return (A(y_p), A(y_s), A(hg_p)[None], A(hg_s)[None], A(rt_p)[None], A(rt_s)[None], A(sm_p)[None], A(sm_s)[None],
            A(cv_p)[None], A(cv_s)[None], A(wk_p)[None], A(wk_s)[None], A(sh_p)[None], A(sh_s)[None])
```

```python
import contextlib
import os
import math
import numpy as np
import concourse.bass as bass
import concourse.mybir as mybir
from concourse.bass_utils import run_bass_kernel_spmd

F32 = mybir.dt.float32
BF16 = mybir.dt.bfloat16
I32 = mybir.dt.int32
U32 = mybir.dt.uint32
AF = mybir.ActivationFunctionType
ALU = mybir.AluOpType
AX = mybir.AxisListType
ENGS = ["pe", "act", "dve", "pool", "sp"]

D = 2048
EVEN_IN = 7168
ODD_IN = 5936
ALPHA = 4.0 ** 0.25
NSEQ = 16
GS = int(os.environ.get("KGS", "5"))


class Buf:
    def __init__(self, name, t):
        self.name = name
        self.t = t
        self.last_write = None
        self.reads = []
        self.dsem = None
        self.dcnt = 0

    def __getitem__(self, k):
        return self.t[k]


class Prog:
    def __init__(self, nc):
        self.nc = nc
        self.stack = contextlib.ExitStack()
        self.ops = {e: [] for e in ENGS}
        self.cnt = {e: 0 for e in ENGS}
        self.esem = {}
        self.waited = {e: {} for e in ENGS}
        self.sem_free = {"sw": [], "hw": []}
        self.sem_val = {}
        self.sem_obj = {}
        self.nsem = 0
        for e in ENGS:
            if e != "sp":
                self.esem[e] = self._sem("e_" + e)

    def _sem(self, name):
        self.nsem += 1
        return self.stack.enter_context(self.nc.semaphore(name))

    def sb(self, name, shape, dt=F32):
        return Buf(name, self.stack.enter_context(self.nc.sbuf_tensor(name, list(shape), dt)))

    def ps(self, name, shape, dt=F32):
        return Buf(name, self.stack.enter_context(self.nc.psum_tensor(name, list(shape), dt)))

    def dram(self, name, shape, dt=F32, kind="Internal"):
        return Buf(name, self.nc.dram_tensor(name, list(shape), dt, kind=kind).ap())

    def _collect(self, eng, reads, writes):
        waits = []
        for b in reads:
            if b.last_write is not None:
                waits.append(b.last_write)
        for b in writes:
            if b.last_write is not None:
                waits.append(b.last_write)
            waits.extend(b.reads)
        w = self.waited[eng]
        best = {}
        for (s, v, src) in waits:
            if src == eng and eng == "pe":
                continue
            if w.get(id(s), (None, 0))[1] >= v:
                continue
            if id(s) not in best or best[id(s)][1] < v:
                best[id(s)] = (s, v)
        out = []
        for k, (s, v) in best.items():
            w[k] = (s, v)
            out.append((s, v))
        return out

    def op(self, eng, fn, reads=(), writes=()):
        waits = self._collect(eng, reads, writes)
        self.cnt[eng] += 1
        tok = (self.esem[eng], self.cnt[eng], eng)
        self.ops[eng].append((waits, fn, (self.esem[eng], 1)))
        for b in reads:
            b.reads.append(tok)
        for b in writes:
            b.last_write = tok
            b.reads = []

    def dma(self, q, fn, src, dst, extra_reads=()):
        qt = "sw" if q == "pool" else "hw"
        if dst.dsem is None:
            dst.dsem = {}
        if qt not in dst.dsem:
            if self.sem_free[qt]:
                sem = self.sem_free[qt].pop()
            else:
                sem = self._sem("d%d" % self.nsem)
                self.sem_val[id(sem)] = 0
                self.sem_obj[id(sem)] = sem
            dst.dsem[qt] = sem
        sem = dst.dsem[qt]
        rd = ([src] if src is not None else []) + list(extra_reads)
        waits = self._collect(q, rd, [dst])
        self.sem_val[id(sem)] += 16
        tok = (sem, self.sem_val[id(sem)], "dma")
        self.ops[q].append((waits, fn, (sem, 16)))
        for b in rd:
            b.reads.append(tok)
        dst.last_write = tok
        dst.reads = []

    def barrier(self):
        allw = [(self.esem[e], self.cnt[e]) for e in self.esem if self.cnt[e] > 0]
        allw += [(self.sem_obj[k], v) for k, v in self.sem_val.items() if v > 0]
        for e in ENGS:
            w = self.waited[e]
            ws = []
            for (s, v) in allw:
                if w.get(id(s), (None, 0))[1] >= v:
                    continue
                w[id(s)] = (s, v)
                ws.append((s, v))
            self.ops[e].append((ws, None, None))

    def emit(self):
        nc = self.nc
        with nc.allow_low_precision("bf16 matmul operands"), nc.Block() as block:
            def run(eng, name):
                for (waits, fn, inc) in self.ops[name]:
                    for (s, v) in waits:
                        eng.wait_ge(s, v)
                    if fn is not None:
                        fn(eng).then_inc(inc[0], inc[1])

            @block.tensor
            def _(e):
                run(e, "pe")

            @block.scalar
            def _(e):
                run(e, "act")

            @block.vector
            def _(e):
                run(e, "dve")

            @block.gpsimd
            def _(e):
                run(e, "pool")

            @block.sync
            def _(e):
                run(e, "sp")
        self.stack.close()

    def retire(self, bufs):
        for b in bufs:
            if b.dsem is not None:
                for qt, sem in b.dsem.items():
                    self.sem_free[qt].append(sem)
                b.dsem = None


class Arena:
    def __init__(self, buf, size, prog):
        self.buf = buf
        self.size = size
        self.off = 0
        self.prog = prog
        self.live = []

    def reset(self):
        self.off = 0
        self.prog.retire(self.live)
        self.live = []

    def alias(self, name, ap):
        b = Buf(name, ap)
        self.live.append(b)
        return b

    def alloc(self, name, n):
        assert self.off + n <= self.size, (name, self.off, n, self.size)
        b = Buf(name, self.buf.t[:, self.off:self.off + n])
        self.live.append(b)
        self.off += n
        return b


def v3(ap, a):
    return ap.rearrange("p (a b) -> p a b", a=a)


def bc(ap, shape, axis):
    return ap.unsqueeze(axis).to_broadcast(list(shape))


class K:
    def __init__(self, NF):
        self.NF = NF
        self.NT = NF + 1
        self.TOK = 128 * (NF + 1)
        self.NP = 128 * NF + 16
        nc = bass.Bass("TRN2", target_bir_lowering=False)
        self.nc = nc
        self.P = Prog(nc)

    def TT(self, eng, out, a, b, op, R, W):
        self.P.op(eng, lambda e: e.tensor_tensor(out=out, in0=a, in1=b, op=op), R, W)

    def TS(self, eng, out, a, s1, s2, op0, op1, R, W):
        if s2 is None:
            self.P.op(eng, lambda e: e.tensor_scalar(out=out, in0=a, scalar1=s1, scalar2=None, op0=op0), R, W)
        else:
            self.P.op(eng, lambda e: e.tensor_scalar(out=out, in0=a, scalar1=s1, scalar2=s2, op0=op0, op1=op1), R, W)

    def STT(self, eng, out, a, s, b, op0, op1, R, W, accum=None):
        if accum is None:
            self.P.op(eng, lambda e: e.scalar_tensor_tensor(out=out, in0=a, scalar=s, in1=b, op0=op0, op1=op1), R, W)
        else:
            self.P.op(eng, lambda e: e.scalar_tensor_tensor(out=out, in0=a, scalar=s, in1=b, op0=op0, op1=op1,
                                                            accum_out=accum), R, W)

    def ACT(self, out, a, func, R, W, scale=None, bias=None, accum=None):
        kw = {}
        if scale is not None:
            kw["scale"] = scale
        if bias is not None:
            kw["bias"] = bias
        if accum is not None:
            kw["accum_out"] = accum
        self.P.op("act", lambda e: e.activation(out=out, in_=a, func=func, **kw), R, W)

    def CP(self, eng, out, a, R, W):
        if eng == "act":
            self.P.op("act", lambda e: e.copy(out=out, in_=a), R, W)
        else:
            self.P.op(eng, lambda e: e.tensor_copy(out=out, in_=a), R, W)

    def RED(self, out, a, op, R, W):
        self.P.op("dve", lambda e: e.tensor_reduce(out=out, in_=a, axis=AX.X, op=op), R, W)

    def MM(self, out, lhsT, rhs, start, stop, R, W):
        self.P.op("pe", lambda e: e.matmul(out, lhsT=lhsT, rhs=rhs, start=start, stop=stop), R, W)

    def TR(self, out, a, R, W):
        ident = self.ident
        n = a.shape[0]
        self.P.op("pe", lambda e: e.transpose(out=out, in_=a, identity=ident[0:n, 0:n]), list(R) + [ident], W)

    def DMA(self, q, out, in_, src, dst, extra=()):
        self.P.dma(q, lambda e: e.dma_start(out=out, in_=in_), src, dst, extra)

    def MEMSET(self, eng, ap, val, W):
        self.P.op(eng, lambda e: e.memset(ap, val), [], W)

    def rsqrt(self, out, a, mul, eps, R, W):
        self.TS("dve", out, a, mul, eps, ALU.mult, ALU.add, R, W)
        self.ACT(out, out, AF.Sqrt, W, W)
        self.P.op("dve", lambda e: e.reciprocal(out=out, in_=out), W, W)

    def bload(self, q, dst, n, src_buf, src_ap):
        if isinstance(src_ap, Buf):
            src_ap = src_ap.t
        self.DMA(q, dst[:, 0:n], src_ap.partition_broadcast(128), src_buf, dst)

    def declare(self):
        P, TOK = self.P, self.TOK
        di = lambda n, s, dt=F32: P.dram(n, s, dt, kind="ExternalInput")
        do = lambda n, s, dt=F32: P.dram(n, s, dt, kind="ExternalOutput")
        self.X = di("X", [TOK, D])
        self.st_even = di("st_even", [NSEQ, 128, 2048])
        self.st_odd = di("st_odd", [NSEQ, 128, 1536])
        self.st_conv = di("st_conv", [3, NSEQ, 1536])
        self.st_shift = di("st_shift", [NSEQ, 3360])
        self.rot = di("rot", [4, TOK, 64])
        self.ln_g = di("ln_g", [4, D])
        self.ln_b = di("ln_b", [4, D])
        self.even_w_in = di("even_w_in", [D, EVEN_IN])
        self.lb_logits = di("lb_logits", [3, 1024])
        self.hgrn_norm_g = di("hgrn_norm_g", [128])
        self.even_w_out = di("even_w_out", [D, D])
        self.odd_w_in = di("odd_w_in", [D, ODD_IN])
        self.conv_w = di("conv_w", [4, 1536])
        self.conv_b = di("conv_b", [1536])
        self.dt_bias = di("dt_bias", [16])
        self.a_log = di("a_log", [16])
        self.d_skip = di("d_skip", [16])
        self.ssm_norm_g = di("ssm_norm_g", [1024])
        self.shift_mu = di("shift_mu", [3360])
        self.rwkv_vec = di("rwkv_vec", [7, 1024])
        self.rwkv_w2 = di("rwkv_w2", [64, 1024])
        self.rwkv_a2 = di("rwkv_a2", [64, 1024])
        self.rwkv_g2 = di("rwkv_g2", [160, 1024])
        self.odd_w_out = di("odd_w_out", [D, D])
        self.peer_wq = di("peer_wq", [2, D, D])
        self.keysT = di("keysT", [2, 128, 16, 128])
        self.peer_u = [di("peer_u%d" % l, [16384, D]) for l in range(2)]
        self.peer_v = [di("peer_v%d" % l, [16384, D]) for l in range(2)]
        self.zeros_d = di("zeros_d", [4, 3360])
        self.rowoff = di("rowoff", [TOK, 1])
        self.ub_d = [P.dram("ub_d%d" % l, [16384, D], BF16) for l in range(2)]
        self.vb_d = [P.dram("vb_d%d" % l, [16384, D], BF16) for l in range(2)]
        self.Y = do("Y", [TOK, D])
        self.So_even = do("So_even", [NSEQ + 1, 128, 2048])
        self.So_odd = do("So_odd", [NSEQ + 1, 128, 1536])
        self.conv_o = do("conv_o", [3, NSEQ + 1, 1536])
        self.shift_o = do("shift_o", [NSEQ + 1, 3360])
        self.outs = [self.Y, self.So_even, self.So_odd, self.conv_o, self.shift_o]
        self.proj_d = P.dram("proj_d", [TOK, EVEN_IN])
        self.mix_d = P.dram("mix_d", [TOK, D])
        self.ym_d = P.dram("ym_d", [TOK, D])
        self.h_d = P.dram("h_d", [TOK, D])
        self.q_d = P.dram("q_d", [TOK, D])
        self.x1_d = P.dram("x1_d", [TOK, D])
        self.vrow_d = P.dram("vrow_d", [TOK, 2080])
        self.xbcp_d = P.dram("xbcp_d", [self.NP + 3, 1536])
        self.xbcs_d = P.dram("xbcs_d", [7, NSEQ, 1536])
        self.rwp_d = P.dram("rwp_d", [self.NP + 1, 3360])
        self.rws_d = P.dram("rws_d", [5, NSEQ, 3360])
        self.AR = P.sb("AR", [128, 38000])
        self.ar = Arena(self.AR, 38000, P)
        self.XT = P.sb("XT", [128, 16 * 128 * GS], BF16)
        self.WBF = P.sb("WBF", [128, 8192], BF16)
        self.TQ = P.sb("TQ", [128, 2048], BF16)
        self.TQ32 = P.sb("TQ32", [128, 2048])
        self.ident = P.sb("ident", [128, 128])
        self.blk1 = P.sb("blk1", [128, 128])
        self.Z = P.sb("Z", [128, 256], BF16)
        self.Z0 = P.sb("Z0", [128, 256], BF16)
        self.Z1 = P.sb("Z1", [128, 256], BF16)
        self.iot = P.sb("iot", [128, 256])
        self.PS = [P.ps("ps%d" % i, [128, 512]) for i in range(8)]

    def consts(self):
        P = self.P
        iot, ident = self.iot, self.ident
        P.op("pool", lambda e: e.iota(iot[:, 0:128], pattern=[[1, 128]], base=0, channel_multiplier=-1,
                                      allow_small_or_imprecise_dtypes=True), [], [iot])
        self.TS("dve", ident[:], iot[:, 0:128], 0.0, None, ALU.is_equal, None, [iot], [ident])
        self.MEMSET("dve", self.blk1[:], 0.0, [self.blk1])
        self.MEMSET("dve", self.blk1[0:64, 0:64], 1.0, [self.blk1])
        self.MEMSET("dve", self.blk1[64:128, 64:128], 1.0, [self.blk1])
        for z in (self.Z, self.Z0, self.Z1):
            self.MEMSET("dve", z[:], 0.0, [z])
        self.MEMSET("dve", self.Z[:, 127:128], 1.0, [self.Z])
        self.MEMSET("dve", self.Z0[0:64, 127:128], 1.0, [self.Z0])
        self.MEMSET("dve", self.Z1[64:128, 127:128], 1.0, [self.Z1])
        P.op("pool", lambda e: e.iota(iot[:], pattern=[[1, 256]], base=0, channel_multiplier=0,
                                      allow_small_or_imprecise_dtypes=True), [ident], [iot])

    def project(self, src, W_ap, W_buf, ncols, dst, dst_col0=0):
        P, ar = self.P, self.ar
        P.barrier()
        ar.reset()
        wst = [ar.alloc("wst%d" % i, 8192) for i in range(2)]
        xt = [ar.alloc("xt%d" % i, 2048) for i in range(2)]
        ev = [ar.alloc("ev%d" % i, 512) for i in range(2)]
        XT, WBF, PS = self.XT, self.WBF, self.PS
        nblk = (ncols + 511) // 512
        cnt = 0
        wc = 0
        ngrp = (self.NT + GS - 1) // GS
        bounds = [(self.NT * g) // ngrp for g in range(ngrp + 1)]
        for gi in range(ngrp):
            tiles = list(range(bounds[gi], bounds[gi + 1]))
            for ti, i in enumerate(tiles):
                xb = xt[i % 2]
                self.DMA("sp", xb[:], src[i * 128:(i + 1) * 128, :], src, xb)
                for kb in range(4):
                    pb = PS[kb % 2]
                    for kk in range(4):
                        kc = kb * 4 + kk
                        self.TR(pb[:, kk * 128:(kk + 1) * 128], xb[:, kc * 128:(kc + 1) * 128], [xb], [pb])
                    o = v3(XT[:], 16)[:, kb * 4:(kb + 1) * 4, ti * 128:(ti + 1) * 128]
                    self.CP("act", o, v3(pb[:], 4), [pb], [XT])
            for cb in range(nblk):
                c0 = cb * 512
                w = min(512, ncols - c0)
                ws = wst[wc % 2]
                wc += 1
                self.DMA("pool", v3(ws[:], 16)[:, :, 0:w],
                         W_ap[:, c0:c0 + w].rearrange("(kc p) c -> p kc c", p=128), W_buf, ws)
                self.CP("dve", v3(WBF[:], 16)[:, :, 0:w], v3(ws[:], 16)[:, :, 0:w], [ws], [WBF])
                for ti, i in enumerate(tiles):
                    pb = PS[2 + cnt % 2]
                    eb = ev[cnt % 2]
                    cnt += 1
                    for kc in range(16):
                        self.MM(pb[:, 0:w], v3(XT[:], 16)[:, kc, ti * 128:(ti + 1) * 128], v3(WBF[:], 16)[:, kc, 0:w],
                                kc == 0, kc == 15, [XT, WBF], [pb])
                    self.CP("act", eb[:, 0:w], pb[:, 0:w], [pb], [eb])
                    self.DMA("sp", dst[i * 128:(i + 1) * 128, dst_col0 + c0:dst_col0 + c0 + w], eb[:, 0:w], eb, dst)

    def ln_tile(self, A, Bt, G, Bb, OUT, tmp, st):
        self.STT("dve", A[:], A[:], ALPHA, Bt[:], ALU.mult, ALU.add, [A, Bt], [A])
        self.RED(st[:, 0:1], A[:], ALU.add, [A], [st])
        self.TS("dve", st[:, 0:1], st[:, 0:1], -1.0 / D, None, ALU.mult, None, [st], [st])
        self.TS("dve", A[:], A[:], st[:, 0:1], None, ALU.add, None, [A, st], [A])
        self.ACT(tmp[:], A[:], AF.Square, [A], [tmp, st], accum=st[:, 1:2])
        self.rsqrt(st[:, 2:3], st[:, 1:2], 1.0 / D, 1e-5, [st], [st])
        self.TS("dve", A[:], A[:], st[:, 2:3], None, ALU.mult, None, [A, st], [A])
        self.TT("dve", A[:], A[:], G[:], ALU.mult, [A, G], [A])
        self.TT("dve", OUT[:], A[:], Bb[:], ALU.add, [A, Bb], [OUT])

    def ln_phase(self, a_d, b_d, lni, out_d):
        P, ar = self.P, self.ar
        P.barrier()
        ar.reset()
        G = ar.alloc("lnG", 2048)
        Bb = ar.alloc("lnB", 2048)
        A = [ar.alloc("lnA%d" % i, 2048) for i in range(2)]
        Bt = [ar.alloc("lnBt%d" % i, 2048) for i in range(2)]
        O = [ar.alloc("lnO%d" % i, 2048) for i in range(2)]
        tmp = ar.alloc("lntmp", 2048)
        st = ar.alloc("lnst", 4)
        self.bload("sp", G, D, self.ln_g, self.ln_g[lni])
        self.bload("sp", Bb, D, self.ln_b, self.ln_b[lni])
        for i in range(self.NT):
            a, b, o = A[i % 2], Bt[i % 2], O[i % 2]
            rows = slice(i * 128, (i + 1) * 128)
            self.DMA("sp", a[:], a_d[rows, :], a_d, a)
            self.DMA("sp", b[:], b_d[rows, :], b_d, b)
            self.ln_tile(a, b, G, Bb, o, tmp, st)
            self.DMA("sp", out_d[rows, :], o[:], o, out_d)

    def scan_step(self, m, first, last, S, ns, dv, Dap, Kap, Qap, Vb, Vap, tmpKV, obanks, extra=None):
        n = ns * dv
        TQ = self.TQ
        S3 = v3(S[:, 0:n], ns)
        shp = [128, ns, dv]
        self.TT("pool", v3(tmpKV[:, 0:n], ns), v3(Vap, ns), bc(Kap[0], shp, 2), ALU.mult, [Vb, Kap[1]], [tmpKV])
        if extra is not None:
            extra[0]()
        self.TT("dve", S3, S3, bc(Dap[0], shp, 2), ALU.mult, [S, Dap[1]], [S])
        if extra is not None:
            extra[1]()
        self.TT("dve", S3, S3, v3(tmpKV[:, 0:n], ns), ALU.add, [S, tmpKV], [S])
        TQ32 = self.TQ32
        self.TT("dve", v3(TQ32[:, 0:n], ns), S3, bc(Qap[0], shp, 2), ALU.mult, [S, Qap[1]], [TQ32])
        self.CP("act", TQ[:, 0:n], TQ32[:, 0:n], [TQ32], [TQ])
        for (pb, c0, w, z) in obanks:
            self.MM(pb[:, 0:w], z[:, 127 - m:255 - m], TQ[:, c0:c0 + w], first, last, [TQ, z], [pb])

    def tile_tokens(self, i):
        if i < self.NF:
            return [(m, i * 128 + m, 0) for m in range(128)]
        toks = [(m, i * 128 + m, 0) for m in range(16)]
        for s in range(NSEQ):
            for t in range(4):
                m = 16 + 16 * t + s
                toks.append((m, i * 128 + m, 1 + s))
        return toks

    def run_scan(self, i, Sbufs, ncol, st_in, st_col0, So, step_fn, zero_fn):
        toks = self.tile_tokens(i)
        cur = getattr(self, "_cur_seq", 0)
        for idx, (m, row, seq) in enumerate(toks):
            if seq != cur:
                Sp = Sbufs[cur % 2]
                self.DMA("sp", So[cur][:, st_col0:st_col0 + ncol], Sp[:, 0:ncol], Sp, So)
                cur = seq
                Sn = Sbufs[cur % 2]
                self.DMA("sp", Sn[:, 0:ncol], st_in[seq - 1][:, st_col0:st_col0 + ncol], st_in, Sn)
            step_fn(m, row, Sbufs[cur % 2], idx == 0, idx == len(toks) - 1)
        self._cur_seq = cur
        if i == self.NT - 1:
            Sp = Sbufs[cur % 2]
            self.DMA("sp", So[cur][:, st_col0:st_col0 + ncol], Sp[:, 0:ncol], Sp, So)
            self._cur_seq = 0

    def chunk_consts(self):
        P = self.P
        self.dm = P.sb("dm", [64, 64])
        self.maskT = P.sb("maskT", [64, 64])
        self.ones64 = P.sb("ones64", [64, 64])
        self.SEG = P.sb("SEG", [64, 256])
        self.GQK = P.sb("GQK", [64, 16])
        dm, maskT, SEG, GQK = self.dm, self.maskT, self.SEG, self.GQK
        P.op("pool", lambda e: e.iota(dm[:], pattern=[[1, 64]], base=0, channel_multiplier=-1,
                                      allow_small_or_imprecise_dtypes=True), [], [dm])
        self.TS("dve", maskT[:], dm[:], 0.0, None, ALU.is_ge, None, [dm], [maskT])
        self.MEMSET("dve", self.ones64[:], 1.0, [self.ones64])
        P.op("pool", lambda e: e.iota(GQK[:, 8:9], pattern=[[0, 1]], base=1, channel_multiplier=1,
                                      allow_small_or_imprecise_dtypes=True), [], [GQK])
        P.op("pool", lambda e: e.iota(GQK[:, 9:10], pattern=[[0, 1]], base=63, channel_multiplier=-1,
                                      allow_small_or_imprecise_dtypes=True), [GQK], [GQK])
        for h in range(4):
            lg = math.log(1.0 - 2.0 ** (-5.0 - h))
            self.ACT(SEG[:, h * 64:(h + 1) * 64], dm[:], AF.Exp, [dm], [SEG], scale=lg)
            self.TT("dve", SEG[:, h * 64:(h + 1) * 64], SEG[:, h * 64:(h + 1) * 64], maskT[:], ALU.mult,
                    [SEG, maskT], [SEG])
            self.ACT(GQK[:, h:h + 1], GQK[:, 8:9], AF.Exp, [GQK], [GQK], scale=lg)
            self.ACT(GQK[:, 4 + h:5 + h], GQK[:, 9:10], AF.Exp, [GQK], [GQK], scale=lg)

    def even_pass(self):
        P, ar, PS = self.P, self.ar, self.PS
        P.barrier()
        ar.reset()
        al = ar.alloc
        Pt = al("Pt", EVEN_IN)
        R0 = ar.off
        Dc, Kc, Qc = al("Dc", 2048), al("Kc", 2048), al("Qc", 2048)
        Vb = [al("Vb0", 2048), al("Vb1", 2048)]
        tmpKV = al("tmpKV", 2048)
        R1 = ar.off
        S = [al("S0", 2048), al("S1", 2048)]
        W1, W2, W3, W4 = al("W1", 1024), al("W2", 1024), al("W3", 1024), al("W4", 1024)
        RQ, RK = al("RQ", 512), al("RK", 512)
        T1, T2 = al("T1", 256), al("T2", 256)
        lb, oml, gn = al("lb", 1024), al("oml", 1024), al("gn", 128)
        L3 = al("L3", 3072)
        rot = al("rot", 256)
        MIX = al("MIX", 2048)
        st = al("st", 32)
        DEC = al("DEC", 8)
        ro = [R0]

        def ral(name, n):
            b_ = ar.alias(name, self.AR.t[:, ro[0]:ro[0] + n])
            ro[0] += n
            assert ro[0] <= R1
            return b_
        LF, EB, ENB, Bs = ral("LF", 1024), ral("EB", 1024), ral("ENB", 1024), ral("Bs", 1024)
        QT, KT, QB = ral("QT", 1024), ral("KT", 1024), ral("QB", 512)
        wb = self.WBF.t
        SBF = ar.alias("SBF", wb[:, 0:2048])
        SBFh = [ar.alias("SBFh%d" % h, wb[:, h * 128:(h + 1) * 128]) for h in range(8)] + \
               [ar.alias("SBFr%d" % h, wb[:, 1024 + h * 256:1024 + (h + 1) * 256]) for h in range(4)]
        VBF = ar.alias("VBF", wb[:, 2048:4096])
        KEB = ar.alias("KEB", wb[:, 4096:5632])
        TRBp = [ar.alias("TRB%d" % i, wb[:, 5632 + 256 * i:5632 + 256 * (i + 1)]) for i in range(2)]
        PTBp = [ar.alias("PTB%d" % i, wb[:, 6144 + 64 * i:6144 + 64 * (i + 1)]) for i in range(2)]
        maskT, ones64, SEG, GQK = self.maskT, self.ones64, self.SEG, self.GQK
        for r in range(3):
            self.bload("sp", ar.alias("L3v", L3.t[:, r * 1024:(r + 1) * 1024]), 1024, self.lb_logits, self.lb_logits[r])
        P.barrier()
        self.ACT(L3[:], L3[:], AF.Exp, [L3], [L3])
        self.TT("dve", W1[:], L3[:, 0:1024], L3[:, 1024:2048], ALU.add, [L3], [W1])
        self.TT("dve", W1[:], W1[:], L3[:, 2048:3072], ALU.add, [L3, W1], [W1])
        P.op("dve", lambda e: e.reciprocal(out=W1[:], in_=W1[:]), [W1], [W1])
        self.TT("dve", lb[:], L3[:, 0:1024], W1[:], ALU.mult, [L3, W1], [lb])
        self.TS("dve", oml[:], lb[:], -1.0, 1.0, ALU.mult, ALU.add, [lb], [oml])
        self.bload("sp", gn, 128, self.hgrn_norm_g, self.hgrn_norm_g)
        self.MEMSET("dve", S[0][:], 0.0, [S[0]])
        for sb_ in SBFh:
            self.MEMSET("dve", sb_[:, :], 0.0, [sb_])
        self._cur_seq = 0
        proj = self.proj_d

        def hgrn_elem(n):
            pp = slice(0, n)
            self.ACT(W1[pp, :], Pt[pp, 1024:2048], AF.Sigmoid, [Pt], [W1])
            self.TT("dve", W1[pp, :], W1[pp, :], oml[pp, :], ALU.mult, [W1, oml], [W1])
            self.TT("dve", W1[pp, :], W1[pp, :], lb[pp, :], ALU.add, [W1, lb], [W1])
            self.TS("dve", W2[pp, :], W1[pp, :], -1.0, 1.0, ALU.mult, ALU.add, [W1], [W2])
            self.ACT(W3[pp, :], Pt[pp, 0:1024], AF.Silu, [Pt], [W3])

        def rotary(n):
            pp = slice(0, n)
            for (dst, c0, ci, si) in ((RQ, 4096, 0, 1), (RK, 4608, 2, 3)):
                xin = v3(Pt[pp, c0:c0 + 512], 4)
                x1, x2 = xin[:, :, 0:64], xin[:, :, 64:128]
                cosb = bc(rot[pp, ci * 64:(ci + 1) * 64], [n, 4, 64], 1)
                sinb = bc(rot[pp, si * 64:(si + 1) * 64], [n, 4, 64], 1)
                d3 = v3(dst[pp, :], 4)
                t1, t2 = v3(T1[pp, :], 4), v3(T2[pp, :], 4)
                self.TT("dve", t1, x1, cosb, ALU.mult, [Pt, rot], [T1])
                self.TT("dve", t2, x2, sinb, ALU.mult, [Pt, rot], [T2])
                self.TT("dve", d3[:, :, 0:64], t1, t2, ALU.subtract, [T1, T2], [dst])
                self.TT("dve", t1, x1, sinb, ALU.mult, [Pt, rot], [T1])
                self.TT("dve", t2, x2, cosb, ALU.mult, [Pt, rot], [T2])
                self.TT("dve", d3[:, :, 64:128], t1, t2, ALU.add, [T1, T2], [dst])

        def post(n, rows):
            pp = slice(0, n)
            for j in range(2):
                self.ACT(W4[pp, 512 * j:512 * (j + 1)], PS[j][pp, :], AF.Square, [PS[j]], [W4])
            self.RED(st[pp, 0:8], v3(W4[pp, :], 8), ALU.add, [W4], [st])
            self.rsqrt(st[pp, 0:8], st[pp, 0:8], 1.0 / 128, 1e-6, [st], [st])
            for j in range(2):
                self.TT("dve", v3(MIX[pp, 512 * j:512 * (j + 1)], 4), v3(PS[j][pp, :], 4),
                        bc(st[pp, 4 * j:4 * j + 4], [n, 4, 128], 2), ALU.mult, [PS[j], st], [MIX])
            self.TT("dve", v3(MIX[pp, 0:1024], 8), v3(MIX[pp, 0:1024], 8), bc(gn[pp, :], [n, 8, 128], 1), ALU.mult,
                    [MIX, gn], [MIX])
            self.ACT(W4[pp, :], Pt[pp, 3072:4096], AF.Silu, [Pt], [W4])
            self.TT("dve", MIX[pp, 0:1024], MIX[pp, 0:1024], W4[pp, :], ALU.mult, [MIX, W4], [MIX])
            for j in range(2):
                self.CP("act", W1[pp, 512 * j:512 * (j + 1)], PS[2 + j][pp, :], [PS[2 + j]], [W1])
            self.RED(st[pp, 8:12], v3(W1[pp, :], 4), ALU.add, [W1], [st])
            self.TS("dve", st[pp, 8:12], st[pp, 8:12], -1.0 / 256, None, ALU.mult, None, [st], [st])
            self.TT("dve", v3(W1[pp, :], 4), v3(W1[pp, :], 4), bc(st[pp, 8:12], [n, 4, 256], 2), ALU.add, [W1, st], [W1])
            self.ACT(W2[pp, :], W1[pp, :], AF.Square, [W1], [W2])
            self.RED(st[pp, 12:16], v3(W2[pp, :], 4), ALU.add, [W2], [st])
            self.rsqrt(st[pp, 12:16], st[pp, 12:16], 1.0 / 256, 1e-5, [st], [st])
            self.TT("dve", v3(MIX[pp, 1024:2048], 4), v3(W1[pp, :], 4), bc(st[pp, 12:16], [n, 4, 256], 2), ALU.mult,
                    [W1, st], [MIX])
            self.ACT(W4[pp, :], Pt[pp, 6144:7168], AF.Silu, [Pt], [W4])
            self.TT("dve", MIX[pp, 1024:2048], MIX[pp, 1024:2048], W4[pp, :], ALU.mult, [MIX, W4], [MIX])
            self.DMA("sp", self.mix_d[rows, :], MIX[pp, :], MIX, self.mix_d)

        S0 = S[0]
        g64 = [(1.0 - 2.0 ** (-5.0 - h)) ** 64 for h in range(4)]
        pp = slice(0, 64)
        for c in range(2 * self.NF):
            r0 = 64 * c
            rows = slice(r0, r0 + 64)
            self.DMA("sp", Pt[pp, :], proj[rows, 0:EVEN_IN], proj, Pt)
            self.DMA("sp", v3(rot[pp, :], 4), self.rot[:, rows, :].rearrange("a p c -> p a c"), self.rot, rot)
            hgrn_elem(64)
            self.ACT(LF[pp, :], W1[pp, :], AF.Ln, [W1], [LF])
            self.CP("act", VBF[pp, 0:1024], Pt[pp, 2048:3072], [Pt], [VBF])
            self.CP("act", VBF[pp, 1024:2048], Pt[pp, 5120:6144], [Pt], [VBF])
            for j in range(2):
                self.MM(PS[4 + j][pp, :], maskT[:], LF[pp, 512 * j:512 * (j + 1)], True, True, [maskT, LF], [PS[4 + j]])
                self.MM(PS[6 + j][pp, :], ones64[:], LF[pp, 512 * j:512 * (j + 1)], True, True, [ones64, LF], [PS[6 + j]])
            for h in range(8):
                self.MM(PS[0][:, h:h + 1], LF[pp, h * 128:(h + 1) * 128], ones64[:, 0:1], True, True, [LF, ones64], [PS[0]])
            self.ACT(DEC[:, 0:8], PS[0][:, 0:8], AF.Exp, [PS[0]], [DEC])
            for j in range(2):
                cs = slice(512 * j, 512 * (j + 1))
                self.CP("act", Bs[pp, cs], PS[4 + j][pp, :], [PS[4 + j]], [Bs])
                self.ACT(EB[pp, cs], PS[4 + j][pp, :], AF.Exp, [PS[4 + j]], [EB])
                self.ACT(ENB[pp, cs], PS[4 + j][pp, :], AF.Exp, [PS[4 + j]], [ENB], scale=-1.0)
                self.TT("dve", Bs[pp, cs], PS[6 + j][pp, :], Bs[pp, cs], ALU.subtract, [PS[6 + j], Bs], [Bs])
            self.ACT(Bs[pp, :], Bs[pp, :], AF.Exp, [Bs], [Bs])
            self.TT("dve", QT[pp, :], W3[pp, :], EB[pp, :], ALU.mult, [W3, EB], [QT])
            self.TT("dve", KT[pp, :], W2[pp, :], ENB[pp, :], ALU.mult, [W2, ENB], [KT])
            self.TT("dve", KEB[pp, 0:1024], W2[pp, :], Bs[pp, :], ALU.mult, [W2, Bs], [KEB])
            def h_stage1(h):
                par = h % 2
                hs = slice(h * 128, (h + 1) * 128)
                pbT = PS[4 + par]
                self.TR(pbT[:, 0:64], QT[pp, hs], [QT], [pbT])
                self.TR(pbT[:, 64:128], KT[pp, hs], [KT], [pbT])
                self.CP("act", TRBp[par][:, 0:128], pbT[:, 0:128], [pbT], [TRBp[par]])

            def h_stage2(h):
                par = h % 2
                hs = slice(h * 128, (h + 1) * 128)
                pbS = PS[6 + par]
                trb, ptb = TRBp[par], PTBp[par]
                qTb, kTb = trb[:, 0:64], trb[:, 64:128]
                self.MM(pbS[pp, 0:64], kTb, qTb, True, True, [trb], [pbS])
                self.TT("dve", ptb[pp, 0:64], pbS[pp, 0:64], maskT[:], ALU.mult, [pbS, maskT], [ptb])
                ob = PS[h // 4][pp, (h % 4) * 128:(h % 4 + 1) * 128]
                self.MM(ob, ptb[pp, 0:64], VBF[pp, hs], True, False, [ptb, VBF], [PS[h // 4]])
                self.MM(ob, qTb, SBFh[h][:, :], False, True, [trb, SBFh[h]], [PS[h // 4]])
                self.MM(pbS[:, 128:256], KEB[pp, hs], VBF[pp, hs], True, True, [KEB, VBF], [pbS])
                self.STT("dve", S0[:, hs], S0[:, hs], DEC[:, h:h + 1], pbS[:, 128:256], ALU.mult, ALU.add,
                         [S0, DEC, pbS], [S0])
                self.CP("act", SBFh[h][:, :], S0[:, hs], [S0], [SBFh[h]])

            for h in range(9):
                if h < 8:
                    h_stage1(h)
                if h > 0:
                    h_stage2(h - 1)
            rotary(64)
            self.TT("dve", v3(QB[pp, :], 4), v3(RQ[pp, :], 4), bc(GQK[:, 0:4], [64, 4, 128], 2), ALU.mult, [RQ, GQK], [QB])
            self.TT("dve", v3(KEB[pp, 1024:1536], 4), v3(RK[pp, :], 4), bc(GQK[:, 4:8], [64, 4, 128], 2), ALU.mult,
                    [RK, GQK], [KEB])

            def r_stage1(h):
                par = h % 2
                hs = slice(h * 128, (h + 1) * 128)
                pbT = PS[4 + par]
                self.TR(pbT[:, 0:64], RQ[pp, hs], [RQ], [pbT])
                self.TR(pbT[:, 64:128], RK[pp, hs], [RK], [pbT])
                self.TR(pbT[:, 128:192], QB[pp, hs], [QB], [pbT])
                self.CP("act", TRBp[par][:, 0:192], pbT[:, 0:192], [pbT], [TRBp[par]])

            def r_stage2(h):
                par = h % 2
                vs = slice(1024 + h * 256, 1024 + (h + 1) * 256)
                pbS = PS[6 + par]
                trb, ptb = TRBp[par], PTBp[par]
                qTb, kTb, qbTb = trb[:, 0:64], trb[:, 64:128], trb[:, 128:192]
                self.MM(pbS[pp, 0:64], kTb, qTb, True, True, [trb], [pbS])
                self.TT("dve", ptb[pp, 0:64], pbS[pp, 0:64], SEG[:, h * 64:(h + 1) * 64], ALU.mult, [pbS, SEG], [ptb])
                ob = PS[2 + h // 2][pp, (h % 2) * 256:(h % 2 + 1) * 256]
                self.MM(ob, ptb[pp, 0:64], VBF[pp, vs], True, False, [ptb, VBF], [PS[2 + h // 2]])
                self.MM(ob, qbTb, SBFh[8 + h][:, :], False, True, [trb, SBFh[8 + h]], [PS[2 + h // 2]])
                self.MM(pbS[:, 128:384], KEB[pp, 1024 + h * 128:1024 + (h + 1) * 128], VBF[pp, vs], True, True,
                        [KEB, VBF], [pbS])
                self.STT("dve", S0[:, vs], S0[:, vs], g64[h], pbS[:, 128:384], ALU.mult, ALU.add, [S0, pbS], [S0])
                self.CP("act", SBFh[8 + h][:, :], S0[:, vs], [S0], [SBFh[8 + h]])

            for h in range(5):
                if h < 4:
                    r_stage1(h)
                if h > 0:
                    r_stage2(h - 1)
            post(64, rows)
        P.barrier()
        for h in range(4):
            gam = 1.0 - 2.0 ** (-5.0 - h)
            self.MEMSET("dve", Dc[:, (8 + 2 * h) * 128:(10 + 2 * h) * 128], gam, [Dc])
        for i in range(self.NF, self.NT):
            rows = slice(i * 128, (i + 1) * 128)
            self.DMA("sp", Pt[:], proj[rows, 0:EVEN_IN], proj, Pt)
            self.DMA("sp", v3(rot[:], 4), self.rot[:, rows, :].rearrange("a p c -> p a c"), self.rot, rot)
            hgrn_elem(128)
            rotary(128)
            pbi = 0
            for (src, dstc) in ((W1, Dc), (W2, Kc), (W3, Qc)):
                for hb in range(2):
                    pb = PS[4 + pbi % 4]
                    pbi += 1
                    for k4 in range(4):
                        h = hb * 4 + k4
                        self.TR(pb[:, k4 * 128:(k4 + 1) * 128], src[:, h * 128:(h + 1) * 128], [src], [pb])
                    self.CP("act", dstc[:, hb * 512:(hb + 1) * 512], pb[:], [pb], [dstc])
            for (src, dstc) in ((RK, Kc), (RQ, Qc)):
                pb = PS[4 + pbi % 4]
                pbi += 1
                for h in range(4):
                    self.TR(pb[:, h * 128:(h + 1) * 128], src[:, h * 128:(h + 1) * 128], [src], [pb])
                o = dstc[:, 1024:2048].rearrange("p (h r t) -> p h r t", h=4, r=2)
                self.CP("act", o, bc(v3(pb[:], 4), [128, 4, 2, 128], 2), [pb], [dstc])
            cntr = [0]

            def step(m, row, Sb, first, last):
                vb = Vb[cntr[0] % 2]
                cntr[0] += 1
                self.DMA("sp", vb[:, 0:1024], proj[row, 2048:3072].partition_broadcast(128), proj, vb)
                self.DMA("sp", vb[:, 1024:2048], proj[row, 5120:6144].partition_broadcast(128), proj, vb)
                col = lambda c_: (v3(c_[:], 16)[:, :, m], c_)
                self.scan_step(m, first, last, Sb, 16, 128, col(Dc), col(Kc), col(Qc), vb, vb[:], tmpKV,
                               [(PS[j], 512 * j, 512, self.Z) for j in range(4)], None)

            self.run_scan(i, S, 2048, self.st_even, 0, self.So_even, step, None)
            post(128, rows)

    def odd_pads(self):
        P, proj, NP = self.P, self.proj_d, self.NP
        P.barrier()
        q = "sp"
        self.DMA(q, self.xbcp_d[0:3, :], self.zeros_d[0:3, 0:1536], self.zeros_d, self.xbcp_d)
        self.DMA(q, self.xbcp_d[3:3 + NP, :], proj[0:NP, 1024:2560], proj, self.xbcp_d)
        self.DMA(q, self.xbcs_d[0:3], self.st_conv[:], self.st_conv, self.xbcs_d)
        self.DMA(q, self.xbcs_d[3:7].rearrange("t s c -> (t s) c"), proj[NP:NP + 64, 1024:2560], proj, self.xbcs_d)
        self.DMA(q, self.rwp_d[0:1, :], self.zeros_d[0:1, :], self.zeros_d, self.rwp_d)
        self.DMA(q, self.rwp_d[1:1 + NP, :], proj[0:NP, 2576:5936], proj, self.rwp_d)
        self.DMA(q, self.rws_d[0], self.st_shift[:], self.st_shift, self.rws_d)
        self.DMA(q, self.rws_d[1:5].rearrange("t s c -> (t s) c"), proj[NP:NP + 64, 2576:5936], proj, self.rws_d)
        self.DMA(q, self.conv_o[:, 0, :], proj[NP - 3:NP, 1024:2560], proj, self.conv_o)
        for t in range(3):
            self.DMA(q, self.conv_o[t, 1:NSEQ + 1, :], proj[NP + 16 * (t + 1):NP + 16 * (t + 2), 1024:2560], proj,
                     self.conv_o)
        self.DMA(q, self.shift_o[0:1, :], proj[NP - 1:NP, 2576:5936], proj, self.shift_o)
        self.DMA(q, self.shift_o[1:NSEQ + 1, :], proj[NP + 48:NP + 64, 2576:5936], proj, self.shift_o)

    def shifted_load(self, i, dst, pad_p, pad_s, lead, shift, ncol):
        if i < self.NF:
            r = i * 128 + lead - shift
            self.DMA("sp", dst[:, 0:ncol], pad_p[r:r + 128, :], pad_p, dst)
        else:
            r = i * 128 + lead - shift
            self.DMA("sp", dst[0:16, 0:ncol], pad_p[r:r + 16, :], pad_p, dst)
            self.DMA("sp", dst[16:80, 0:ncol], pad_s[lead - shift:lead - shift + 4].rearrange("t s c -> (t s) c"),
                     pad_s, dst)

    def ssd_pass(self):
        P, ar, PS = self.P, self.ar, self.PS
        P.barrier()
        ar.reset()
        al = ar.alloc
        proj = self.proj_d
        Zt, DT = al("Zt", 1024), al("DT", 16)
        XS = [al("XS%d" % s, 1536) for s in range(4)]
        ACC, TMP = al("ACC", 1536), al("TMP", 1536)
        Kc, Qc = al("Kc", 2048), al("Qc", 2048)
        Vb = [al("Vb0", 1040), al("Vb1", 1040)]
        S = [al("S0", 1024), al("S1", 1024)]
        tmpKV = al("tmpKV", 1024)
        CW = al("CW", 4 * 1536)
        CB = al("CB", 1536)
        dtb, Ab, dsk = al("dtb", 16), al("Ab", 16), al("dsk", 16)
        sg = al("sg", 1024)
        VR = al("VR", 1040)
        MIX = al("MIX", 1024)
        st = al("st", 8)
        SEGT, RB, BLK, O2 = al("SEGT", 1024), al("RB", 1024), al("BLK", 1024), al("O2", 1024)
        MB, NBT = al("MB", 64), al("NBT", 64)
        LFs, Bsb, EBt, WE, DECB = al("LFs", 16), al("Bsb", 16), al("EBt", 16), al("WE", 16), al("DECB", 16)
        ones128 = al("ones128", 128)
        wb = self.WBF.t
        SBF = ar.alias("SBF", wb[:, 0:1024])
        VB = ar.alias("VB", wb[:, 1024:2048])
        VE = ar.alias("VE", wb[:, 2048:3072])
        BMb = ar.alias("BMb", wb[:, 3072:3328])
        TRB = ar.alias("TRB", wb[:, 3328:3584])
        PT = ar.alias("PT", wb[:, 3584:4608])
        maskT, ones64, dm, ident = self.maskT, self.ones64, self.dm, self.ident
        for j in range(4):
            self.bload("sp", ar.alias("CWj", CW.t[:, j * 1536:(j + 1) * 1536]), 1536, self.conv_w, self.conv_w[j])
        self.bload("sp", CB, 1536, self.conv_b, self.conv_b)
        self.bload("sp", dtb, 16, self.dt_bias, self.dt_bias)
        self.bload("sp", Ab, 16, self.a_log, self.a_log)
        self.bload("sp", dsk, 16, self.d_skip, self.d_skip)
        self.bload("sp", sg, 1024, self.ssm_norm_g, self.ssm_norm_g)
        P.barrier()
        self.ACT(Ab[:], Ab[:], AF.Exp, [Ab], [Ab])
        for xs in XS:
            self.MEMSET("dve", xs[:], 0.0, [xs])
        self.MEMSET("dve", S[0][:], 0.0, [S[0]])
        self.MEMSET("dve", SBF[:], 0.0, [SBF])
        self.MEMSET("dve", ones128[0:64, :], 1.0, [ones128])
        P.op("pool", lambda e: e.iota(v3(BLK[0:16, :], 16), pattern=[[1, 16], [0, 64]], base=0, channel_multiplier=-1,
                                      allow_small_or_imprecise_dtypes=True), [], [BLK])
        self.TS("dve", BLK[0:16, :], BLK[0:16, :], 0.0, None, ALU.is_equal, None, [BLK], [BLK])
        self.TS("dve", MB[0:64, :], dm[:], 0.0, -30000.0, ALU.is_lt, ALU.mult, [dm], [MB])
        self._cur_seq = 0

        def prep(n, XSl):
            pp = slice(0, n)
            self.TT("dve", ACC[pp, :], XSl[0][pp, :], CW[pp, 3 * 1536:4 * 1536], ALU.mult, [XSl[0], CW], [ACC])
            for s_ in range(1, 4):
                self.TT("dve", TMP[pp, :], XSl[s_][pp, :], CW[pp, (3 - s_) * 1536:(4 - s_) * 1536], ALU.mult,
                        [XSl[s_], CW], [TMP])
                self.TT("dve", ACC[pp, :], ACC[pp, :], TMP[pp, :], ALU.add, [ACC, TMP], [ACC])
            self.TT("dve", ACC[pp, :], ACC[pp, :], CB[pp, :], ALU.add, [ACC, CB], [ACC])
            self.ACT(ACC[pp, :], ACC[pp, :], AF.Silu, [ACC], [ACC])
            self.TT("dve", DT[pp, :], DT[pp, :], dtb[pp, :], ALU.add, [DT, dtb], [DT])
            self.ACT(DT[pp, :], DT[pp, :], AF.Exp, [DT], [DT])
            self.ACT(DT[pp, :], DT[pp, :], AF.Ln, [DT], [DT], bias=1.0)

        def post(n, rows):
            pp = slice(0, n)
            self.TT("dve", v3(TMP[pp, 0:1024], 16), v3(ACC[pp, 0:1024], 16), bc(dsk[pp, :], [n, 16, 64], 2), ALU.mult,
                    [ACC, dsk], [TMP])
            self.TT("dve", MIX[pp, :], MIX[pp, :], TMP[pp, 0:1024], ALU.add, [MIX, TMP], [MIX])
            self.ACT(Zt[pp, :], Zt[pp, :], AF.Silu, [Zt], [Zt])
            self.TT("dve", MIX[pp, :], MIX[pp, :], Zt[pp, :], ALU.mult, [MIX, Zt], [MIX])
            for g in range(2):
                self.ACT(TMP[pp, 0:512], MIX[pp, 512 * g:512 * (g + 1)], AF.Square, [MIX], [TMP, st],
                         accum=st[pp, g:g + 1])
            self.rsqrt(st[pp, 0:2], st[pp, 0:2], 1.0 / 512, 1e-6, [st], [st])
            self.TT("dve", v3(MIX[pp, :], 2), v3(MIX[pp, :], 2), bc(st[pp, 0:2], [n, 2, 512], 2), ALU.mult, [MIX, st], [MIX])
            self.TT("dve", MIX[pp, :], MIX[pp, :], sg[pp, :], ALU.mult, [MIX, sg], [MIX])
            self.DMA("sp", self.mix_d[rows, 0:1024], MIX[pp, :], MIX, self.mix_d)

        pp = slice(0, 64)
        S0 = S[0]
        for c in range(2 * self.NF):
            r0 = 64 * c
            rows = slice(r0, r0 + 64)
            self.DMA("sp", Zt[pp, :], proj[rows, 0:1024], proj, Zt)
            self.DMA("sp", DT[pp, :], proj[rows, 2560:2576], proj, DT)
            for s_ in range(4):
                self.DMA("sp", XS[s_][pp, :], self.xbcp_d[r0 + 3 - s_:r0 + 3 - s_ + 64, :], self.xbcp_d, XS[s_])
            prep(64, XS)
            self.STT("dve", LFs[pp, :], DT[pp, :], -1.0, Ab[pp, :], ALU.mult, ALU.mult, [DT, Ab], [LFs])
            self.MM(PS[6][pp, 0:16], maskT[:], LFs[pp, :], True, True, [maskT, LFs], [PS[6]])
            self.MM(PS[6][pp, 16:32], ones64[:], LFs[pp, :], True, True, [ones64, LFs], [PS[6]])
            self.MM(PS[6][:, 32:48], ones128[0:64, :], LFs[pp, :], True, True, [ones128, LFs], [PS[6]])
            self.CP("act", Bsb[pp, :], PS[6][pp, 0:16], [PS[6]], [Bsb])
            self.ACT(EBt[pp, :], PS[6][pp, 0:16], AF.Exp, [PS[6]], [EBt])
            self.TT("dve", WE[pp, :], PS[6][pp, 16:32], Bsb[pp, :], ALU.subtract, [PS[6], Bsb], [WE])
            self.ACT(WE[pp, :], WE[pp, :], AF.Exp, [WE], [WE])
            self.ACT(DECB[:, :], PS[6][:, 32:48], AF.Exp, [PS[6]], [DECB])
            self.TT("dve", v3(O2[pp, :], 16), v3(ACC[pp, 0:1024], 16), bc(DT[pp, :], [64, 16, 64], 2), ALU.mult,
                    [ACC, DT], [O2])
            self.CP("act", VB[pp, :], O2[pp, :], [O2], [VB])
            self.TT("dve", v3(VE[pp, :], 16), v3(O2[pp, :], 16), bc(WE[pp, :], [64, 16, 64], 2), ALU.mult, [O2, WE], [VE])
            self.CP("act", BMb[pp, :], ACC[pp, 1024:1280], [ACC], [BMb])
            self.TR(PS[6][0:16, 64:128], Bsb[pp, :], [Bsb], [PS[6]])
            self.ACT(NBT[0:16, :], PS[6][0:16, 64:128], AF.Copy, [PS[6]], [NBT], scale=-1.0)
            self.TT("dve", v3(RB[pp, :], 16), bc(Bsb[pp, :], [64, 16, 64], 2), bc(ident[0:64, 0:64], [64, 16, 64], 1),
                    ALU.mult, [Bsb, ident], [RB])
            for j in range(2):
                cs = slice(512 * j, 512 * (j + 1))
                self.MM(PS[4 + j][pp, :], NBT[0:16, :], BLK[0:16, cs], True, False, [NBT, BLK], [PS[4 + j]])
                self.MM(PS[4 + j][pp, :], ones64[:], RB[pp, cs], False, True, [ones64, RB], [PS[4 + j]])
                self.STT("dve", v3(SEGT[pp, cs], 8), v3(PS[4 + j][pp, :], 8), 0.0, bc(MB[pp, :], [64, 8, 64], 1),
                         ALU.min, ALU.add, [PS[4 + j], MB], [SEGT])
            self.ACT(SEGT[pp, :], SEGT[pp, :], AF.Exp, [SEGT], [SEGT])
            for g in range(2):
                self.TR(PS[7][:, g * 128:g * 128 + 64], ACC[pp, 1024 + g * 128:1024 + (g + 1) * 128], [ACC], [PS[7]])
                self.TR(PS[7][:, g * 128 + 64:(g + 1) * 128], ACC[pp, 1280 + g * 128:1280 + (g + 1) * 128], [ACC], [PS[7]])
            self.CP("act", TRB[:, 0:256], PS[7][:, 0:256], [PS[7]], [TRB])
            for g in range(2):
                BTg, CTg = TRB[:, g * 128:g * 128 + 64], TRB[:, g * 128 + 64:(g + 1) * 128]
                self.MM(PS[7][pp, 256 + g * 64:256 + (g + 1) * 64], BTg, CTg, True, True, [TRB], [PS[7]])
            for g in range(2):
                cs = slice(512 * g, 512 * (g + 1))
                self.TT("dve", v3(PT[pp, cs], 8), v3(SEGT[pp, cs], 8),
                        bc(PS[7][pp, 256 + g * 64:256 + (g + 1) * 64], [64, 8, 64], 1), ALU.mult, [SEGT, PS[7]], [PT])
            for h in range(16):
                hs = slice(h * 64, (h + 1) * 64)
                self.MM(PS[h // 8][pp, (h % 8) * 64:(h % 8 + 1) * 64], PT[pp, hs], VB[pp, hs], True, True,
                        [PT, VB], [PS[h // 8]])
            for g in range(2):
                cs = slice(512 * g, 512 * (g + 1))
                CTg = TRB[:, g * 128 + 64:(g + 1) * 128]
                self.MM(PS[2 + g][pp, :], CTg, SBF[:, cs], True, True, [TRB, SBF], [PS[2 + g]])
                self.TT("dve", v3(O2[pp, cs], 8), v3(PS[2 + g][pp, :], 8), bc(EBt[pp, 8 * g:8 * g + 8], [64, 8, 64], 2),
                        ALU.mult, [PS[2 + g], EBt], [O2])
                self.TT("dve", MIX[pp, cs], PS[g][pp, :], O2[pp, cs], ALU.add, [PS[g], O2], [MIX])
            for g in range(2):
                cs = slice(512 * g, 512 * (g + 1))
                self.MM(PS[4 + g][:, :], BMb[pp, g * 128:(g + 1) * 128], VE[pp, cs], True, True, [BMb, VE], [PS[4 + g]])
                self.TT("dve", v3(S0[:, cs], 8), v3(S0[:, cs], 8), bc(DECB[:, 8 * g:8 * g + 8], [128, 8, 64], 2), ALU.mult,
                        [S0, DECB], [S0])
                self.TT("dve", S0[:, cs], S0[:, cs], PS[4 + g][:, :], ALU.add, [S0, PS[4 + g]], [S0])
            self.CP("act", SBF[:, :], S0[:, :], [S0], [SBF])
            post(64, rows)
        P.barrier()
        for i in range(self.NF, self.NT):
            rows = slice(i * 128, (i + 1) * 128)
            self.DMA("sp", Zt[:], proj[rows, 0:1024], proj, Zt)
            self.DMA("sp", DT[:], proj[rows, 2560:2576], proj, DT)
            for s_ in range(4):
                self.shifted_load(i, XS[s_], self.xbcp_d, self.xbcs_d, 3, s_, 1536)
            prep(128, XS)
            self.TT("dve", VR[:, 1024:1040], DT[:], Ab[:], ALU.mult, [DT, Ab], [VR])
            self.ACT(VR[:, 1024:1040], VR[:, 1024:1040], AF.Exp, [VR], [VR], scale=-1.0)
            self.TT("dve", v3(VR[:, 0:1024], 16), v3(ACC[:, 0:1024], 16), bc(DT[:], [128, 16, 64], 2), ALU.mult,
                    [ACC, DT], [VR])
            self.DMA("sp", self.vrow_d[rows, 0:1040], VR[:], VR, self.vrow_d)
            for (c0, dstc, pb) in ((1024, Kc, PS[4]), (1280, Qc, PS[5])):
                for g in range(2):
                    self.TR(pb[:, g * 128:(g + 1) * 128], ACC[:, c0 + g * 128:c0 + (g + 1) * 128], [ACC], [pb])
                o = dstc[:].rearrange("p (g h t) -> p g h t", g=2, h=8)
                self.CP("act", o, bc(v3(pb[:, 0:256], 2), [128, 2, 8, 128], 2), [pb], [dstc])
            cntr = [0]

            def step(m, row, Sb, first, last):
                vb = Vb[cntr[0] % 2]
                cntr[0] += 1
                self.DMA("sp", vb[:], self.vrow_d[row, 0:1040].partition_broadcast(128), self.vrow_d, vb)
                col = lambda c_: (v3(c_[:], 16)[:, :, m], c_)
                self.scan_step(m, first, last, Sb, 16, 64, (vb[:, 1024:1040], vb), col(Kc), col(Qc), vb,
                               vb[:, 0:1024], tmpKV, [(PS[j], 512 * j, 512, self.Z) for j in range(2)], None)

            self.run_scan(i, S, 1024, self.st_odd, 0, self.So_odd, step, None)
            for j in range(2):
                self.CP("act", MIX[:, 512 * j:512 * (j + 1)], PS[j][:], [PS[j]], [MIX])
            post(128, rows)

    def rwkv_pass(self):
        P, ar, PS = self.P, self.ar, self.PS
        P.barrier()
        ar.reset()
        al = ar.alloc
        proj = self.proj_d
        RW, PV, MU = al("RW", 3360), al("PV", 3360), al("MU", 3360)
        Dc, Kc, Qc, NKc, KAc = al("Dc", 1024), al("Kc", 1024), al("Qc", 1024), al("NKc", 1024), al("KAc", 1024)
        Vb = [al("Vb0", 512), al("Vb1", 512)]
        S = [al("S0", 512), al("S1", 512)]
        tmpKV, tmpA, Ub = al("tmpKV", 512), al("tmpA", 512), al("Ub", 512)
        VEC = al("VEC", 7 * 1024)
        w2, a2, g2 = al("w2", 1024), al("a2", 1024), al("g2", 2048)
        WD, AA, GG, KK, KP, T1 = al("WD", 1024), al("AA", 1024), al("GG", 1024), al("KK", 1024), al("KP", 1024), al("T1", 1024)
        KA = al("KA", 1024)
        NK = KK
        TW = al("TW", 384)
        SGd = al("SGd", 160)
        st = al("st", 64)
        vec = lambda j: VEC[:, j * 1024:(j + 1) * 1024]
        w0b, a0b, kkb, kab, rkb, lgb, lbb = [vec(j) for j in range(7)]
        for j in range(7):
            self.bload("sp", ar.alias("VECj", VEC.t[:, j * 1024:(j + 1) * 1024]), 1024, self.rwkv_vec, self.rwkv_vec[j])
        self.bload("sp", MU, 3360, self.shift_mu, self.shift_mu)
        self.DMA("sp", w2[0:64, :], self.rwkv_w2[:], self.rwkv_w2, w2)
        self.DMA("sp", a2[0:64, :], self.rwkv_a2[:], self.rwkv_a2, a2)
        self.DMA("sp", g2[:, 0:1024], self.rwkv_g2[0:128, :], self.rwkv_g2, g2)
        self.DMA("sp", g2[0:32, 1024:2048], self.rwkv_g2[128:160, :], self.rwkv_g2, g2)
        P.barrier()
        self.MEMSET("dve", PV[:], 0.0, [PV])
        self.MEMSET("dve", RW[:], 0.0, [RW])
        self.MEMSET("dve", S[0][:], 0.0, [S[0]])
        self._cur_seq = 0
        vrow = self.vrow_d
        for i in range(self.NT):
            rows = slice(i * 128, (i + 1) * 128)
            self.DMA("sp", RW[:], proj[rows, 2576:5936], proj, RW)
            self.shifted_load(i, PV, self.rwp_d, self.rws_d, 1, 1, 3360)
            self.TT("dve", PV[:], PV[:], RW[:], ALU.subtract, [PV, RW], [PV])
            self.TT("dve", PV[:], PV[:], MU[:], ALU.mult, [PV, MU], [PV])
            self.TT("dve", RW[:], RW[:], PV[:], ALU.add, [RW, PV], [RW])
            r_, k_, v_ = RW[:, 0:1024], RW[:, 1024:2048], RW[:, 2048:3072]
            self.DMA("sp", vrow[rows, 1040:2064], v_, RW, vrow)
            self.ACT(SGd[:, 0:64], RW[:, 3072:3136], AF.Tanh, [RW], [SGd])
            self.TR(PS[4][0:64, 0:128], SGd[:, 0:64], [SGd], [PS[4]])
            self.TR(PS[4][0:64, 128:256], RW[:, 3136:3200], [RW], [PS[4]])
            self.CP("act", TW[0:64, 0:256], PS[4][0:64, 0:256], [PS[4]], [TW])
            for (lo, wgt, dst, bias) in ((0, w2, WD, w0b), (128, a2, AA, a0b)):
                for j in range(2):
                    self.MM(PS[5 + j][:], TW[0:64, lo:lo + 128], wgt[0:64, 512 * j:512 * (j + 1)], True, True,
                            [TW, wgt], [PS[5 + j]])
                    self.TT("dve", dst[:, 512 * j:512 * (j + 1)], PS[5 + j][:], bias[:, 512 * j:512 * (j + 1)],
                            ALU.add, [PS[5 + j], VEC], [dst])
            self.ACT(WD[:], WD[:], AF.Exp, [WD], [WD], scale=-1.0)
            self.ACT(WD[:], WD[:], AF.Ln, [WD], [WD], bias=1.0)
            self.ACT(WD[:], WD[:], AF.Exp, [WD], [WD], scale=-1.0)
            self.ACT(WD[:], WD[:], AF.Exp, [WD], [WD], scale=-math.exp(-0.5))
            self.ACT(AA[:], AA[:], AF.Sigmoid, [AA], [AA])
            self.ACT(SGd[:], RW[:, 3200:3360], AF.Sigmoid, [RW], [SGd])
            self.TR(PS[4][:, 0:128], SGd[:, 0:128], [SGd], [PS[4]])
            self.TR(PS[4][0:32, 128:256], SGd[:, 128:160], [SGd], [PS[4]])
            self.CP("act", TW[:, 0:128], PS[4][:, 0:128], [PS[4]], [TW])
            self.CP("act", TW[0:32, 128:256], PS[4][0:32, 128:256], [PS[4]], [TW])
            for j in range(2):
                self.MM(PS[5 + j][:], TW[:, 0:128], g2[:, 512 * j:512 * (j + 1)], True, False, [TW, g2], [PS[5 + j]])
                self.MM(PS[5 + j][:], TW[0:32, 128:256], g2[0:32, 1024 + 512 * j:1024 + 512 * (j + 1)], False, True,
                        [TW, g2], [PS[5 + j]])
                self.CP("act", GG[:, 512 * j:512 * (j + 1)], PS[5 + j][:], [PS[5 + j]], [GG])
            self.TT("dve", KK[:], k_, kkb, ALU.mult, [RW, VEC], [KK])
            self.ACT(T1[:], KK[:], AF.Square, [KK], [T1])
            self.RED(st[:, 0:16], v3(T1[:], 16), ALU.add, [T1], [st])
            self.TS("dve", st[:, 0:16], st[:, 0:16], 1e-24, None, ALU.max, None, [st], [st])
            self.ACT(st[:, 0:16], st[:, 0:16], AF.Sqrt, [st], [st])
            P.op("dve", lambda e: e.reciprocal(out=st[:, 0:16], in_=st[:, 0:16]), [st], [st])
            self.TT("dve", v3(KK[:], 16), v3(KK[:], 16), bc(st[:, 0:16], [128, 16, 64], 2), ALU.mult, [KK, st], [KK])
            self.STT("dve", T1[:], AA[:], -1.0, kab, ALU.add, ALU.mult, [AA, VEC], [T1])
            self.STT("dve", KP[:], T1[:], 1.0, k_, ALU.add, ALU.mult, [T1, RW], [KP])
            self.TT("dve", KA[:], KK[:], AA[:], ALU.mult, [KK, AA], [KA])
            self.TS("dve", KK[:], KK[:], -1.0, None, ALU.mult, None, [KK], [KK])
            pbi = 0
            for (src, sb_, dstc) in ((WD[:], WD, Dc), (KP[:], KP, Kc), (r_, RW, Qc), (NK[:], NK, NKc), (KA[:], KA, KAc)):
                for hb2 in range(2):
                    pb = PS[5 + pbi % 3]
                    pbi += 1
                    for k4 in range(4):
                        hb = hb2 * 4 + k4
                        self.TR(pb[:, k4 * 128:(k4 + 1) * 128], src[:, hb * 128:(hb + 1) * 128], [sb_], [pb])
                    self.CP("act", dstc[:, hb2 * 512:(hb2 + 1) * 512], pb[:], [pb], [dstc])
            cntr = [0]

            def step(m, row, Sb, first, last, i=i):
                vb = Vb[cntr[0] % 2]
                cntr[0] += 1
                src = vrow[row, 1040:2064].rearrange("(hb k i) -> hb k i", hb=8, k=2)
                for k in range(2):
                    self.DMA("sp", v3(vb[64 * k:64 * (k + 1), :], 8), src[:, k, :].partition_broadcast(64), vrow, vb)
                col = lambda c, mm=m: v3(c[:], 8)[:, :, mm]
                S3 = v3(Sb[:], 8)
                shp = [128, 8, 64]
                TQ, TQ32 = self.TQ, self.TQ32

                def pre(mm):
                    self.TT("dve", v3(tmpA[:], 8), S3, bc(col(NKc, mm), shp, 2), ALU.mult, [Sb, NKc], [tmpA])
                    self.MM(PS[4][:], self.blk1[:], tmpA[:], True, True, [self.blk1, tmpA], [PS[4]])

                full = i < self.NF
                if (not full) or m == 0:
                    pre(m)
                self.TT("pool", v3(tmpKV[:], 8), v3(vb[:], 8), bc(col(Kc), shp, 2), ALU.mult, [vb, Kc], [tmpKV])
                self.TT("dve", S3, S3, bc(col(Dc), shp, 2), ALU.mult, [Sb, Dc], [Sb])
                self.TT("dve", S3, S3, v3(tmpKV[:], 8), ALU.add, [Sb, tmpKV], [Sb])
                self.TT("dve", v3(Ub[:], 8), v3(PS[4][:], 8), bc(col(KAc), shp, 2), ALU.mult, [PS[4], KAc], [Ub])
                self.TT("dve", S3, S3, v3(Ub[:], 8), ALU.add, [Sb, Ub], [Sb])
                if full and m < 127:
                    pre(m + 1)
                self.TT("dve", v3(TQ32[:, 0:512], 8), S3, bc(col(Qc), shp, 2), ALU.mult, [Sb, Qc], [TQ32])
                self.CP("act", TQ[:, 0:512], TQ32[:, 0:512], [TQ32], [TQ])
                for (pb, z) in ((PS[0], self.Z0), (PS[1], self.Z1)):
                    self.MM(pb[:, 0:512], z[:, 127 - m:255 - m], TQ[:, 0:512], first, last, [TQ, z], [pb])

            self.run_scan(i, S, 512, self.st_odd, 1024, self.So_odd, step, None)
            Y = T1
            Y4 = Y[:].rearrange("p (hb k i) -> p hb k i", hb=8, k=2)
            for k in range(2):
                self.CP("act", Y4[:, :, k, :], v3(PS[k][:], 8), [PS[k]], [Y])
            Y3 = v3(Y[:], 16)
            self.RED(st[:, 16:32], Y3, ALU.add, [Y], [st])
            self.TS("dve", st[:, 16:32], st[:, 16:32], -1.0 / 64, None, ALU.mult, None, [st], [st])
            self.TT("dve", Y3, Y3, bc(st[:, 16:32], [128, 16, 64], 2), ALU.add, [Y, st], [Y])
            self.ACT(WD[:], Y[:], AF.Square, [Y], [WD])
            self.RED(st[:, 32:48], v3(WD[:], 16), ALU.add, [WD], [st])
            self.rsqrt(st[:, 32:48], st[:, 32:48], 1.0 / 64, 64e-5, [st], [st])
            self.TT("dve", Y3, Y3, bc(st[:, 32:48], [128, 16, 64], 2), ALU.mult, [Y, st], [Y])
            self.TT("dve", Y[:], Y[:], lgb, ALU.mult, [Y, VEC], [Y])
            self.TT("dve", Y[:], Y[:], lbb, ALU.add, [Y, VEC], [Y])
            self.TT("dve", WD[:], r_, KP[:], ALU.mult, [RW, KP], [WD])
            self.TT("dve", WD[:], WD[:], rkb, ALU.mult, [WD, VEC], [WD])
            self.RED(st[:, 48:64], v3(WD[:], 16), ALU.add, [WD], [st])
            self.TT("dve", v3(WD[:], 16), v3(v_, 16), bc(st[:, 48:64], [128, 16, 64], 2), ALU.mult, [RW, st], [WD])
            self.TT("dve", Y[:], Y[:], WD[:], ALU.add, [Y, WD], [Y])
            self.TT("dve", Y[:], Y[:], GG[:], ALU.mult, [Y, GG], [Y])
            self.DMA("sp", self.mix_d[rows, 1024:2048], Y[:], Y, self.mix_d)

    def convert_tables(self):
        P, ar = self.P, self.ar
        P.barrier()
        ar.reset()
        stg = [ar.alloc("cst%d" % i, 8192) for i in range(2)]
        outb = [self.XT, self.WBF]
        k = 0
        for (src, dst) in ((self.peer_u[0], self.ub_d[0]), (self.peer_v[0], self.vb_d[0]),
                           (self.peer_u[1], self.ub_d[1]), (self.peer_v[1], self.vb_d[1])):
            sv = src.t.rearrange("(p r) d -> p r d", p=128)
            dv = dst.t.rearrange("(p r) d -> p r d", p=128)
            for c in range(32):
                sb_, ob = stg[k % 2], outb[k % 2]
                self.DMA("sp" if k % 2 == 0 else "pool", v3(sb_[:], 4), sv[:, 4 * c:4 * c + 4, :], src, sb_)
                self.CP("act" if k % 2 == 0 else "dve", ob[:, 0:8192], sb_[:], [sb_], [ob])
                self.DMA("sp", dv[:, 4 * c:4 * c + 4, :], v3(ob[:, 0:8192], 4), ob, dst)
                k += 1

    def peer_pass(self, l, h_d, q_d, lni, out_d):
        P, ar, PS = self.P, self.ar, self.PS
        P.barrier()
        ar.reset()
        al = ar.alloc
        Hs = [al("H0", 2048), al("H1", 2048)]
        Q, SC, LO = al("Q", 2048), al("SC", 2048), al("LO", 2048)
        TSv, TI, TIf = al("TSv", 256), al("TI", 256), al("TIf", 256)
        cand, candI = al("cand", 2048), al("candI", 2048)
        TMP, TMP2 = al("TMP", 128), al("TMP2", 256)
        BS, BJ, BJf = al("BS", 128), al("BJ", 128), al("BJf", 128)
        OH = al("OH", 4096)
        IDXf, ACTV = al("IDXf", 128), al("ACTV", 128)
        IDXs = [al("IDX0", 128), al("IDX1", 128)]
        GATEs = [al("GATE0", 128), al("GATE1", 128)]
        NG = 6
        UGf = [al("UG%d" % i, 1024) for i in range(NG)]
        UG = [Buf(b_.name, b_.t.bitcast(BF16)) for b_ in UGf]
        ar.live.extend(UG)
        DGf = [al("DG%d" % i, 64) for i in range(2)]
        DG = [Buf(b_.name, b_.t.bitcast(BF16)) for b_ in DGf]
        ROFF = al("ROFF", 1)
        junk, ACC = al("junk", 2048), al("ACC", 2048)
        G, Bb = al("G", 2048), al("Bb", 2048)
        KT32 = al("KT32", 2048)
        stA, stB = al("stA", 16), al("stB", 16)
        XT = self.XT
        KT = self.WBF
        TIu = TI[:].bitcast(U32)
        BJu = BJ[:].bitcast(U32)
        self.bload("sp", G, D, self.ln_g, self.ln_g[lni])
        self.bload("sp", Bb, D, self.ln_b, self.ln_b[lni])
        self.DMA("sp", v3(KT32[:], 16), self.keysT[l], self.keysT, KT32)
        self.CP("dve", KT[:, 0:2048], KT32[:], [KT32], [KT])
        Ut, Vt = self.ub_d[l].t, self.vb_d[l].t
        for ug in UG:
            self.MEMSET("dve", ug[:], 0.0, [ug])

        def stageA(i):
            par = i % 2
            H, IDX, GATE = Hs[par], IDXs[par], GATEs[par]
            IDXi = IDX[:].bitcast(I32)
            rows = slice(i * 128, (i + 1) * 128)
            self.DMA("sp", H[:], h_d[rows, :], h_d, H)
            self.DMA("sp", Q[:], q_d[rows, :], q_d, Q)
            self.DMA("sp", ROFF[:], self.rowoff[rows, :], self.rowoff, ROFF)
            for kb in range(4):
                pb = PS[kb % 2]
                for kk in range(4):
                    kc = kb * 4 + kk
                    self.TR(pb[:, kk * 128:(kk + 1) * 128], Q[:, kc * 128:(kc + 1) * 128], [Q], [pb])
                self.CP("act", XT[:, kb * 512:(kb + 1) * 512], pb[:], [pb], [XT])
            for kb in range(4):
                pb = PS[2 + kb % 2]
                for kk in range(4):
                    j = kb * 4 + kk
                    self.MM(pb[:, kk * 128:(kk + 1) * 128], XT[:, j * 128:(j + 1) * 128], KT[:, j * 128:(j + 1) * 128],
                            True, True, [XT, KT], [pb])
                self.CP("act", SC[:, kb * 512:(kb + 1) * 512], pb[:], [pb], [SC])
            yield
            for j in range(16):
                sg_ = SC[:, j * 128:(j + 1) * 128]
                a8, b8 = TSv[:, j * 16:j * 16 + 8], TSv[:, j * 16 + 8:j * 16 + 16]
                ia, ib = TIu[:, j * 16:j * 16 + 8], TIu[:, j * 16 + 8:j * 16 + 16]
                P.op("dve", lambda e, a8=a8, sg_=sg_: e.max(out=a8, in_=sg_), [SC], [TSv])
                P.op("dve", lambda e, a8=a8, sg_=sg_: e.match_replace(out=TMP[:], in_to_replace=a8, in_values=sg_,
                                                                    imm_value=-1e30), [SC, TSv], [TMP])
                P.op("dve", lambda e, b8=b8: e.max(out=b8, in_=TMP[:]), [TMP], [TSv])
                P.op("dve", lambda e, ia=ia, a8=a8, sg_=sg_: e.max_index(out=ia, in_max=a8, in_values=sg_), [SC, TSv], [TI])
                P.op("dve", lambda e, ib=ib, b8=b8: e.max_index(out=ib, in_max=b8, in_values=TMP[:]), [TMP, TSv], [TI])
                yield
            self.CP("dve", TIf[:], TIu, [TI], [TIf])
            s4 = TSv[:].rearrange("p (h c k) -> p h c k", h=8, c=2)
            i4 = TIf[:].rearrange("p (h c k) -> p h c k", h=8, c=2)
            c4 = cand[:].rearrange("p (h a b) -> p h a b", h=8, a=16)
            ci4 = candI[:].rearrange("p (h a b) -> p h a b", h=8, a=16)
            shp = [128, 8, 16, 16]
            self.TT("dve", c4, bc(s4[:, :, 0, :], shp, 3), bc(s4[:, :, 1, :], shp, 2), ALU.add, [TSv], [cand])
            yield
            for h in range(8):
                self.STT("dve", ci4[:, h], bc(i4[:, h, 0, :], [128, 16, 16], 2), 128.0,
                         bc(i4[:, h, 1, :], [128, 16, 16], 1), ALU.mult, ALU.add, [TIf], [candI])
            yield
            for h in range(8):
                cg = cand[:, h * 256:(h + 1) * 256]
                a8, b8 = BS[:, h * 16:h * 16 + 8], BS[:, h * 16 + 8:h * 16 + 16]
                ia, ib = BJu[:, h * 16:h * 16 + 8], BJu[:, h * 16 + 8:h * 16 + 16]
                P.op("dve", lambda e, a8=a8, cg=cg: e.max(out=a8, in_=cg), [cand], [BS])
                P.op("dve", lambda e, a8=a8, cg=cg: e.match_replace(out=TMP2[:], in_to_replace=a8, in_values=cg,
                                                                  imm_value=-1e30), [cand, BS], [TMP2])
                P.op("dve", lambda e, b8=b8: e.max(out=b8, in_=TMP2[:]), [TMP2], [BS])
                P.op("dve", lambda e, ia=ia, a8=a8, cg=cg: e.max_index(out=ia, in_max=a8, in_values=cg), [cand, BS], [BJ])
                P.op("dve", lambda e, ib=ib, b8=b8: e.max_index(out=ib, in_max=b8, in_values=TMP2[:]), [TMP2, BS], [BJ])
                self.CP("dve", BJf[:, h * 16:(h + 1) * 16], BJu[:, h * 16:(h + 1) * 16], [BJ], [BJf])
                yield
                oh3 = v3(OH[:], 16)
                s3 = [128, 16, 256]
                self.TT("dve", oh3, bc(self.iot[:], s3, 1), bc(BJf[:, h * 16:(h + 1) * 16], s3, 2), ALU.is_equal,
                        [self.iot, BJf], [OH])
                yield
                self.TT("dve", oh3, oh3, bc(candI[:, h * 256:(h + 1) * 256], s3, 1), ALU.mult, [OH, candI], [OH])
                yield
                self.RED(IDXf[:, h * 16:(h + 1) * 16], oh3, ALU.add, [OH], [IDXf])
                yield
            b3 = v3(BS[:], 8)
            self.TT("dve", v3(GATE[:], 8), b3, bc(b3[:, :, 0], [128, 8, 16], 2), ALU.subtract, [BS], [GATE])
            self.ACT(GATE[:], GATE[:], AF.Exp, [GATE], [GATE])
            self.RED(stA[:, 0:8], v3(GATE[:], 8), ALU.add, [GATE], [stA])
            P.op("dve", lambda e: e.reciprocal(out=stA[:, 0:8], in_=stA[:, 0:8]), [stA], [stA])
            self.TT("dve", v3(GATE[:], 8), v3(GATE[:], 8), bc(stA[:, 0:8], [128, 8, 16], 2), ALU.mult, [GATE, stA], [GATE])
            self.CP("dve", IDXi, IDXf[:], [IDXf], [IDX])
            yield

        def drain(g, n=None):
            if g is None:
                return None
            k = 0
            while n is None or k < n:
                try:
                    next(g)
                except StopIteration:
                    return None
                k += 1
            return g

        drain(stageA(0))
        for i in range(self.NT):
            par = i % 2
            H, IDX, GATE = Hs[par], IDXs[par], GATEs[par]
            IDXi = IDX[:].bitcast(I32)
            rows = slice(i * 128, (i + 1) * 128)
            gA = stageA(i + 1) if i + 1 < self.NT else None
            for j in range(128):
                ug = UG[j % NG]
                P.dma("pool", lambda e, ug=ug, j=j, IDXi=IDXi: e.indirect_dma_start(
                    out=ug[:], out_offset=None, in_=Ut,
                    in_offset=bass.IndirectOffsetOnAxis(ap=IDXi[:, j:j + 1], axis=0)), self.ub_d[l], ug, [IDX])
                self.STT("dve", junk[:], ug[:], 1.0, H[:], ALU.mult, ALU.mult, [ug, H], [junk, ACTV],
                         accum=ACTV[:, j:j + 1])
            self.ACT(ACTV[:], ACTV[:], AF.Gelu, [ACTV], [ACTV])
            self.TT("dve", ACTV[:], ACTV[:], GATE[:], ALU.mult, [ACTV, GATE], [ACTV])
            for j in range(128):
                ug = UG[j % NG]
                dg = DG[j % 2]
                P.dma("pool", lambda e, ug=ug, j=j, IDXi=IDXi: e.indirect_dma_start(
                    out=ug[:], out_offset=None, in_=Vt,
                    in_offset=bass.IndirectOffsetOnAxis(ap=IDXi[:, j:j + 1], axis=0)), self.vb_d[l], ug, [IDX])
                self.ACT(dg[:], self.ident[:], AF.Copy, [self.ident, ACTV], [dg], scale=ACTV[:, j:j + 1])
                for b4 in range(4):
                    self.MM(PS[4 + b4][:], dg[:], ug[:, 512 * b4:512 * (b4 + 1)], j == 0, j == 127, [dg, ug], [PS[4 + b4]])
                if j % 2 == 1:
                    gA = drain(gA, 1)
            gA = drain(gA)
            for b4 in range(4):
                self.CP("act", ACC[:, 512 * b4:512 * (b4 + 1)], PS[4 + b4][:], [PS[4 + b4]], [ACC])
            self.ln_tile(H, ACC, G, Bb, LO, junk, stB)
            self.DMA("sp", out_d[rows, :], LO[:], LO, out_d)

    def build(self):
        P = self.P
        self.declare()
        self.consts()
        self.chunk_consts()
        self.convert_tables()
        self.project(self.X, self.even_w_in[:], self.even_w_in, EVEN_IN, self.proj_d)
        self.even_pass()
        self.project(self.mix_d, self.even_w_out[:], self.even_w_out, D, self.ym_d)
        self.ln_phase(self.X, self.ym_d, 0, self.h_d)
        self.project(self.h_d, self.peer_wq[0], self.peer_wq, D, self.q_d)
        self.peer_pass(0, self.h_d, self.q_d, 1, self.x1_d)
        self.project(self.x1_d, self.odd_w_in[:], self.odd_w_in, ODD_IN, self.proj_d)
        self.odd_pads()
        self.ssd_pass()
        self.rwkv_pass()
        self.project(self.mix_d, self.odd_w_out[:], self.odd_w_out, D, self.ym_d)
        self.ln_phase(self.x1_d, self.ym_d, 2, self.h_d)
        self.project(self.h_d, self.peer_wq[1], self.peer_wq, D, self.q_d)
        self.peer_pass(1, self.h_d, self.q_d, 3, self.Y)
        P.barrier()
        P.emit()
        return self.nc


def host_inputs(NF, c, inp, nprompt_b, pos_prompt0=0, past_len=16384):
    f = lambda a: np.ascontiguousarray(a, dtype=np.float32)
    NP = 128 * NF + 16
    TOK = 128 * (NF + 1)
    b = c % nprompt_b
    sl = slice(c * NSEQ, (c + 1) * NSEQ)
    X = np.zeros((TOK, D), np.float32)
    X[0:16] = inp["meta_tokens"]
    X[16:NP] = inp["x_prompt"][b]
    X[NP:NP + 64] = inp["x_sample"][sl].transpose(1, 0, 2).reshape(64, D)
    hg = inp["state_hgrn"][0, sl]
    rt = inp["state_ret"][0, sl]
    st_even = np.concatenate([hg.transpose(0, 2, 1, 3).reshape(NSEQ, 128, 1024),
                              rt.transpose(0, 2, 1, 3).reshape(NSEQ, 128, 1024)], axis=2)
    sm = inp["state_ssm"][0, sl]
    wk = inp["state_wkv"][0, sl]
    wk2 = wk.reshape(NSEQ, 8, 2, 64, 64).transpose(0, 2, 4, 1, 3).reshape(NSEQ, 128, 512)
    st_odd = np.concatenate([sm.transpose(0, 2, 1, 3).reshape(NSEQ, 128, 1024), wk2], axis=2)
    st_conv = inp["state_conv"][0, sl].transpose(1, 0, 2)
    st_shift = inp["state_shift"][0, sl]
    pos = np.zeros(TOK, np.float64)
    pos[0:NP] = pos_prompt0 + np.arange(NP)
    pos[NP:NP + 64] = past_len + np.repeat(np.arange(4), NSEQ)
    inv = (10000.0 ** (-np.arange(64, dtype=np.float32) / 64)).astype(np.float32)
    ang = pos.astype(np.float32)[:, None] * inv[None, :]
    cs, sn = np.cos(ang).astype(np.float32), np.sin(ang).astype(np.float32)
    sc = np.float32(128 ** -0.5)
    rot = np.stack([cs, sn, cs * sc, sn * sc])
    rowoff = np.full((TOK, 1), 1.0e6, np.float32)
    if c < nprompt_b:
        rowoff[0:NP] = 0.0
    rowoff[NP:NP + 64] = 0.0
    m = {
        "rowoff": rowoff, "X": X, "st_even": st_even, "st_odd": st_odd, "st_conv": st_conv, "st_shift": st_shift, "rot": rot,
        "ln_g": inp["ln_g"].reshape(4, D), "ln_b": inp["ln_b"].reshape(4, D),
        "even_w_in": inp["even_w_in"][0], "lb_logits": inp["hgrn_lb_logits"], "hgrn_norm_g": inp["hgrn_norm_g"][0],
        "even_w_out": inp["even_w_out"][0], "odd_w_in": inp["odd_w_in"][0], "conv_w": inp["conv_w"][0],
        "conv_b": inp["conv_b"][0], "dt_bias": inp["dt_bias"][0], "a_log": inp["a_log"][0], "d_skip": inp["d_skip"][0],
        "ssm_norm_g": inp["ssm_norm_g"][0], "shift_mu": inp["shift_mu"][0],
        "rwkv_vec": np.stack([inp["rwkv_w0"][0], inp["rwkv_a0"][0], inp["rwkv_k_k"][0], inp["rwkv_k_a"][0],
                              inp["rwkv_r_k"][0].reshape(1024), inp["lnx_g"][0], inp["lnx_b"][0]]),
        "rwkv_w2": inp["rwkv_w2"][0], "rwkv_a2": inp["rwkv_a2"][0], "rwkv_g2": inp["rwkv_g2"][0],
        "odd_w_out": inp["odd_w_out"][0], "peer_wq": inp["peer_w_query"],
        "keysT": inp["peer_sub_keys"].reshape(2, 16, 128, 128).transpose(0, 3, 1, 2),
        "peer_u0": inp["peer_u"][0], "peer_u1": inp["peer_u"][1], "peer_v0": inp["peer_v"][0],
        "peer_v1": inp["peer_v"][1], "zeros_d": np.zeros((4, 3360), np.float32),
    }
    return {k: f(v) for k, v in m.items()}


_NC_CACHE = {}


def kernel(**inputs):
    inp = {k: np.asarray(v) for k, v in inputs.items()}
    SEQ = inp["x_prompt"].shape[1]
    NF = (SEQ + 16 - 16) // 128
    assert 128 * NF == SEQ
    NP = SEQ + 16
    if NF not in _NC_CACHE:
        _NC_CACHE[NF] = K(NF).build()
    nc = _NC_CACHE[NF]
    B = inp["x_prompt"].shape[0]
    in_maps = [host_inputs(NF, c, inp, B) for c in range(8)]
    res = run_bass_kernel_spmd(nc, in_maps, core_ids=list(range(8))).results
    y_p = np.stack([res[b]["Y"][16:NP] for b in range(B)])
    y_s = np.concatenate([res[c]["Y"][NP:NP + 64].reshape(4, NSEQ, D).transpose(1, 0, 2) for c in range(8)])

    def even_split(a):
        n = a.shape[0]
        return (a[:, :, 0:1024].reshape(n, 128, 8, 128).transpose(0, 2, 1, 3),
                a[:, :, 1024:2048].reshape(n, 128, 4, 256).transpose(0, 2, 1, 3))

    def odd_split(a):
        n = a.shape[0]
        ssm = a[:, :, 0:1024].reshape(n, 128, 16, 64).transpose(0, 2, 1, 3)
        wkv = a[:, :, 1024:1536].reshape(n, 2, 64, 8, 64).transpose(0, 3, 1, 4, 2).reshape(n, 16, 64, 64)
        return ssm, wkv

    ep = np.stack([res[b]["So_even"][0] for b in range(B)])
    es = np.concatenate([res[c]["So_even"][1:] for c in range(8)])
    op_ = np.stack([res[b]["So_odd"][0] for b in range(B)])
    os_ = np.concatenate([res[c]["So_odd"][1:] for c in range(8)])
    hg_p, rt_p = even_split(ep)
    hg_s, rt_s = even_split(es)
    sm_p, wk_p = odd_split(op_)
    sm_s, wk_s = odd_split(os_)
    cv_p = np.stack([res[b]["conv_o"][:, 0, :] for b in range(B)])
    cv_s = np.concatenate([res[c]["conv_o"][:, 1:, :].transpose(1, 0, 2) for c in range(8)])
    sh_p = np.stack([res[b]["shift_o"][0] for b in range(B)])
    sh_s = np.concatenate([res[c]["shift_o"][1:] for c in range(8)])
    A = lambda a: np.ascontiguousarray(a, dtype=np.float32)
    return (A(y_p), A(y_s), A(hg_p)[None], A(hg_s)[None], A(rt_p)[None], A(rt_s)[None], A(sm_p)[None], A(sm_s)[None],
            A(cv_p)[None], A(cv_s)[None], A(wk_p)[None], A(wk_s)[None], A(sh_p)[None], A(sh_s)[None])
```
